# Optimizing a Trainium2 kernel written in Bass

```python
import math
import jax, jax.numpy as jnp
from jax import lax
import numpy as np

D_MODEL = 2048
BATCH = 4
SEQ = 2048
DEPTH = 1
DEC_BATCH = 128
DEC_SEQ = 4
PAST_LEN = 16384
PAGE_SIZE = 128

MIX_WIDTH = D_MODEL
POOL_WIDTH = MIX_WIDTH // 2
SSM_WIDTH = MIX_WIDTH - POOL_WIDTH
POOL_WINDOWS = (2, 4, 8, 16)
N_POOL_GROUPS = len(POOL_WINDOWS)
POOL_GROUP = POOL_WIDTH // N_POOL_GROUPS
POOL_BUF = max(POOL_WINDOWS) - 1
SSM_GROUP = 16
N_SSM_GROUPS = SSM_WIDTH // SSM_GROUP
SSM_STATE = 64
N_MEM = 256
N_XHEADS = 4
XHEAD_DIM = D_MODEL // N_XHEADS
D_FF = -(-8 * D_MODEL // (3 * 256)) * 256
EPS = 1e-6
DT_MIN = 1e-3
DT_MAX = 1e-1

kernel_name = "hymba_pool_s5_xattn_step"


def _normal(k, shape, scale):
    return scale * jax.random.normal(k, shape, jnp.float32)


def _rmsnorm(x, g):
    xf = x.astype(jnp.float32)
    r = lax.rsqrt(jnp.mean(xf * xf, axis=-1, keepdims=True) + EPS)
    return (xf * r * g.astype(jnp.float32)).astype(x.dtype)


def _pool_mixer(u, buf, pos0, w_pool, pool_scale):
    f32 = jnp.float32
    bsz, t, c = u.shape
    z = jnp.concatenate([buf.astype(f32), u.astype(f32)], axis=1)
    cs = jnp.concatenate([jnp.zeros((bsz, 1, c), f32), jnp.cumsum(z, axis=1)], axis=1)
    end = cs[:, POOL_BUF + 1:]
    pos = pos0 + jnp.arange(t)
    means = []
    for g, w in enumerate(POOL_WINDOWS):
        sl = slice(g * POOL_GROUP, (g + 1) * POOL_GROUP)
        start = cs[:, POOL_BUF + 1 - w:POOL_BUF + 1 - w + t, sl]
        cnt = jnp.minimum(pos + 1, w).astype(f32)[None, :, None]
        means.append((end[..., sl] - start) / cnt)
    pooled = (jnp.concatenate(means, axis=-1) - u.astype(f32)).reshape(bsz, t, N_POOL_GROUPS, POOL_GROUP)
    out = jnp.einsum('btgc,gcd->btgd', pooled, w_pool.astype(f32)).reshape(bsz, t, POOL_WIDTH)
    out = out * pool_scale.astype(f32)
    new_buf = z[:, -POOL_BUF:].astype(u.dtype)
    return out.astype(u.dtype), new_buf


def _ssm_combine(e1, e2):
    a1, b1 = e1
    a2, b2 = e2
    return a1 * a2, a2 * b1 + b2


def _s5_mixer(u, h_re, h_im, lam_re, lam_im, log_step, b_re, b_im, c_re, c_im, d_skip, w_glu, b_glu):
    f32 = jnp.float32
    bsz, t, _ = u.shape
    uf = u.astype(f32).reshape(bsz, t, N_SSM_GROUPS, SSM_GROUP)
    lam = lax.complex(lam_re.astype(f32), lam_im.astype(f32))
    delta = jnp.exp(log_step.astype(f32))[:, None]
    a_bar = jnp.exp(lam * delta)
    b_bar = ((a_bar - 1.0) / lam)[..., None] * lax.complex(b_re.astype(f32), b_im.astype(f32))
    bu = jnp.einsum('gpc,btgc->btgp', b_bar, uf.astype(jnp.complex64))
    h0 = lax.complex(h_re.astype(f32), h_im.astype(f32))
    bu = bu.at[:, 0].add(a_bar * h0)
    a = jnp.broadcast_to(a_bar, bu.shape)
    _, s = lax.associative_scan(_ssm_combine, (a, bu), axis=1)
    y = (jnp.einsum('gcp,btgp->btgc', c_re.astype(f32), jnp.real(s))
         - jnp.einsum('gcp,btgp->btgc', c_im.astype(f32), jnp.imag(s)))
    y = (y + d_skip.astype(f32).reshape(N_SSM_GROUPS, SSM_GROUP) * uf).reshape(bsz, t, SSM_WIDTH)
    g = jax.nn.gelu(y)
    out = g * jax.nn.sigmoid(g @ w_glu.astype(f32) + b_glu.astype(f32))
    h_last = s[:, -1]
    return out.astype(u.dtype), jnp.real(h_last).astype(h_re.dtype), jnp.imag(h_last).astype(h_im.dtype)


def _memory_kv(mem, g_mem, w_k, w_v):
    bsz, m, _ = mem.shape
    mn = _rmsnorm(mem, g_mem)
    k = (mn @ w_k).reshape(bsz, m, N_XHEADS, XHEAD_DIM)
    v = (mn @ w_v).reshape(bsz, m, N_XHEADS, XHEAD_DIM)
    return k, v


def _cross_attn(h, mem_k, mem_v, w_q, w_o):
    f32 = jnp.float32
    bsz, t, _ = h.shape
    q = (h @ w_q).reshape(bsz, t, N_XHEADS, XHEAD_DIM)
    sc = jnp.einsum('bthd,bmhd->bhtm', q.astype(f32), mem_k.astype(f32)) * (XHEAD_DIM ** -0.5)
    p = jax.nn.softmax(sc, axis=-1)
    o = jnp.einsum('bhtm,bmhd->bthd', p, mem_v.astype(f32)).astype(h.dtype).reshape(bsz, t, D_MODEL)
    return o @ w_o


def _layer(x, pool_buf, h_re, h_im, mem_k, mem_v, pos0, lw):
    h = _rmsnorm(x, lw['g_mix'])
    p = h @ lw['w_in']
    u_pool, u_ssm = p[..., :POOL_WIDTH], p[..., POOL_WIDTH:]
    pool_out, new_buf = _pool_mixer(u_pool, pool_buf, pos0, lw['w_pool'], lw['pool_scale'])
    ssm_out, new_re, new_im = _s5_mixer(u_ssm, h_re, h_im, lw['lam_re'], lw['lam_im'], lw['log_step'],
                                        lw['b_re'], lw['b_im'], lw['c_re'], lw['c_im'], lw['d'],
                                        lw['w_glu'], lw['b_glu'])
    x = x + jnp.concatenate([pool_out, ssm_out], axis=-1) @ lw['w_out']
    x = x + _cross_attn(_rmsnorm(x, lw['g_cross']), mem_k, mem_v, lw['w_q'], lw['w_o'])
    h = _rmsnorm(x, lw['g_ffn'])
    x = x + (jax.nn.silu(h @ lw['w_gate']) * (h @ lw['w_up'])) @ lw['w_down']
    return x, new_buf, new_re, new_im


def setup_inputs(seed: int = 0) -> dict:
    key = jax.random.key(seed)
    k = jax.random.split(key, 34)
    f32 = jnp.float32
    G, P = N_SSM_GROUPS, SSM_STATE
    lam_im = jnp.broadcast_to(jnp.pi * jnp.arange(P, dtype=f32), (DEPTH, G, P))
    return {
        'x_prompt': _normal(k[0], (BATCH, SEQ, D_MODEL), 1.0),
        'x_sample': _normal(k[1], (DEC_BATCH, DEC_SEQ, D_MODEL), 1.0),
        'mem_prompt': _normal(k[2], (BATCH, N_MEM, D_MODEL), 1.0),
        'state_pool_buf': _normal(k[3], (DEPTH, DEC_BATCH, POOL_BUF, POOL_WIDTH), 1.0),
        'state_ssm_re': _normal(k[4], (DEPTH, DEC_BATCH, G, P), 0.1),
        'state_ssm_im': _normal(k[5], (DEPTH, DEC_BATCH, G, P), 0.1),
        'cache_mem_k': _normal(k[6], (DEPTH, DEC_BATCH, N_MEM, N_XHEADS, XHEAD_DIM), 1.0),
        'cache_mem_v': _normal(k[7], (DEPTH, DEC_BATCH, N_MEM, N_XHEADS, XHEAD_DIM), 1.0),
        'g_mix': 1.0 + _normal(k[8], (DEPTH, D_MODEL), 0.02),
        'w_in': _normal(k[9], (DEPTH, D_MODEL, MIX_WIDTH), D_MODEL ** -0.5),
        'w_pool': _normal(k[10], (DEPTH, N_POOL_GROUPS, POOL_GROUP, POOL_GROUP), POOL_GROUP ** -0.5),
        'pool_scale': 1.0 + _normal(k[11], (DEPTH, POOL_WIDTH), 0.02),
        'ssm_lam_re': -0.5 + _normal(k[12], (DEPTH, G, P), 0.01),
        'ssm_lam_im': lam_im,
        'ssm_log_step': jax.random.uniform(k[13], (DEPTH, G), f32, math.log(DT_MIN), math.log(DT_MAX)),
        'ssm_b_re': _normal(k[14], (DEPTH, G, P, SSM_GROUP), (2 * SSM_GROUP) ** -0.5),
        'ssm_b_im': _normal(k[15], (DEPTH, G, P, SSM_GROUP), (2 * SSM_GROUP) ** -0.5),
        'ssm_c_re': _normal(k[16], (DEPTH, G, SSM_GROUP, P), P ** -0.5),
        'ssm_c_im': _normal(k[17], (DEPTH, G, SSM_GROUP, P), P ** -0.5),
        'ssm_d': _normal(k[18], (DEPTH, SSM_WIDTH), 1.0),
        'w_glu': _normal(k[19], (DEPTH, SSM_WIDTH, SSM_WIDTH), SSM_WIDTH ** -0.5),
        'b_glu': _normal(k[20], (DEPTH, SSM_WIDTH), 0.01),
        'w_out': _normal(k[21], (DEPTH, MIX_WIDTH, D_MODEL), MIX_WIDTH ** -0.5),
        'g_cross': 1.0 + _normal(k[22], (DEPTH, D_MODEL), 0.02),
        'g_mem': 1.0 + _normal(k[23], (DEPTH, D_MODEL), 0.02),
        'w_q': _normal(k[24], (DEPTH, D_MODEL, D_MODEL), D_MODEL ** -0.5),
        'w_k': _normal(k[25], (DEPTH, D_MODEL, D_MODEL), D_MODEL ** -0.5),
        'w_v': _normal(k[26], (DEPTH, D_MODEL, D_MODEL), D_MODEL ** -0.5),
        'w_o': _normal(k[27], (DEPTH, D_MODEL, D_MODEL), D_MODEL ** -0.5),
        'g_ffn': 1.0 + _normal(k[28], (DEPTH, D_MODEL), 0.02),
        'w_gate': _normal(k[29], (DEPTH, D_MODEL, D_FF), D_MODEL ** -0.5),
        'w_up': _normal(k[30], (DEPTH, D_MODEL, D_FF), D_MODEL ** -0.5),
        'w_down': _normal(k[31], (DEPTH, D_FF, D_MODEL), D_FF ** -0.5),
        'g_final': 1.0 + _normal(k[32], (D_MODEL,), 0.02),
    }


def reference(x_prompt, x_sample, mem_prompt, state_pool_buf, state_ssm_re, state_ssm_im,
              cache_mem_k, cache_mem_v, g_mix, w_in, w_pool, pool_scale, ssm_lam_re, ssm_lam_im,
              ssm_log_step, ssm_b_re, ssm_b_im, ssm_c_re, ssm_c_im, ssm_d, w_glu, b_glu, w_out,
              g_cross, g_mem, w_q, w_k, w_v, w_o, g_ffn, w_gate, w_up, w_down, g_final):
    bsz = x_prompt.shape[0]
    yp, ys = x_prompt, x_sample
    pb_p, re_p, im_p, mk_p, mv_p, pb_s, re_s, im_s = [], [], [], [], [], [], [], []
    for l in range(DEPTH):
        lw = {
            'g_mix': g_mix[l], 'w_in': w_in[l], 'w_pool': w_pool[l], 'pool_scale': pool_scale[l],
            'lam_re': ssm_lam_re[l], 'lam_im': ssm_lam_im[l], 'log_step': ssm_log_step[l],
            'b_re': ssm_b_re[l], 'b_im': ssm_b_im[l], 'c_re': ssm_c_re[l], 'c_im': ssm_c_im[l],
            'd': ssm_d[l], 'w_glu': w_glu[l], 'b_glu': b_glu[l], 'w_out': w_out[l],
            'g_cross': g_cross[l], 'w_q': w_q[l], 'w_o': w_o[l],
            'g_ffn': g_ffn[l], 'w_gate': w_gate[l], 'w_up': w_up[l], 'w_down': w_down[l],
        }
        mk, mv = _memory_kv(mem_prompt, g_mem[l], w_k[l], w_v[l])
        zero_buf = jnp.zeros((bsz, POOL_BUF, POOL_WIDTH), x_prompt.dtype)
        zero_h = jnp.zeros((bsz, N_SSM_GROUPS, SSM_STATE), state_ssm_re.dtype)
        yp, nb, nr, ni = _layer(yp, zero_buf, zero_h, zero_h, mk, mv, 0, lw)
        pb_p.append(nb); re_p.append(nr); im_p.append(ni); mk_p.append(mk); mv_p.append(mv)
        ys, nb, nr, ni = _layer(ys, state_pool_buf[l], state_ssm_re[l], state_ssm_im[l],
                                cache_mem_k[l], cache_mem_v[l], PAST_LEN, lw)
        pb_s.append(nb); re_s.append(nr); im_s.append(ni)
    y_prompt = _rmsnorm(yp, g_final)
    y_sample = _rmsnorm(ys, g_final)
    return (y_prompt, y_sample, jnp.stack(pb_p), jnp.stack(re_p), jnp.stack(im_p), jnp.stack(mk_p),
            jnp.stack(mv_p), jnp.stack(pb_s), jnp.stack(re_s), jnp.stack(im_s))
```

```python
import contextlib
import numpy as np
import concourse.bass as bass
import concourse.mybir as mybir
from concourse.bass_utils import run_bass_kernel_spmd

F32 = mybir.dt.float32
BF16 = mybir.dt.bfloat16
AF = mybir.ActivationFunctionType
ALU = mybir.AluOpType

D = 2048
NPR = 1024
NSQ = 16
NS = 64
N = NPR + NS
DFF = 5632
NTL = [(0, 512), (512, 512), (1024, 64)]
TT = [(i * 128, 128) for i in range(8)] + [(1024, 64)]
EPS = 1e-6


class _Op:
    __slots__ = ("eng", "fn", "deps", "dsem", "val", "sem", "need")

    def __init__(self, eng, fn, deps, dsem):
        self.eng = eng
        self.fn = fn
        self.deps = deps
        self.dsem = dsem
        self.val = 0
        self.sem = None
        self.need = False


class Prog:
    ENGS = ("pe", "act", "dve", "pool", "sp")

    def __init__(self):
        self.ops = []
        self.base_w = {}
        self.join_w = {}
        self.readers = {}
        self.aliases = {}
        self.spacer = None

    def alias(self, name, keys):
        self.aliases[name] = list(keys)

    def _expand(self, keys):
        out = []
        for k in keys:
            a = self.aliases.get(k)
            if a is None:
                out.append(k)
            else:
                out.extend(a)
        return out

    def op(self, eng, fn, reads=(), writes=(), dsem=None, join=False):
        idx = len(self.ops)
        reads = self._expand(reads)
        writes = self._expand(writes)
        deps = {}

        def add(d, raw):
            if d is None or d == idx:
                return
            deps[d] = deps.get(d, False) or raw

        for k in reads:
            add(self.base_w.get(k), True)
            for d in self.join_w.get(k, ()):
                add(d, True)
        for k in writes:
            add(self.base_w.get(k), False)
            if not join:
                for d in self.join_w.get(k, ()):
                    add(d, False)
            for d in self.readers.get(k, ()):
                add(d, False)
        for k in reads:
            self.readers.setdefault(k, []).append(idx)
        for k in writes:
            if join:
                self.join_w.setdefault(k, []).append(idx)
            else:
                self.base_w[k] = idx
                self.join_w[k] = []
                self.readers[k] = []
        self.ops.append(_Op(eng, fn, deps, dsem))
        return idx

    def emit(self, nc, stack):
        ops = self.ops
        pos = {}
        cnt = {e: 0 for e in self.ENGS}
        for i, o in enumerate(ops):
            if o.dsem is None:
                pos[i] = cnt[o.eng]
                cnt[o.eng] += 1
        waits = []
        spacers = set()
        for i, o in enumerate(ops):
            w = set()
            best = {}
            for d, raw in o.deps.items():
                od = ops[d]
                if od.dsem is not None:
                    w.add(d)
                    continue
                if o.dsem is None and od.eng == o.eng:
                    if o.eng == "pe":
                        continue
                    if o.eng in ("dve", "act"):
                        if not raw or pos[i] - pos[d] >= 3:
                            continue
                        if o.eng == "dve" and self.spacer is not None:
                            spacers.add(i)
                            continue
                if d > best.get(od.eng, -1):
                    best[od.eng] = d
            w.update(best.values())
            waits.append(w)
            for d in w:
                ops[d].need = True
        esem = {e: stack.enter_context(nc.semaphore("s_" + e)) for e in self.ENGS}
        dsems = {}
        ecount = {e: 0 for e in self.ENGS}
        dcount = {}
        for o in ops:
            if o.dsem is not None:
                if o.dsem not in dsems:
                    dsems[o.dsem] = stack.enter_context(nc.semaphore("d_" + o.dsem))
                    dcount[o.dsem] = 0
                dcount[o.dsem] += 16
                o.sem = dsems[o.dsem]
                o.val = dcount[o.dsem]
                o.need = True
            elif o.need:
                ecount[o.eng] += 1
                o.sem = esem[o.eng]
                o.val = ecount[o.eng]
        block = stack.enter_context(nc.Block())
        final = dict(dcount)

        def run(engname, e):
            waited = {}
            for i, o in enumerate(ops):
                if o.eng != engname:
                    continue
                need = {}
                for d in waits[i]:
                    od = ops[d]
                    if need.get(od.sem, (0, None))[0] < od.val:
                        need[od.sem] = (od.val, od.sem)
                for key, (v, s_) in need.items():
                    if waited.get(key, 0) < v:
                        e.wait_ge(s_, v)
                        waited[key] = v
                if i in spacers:
                    self.spacer(e)
                ins = o.fn(e)
                if o.need:
                    ins.then_inc(o.sem, 16 if o.dsem is not None else 1)
            if engname == "sp":
                for name, v in final.items():
                    e.wait_ge(dsems[name], v)

        @block.tensor
        def _(e):
            run("pe", e)

        @block.scalar
        def _(e):
            run("act", e)

        @block.vector
        def _(e):
            run("dve", e)

        @block.gpsimd
        def _(e):
            run("pool", e)

        @block.sync
        def _(e):
            run("sp", e)


def build(stop=None, dbg=False):
    nc = bass.Bass("TRN2", target_bir_lowering=False)
    st = contextlib.ExitStack()
    P = Prog()

    def din(name, shape):
        return nc.dram_tensor(name, shape, F32, kind="ExternalInput").ap()

    def dout(name, shape):
        return nc.dram_tensor(name, shape, F32, kind="ExternalOutput").ap()

    xp = din("xp", [NPR, D]); xprev = din("xprev", [NPR, D]); xs = din("xs", [NS, D]); mem = din("mem", [256, D])
    pbuf = din("pbuf", [240, 1024]); sre = din("sre", [16, 4096]); sim = din("sim", [16, 4096])
    ck = din("ck", [16, 256, D]); cv = din("cv", [16, 256, D])
    invc = din("invc", [128, 4, 15]); ident = din("ident", [128, 128]); bmask = din("bmask", [128, 128])
    g_mix = din("g_mix", [D]); w_in = din("w_in", [D, D]); w_pool = din("w_pool", [4, 256, 256]); pool_scale = din("pool_scale", [1024])
    lam_re = din("lam_re", [64, 64]); lam_im = din("lam_im", [64, 64]); log_step = din("log_step", [64])
    b_re = din("b_re", [64, 64, 16]); b_im = din("b_im", [64, 64, 16]); c_re = din("c_re", [64, 16, 64]); c_im = din("c_im", [64, 16, 64])
    ssm_d = din("ssm_d", [1024]); w_glu = din("w_glu", [1024, 1024]); b_glu = din("b_glu", [1024]); w_out = din("w_out", [D, D])
    g_cross = din("g_cross", [D]); g_mem = din("g_mem", [D]); w_q = din("w_q", [D, D]); w_k = din("w_k", [D, D]); w_v = din("w_v", [D, D])
    w_o = din("w_o", [D, D]); g_ffn = din("g_ffn", [D]); w_gate = din("w_gate", [D, DFF]); w_up = din("w_up", [D, DFF]); w_down = din("w_down", [DFF, D])
    g_final = din("g_final", [D])
    yp = dout("yp", [NPR, D]); ys = dout("ys", [NS, D]); pbp = dout("pbp", [15, 1024]); srp = dout("srp", [32, 128]); sip = dout("sip", [32, 128])
    mk = dout("mk", [256, D]); mv = dout("mv", [256, D]); pbs = dout("pbs", [240, 1024]); srs = dout("srs", [16, 4096]); sis = dout("sis", [16, 4096])

    def sb(name, shape, dt=F32):
        return st.enter_context(nc.sbuf_tensor(name, shape, dt))

    XC = sb("XC", [128, 9 * D])
    A = sb("A", [128, 16, N], BF16)
    B = sb("B", [128, 16, N], BF16)
    WSZ = 12800
    W = sb("W", [128, WSZ])
    identf = sb("identf", [128, 128]); identb = sb("identb", [128, 128], BF16); onesb = sb("onesb", [128, 128], BF16)
    gv = sb("gv", [128, 4, 16])
    pv = sb("pv", [128, 24]); pscale = pv[:, 0:8]; dvec = pv[:, 8:16]; bglu = pv[:, 16:24]
    invc_t = sb("invc_t", [128, 4, 15])
    sq = sb("sq", [128, 32, 24])
    H0b = sb("H0b", [128, 2, 32, 16], BF16)

    Sst = sb("Sst", [128, 2, 32])
    H0 = sb("H0", [128, 2, 32, 16]); SF = sb("SF", [128, 2, 32, 16])
    HISTP = sb("HISTP", [128, 8, 15])
    stat = sb("stat", [128, 8]); tiny = sb("tiny", [128, 8]); spc = sb("spc", [128, 2])
    P.spacer = None
    PSALL = st.enter_context(nc.psum_tensor("psall", [128, 8 * 512], F32))
    PS = [PSALL[:, i * 512:(i + 1) * 512] for i in range(8)]
    pctr = [0]
    bpool = [list(range(8))]

    def bank():
        b = bpool[0][pctr[0] % len(bpool[0])]
        pctr[0] += 1
        return b

    WBLK = 128

    def WA(name, off, n, bf=False):
        assert off + n <= WSZ, (name, off, n)
        P.alias(name, [("W", b) for b in range(off // WBLK, (off + n - 1) // WBLK + 1)])
        v = W[:, off:off + n]
        return v.bitcast(BF16) if bf else v

    cosT = XC[:, 0:4096].rearrange("p (q j) -> p q j", q=32)
    sinT = XC[:, 4096:8192].rearrange("p (q j) -> p q j", q=32)
    WBt = XC[:, 8192:12288].bitcast(BF16).rearrange("p (c a i j) -> p c a i j", c=8, a=2, i=4)
    WCt = XC[:, 12288:16384].bitcast(BF16).rearrange("p (c a i j) -> p c a i j", c=8, a=2, i=4)
    KTt = XC[:, 16384:18432].bitcast(BF16).rearrange("p (c i j) -> p c i j", c=8, i=4)
    P.alias("cosT", [("XC", 0), ("XC", 1)]); P.alias("sinT", [("XC", 2), ("XC", 3)])
    P.alias("WB", [("XC", 4), ("XC", 5)]); P.alias("WC", [("XC", 6), ("XC", 7)]); P.alias("KT", [("XC", 8)])
    for ti in range(9):
        P.alias(("X", ti), [("XC", ti)])
    TAB = ["cosT", "sinT"]

    def Xt(ti):
        return XC[:, ti * D:(ti + 1) * D]

    def dma(eng, out, in_, reads, writes, buf, join=False, slow=False):
        sem = buf if isinstance(buf, str) else "_".join(str(x) for x in buf)
        if slow:
            return P.op(eng, lambda e: e.dma_start(out=out, in_=in_, allow_slow_non_contiguous=True), reads, writes, dsem=sem, join=join)
        return P.op(eng, lambda e: e.dma_start(out=out, in_=in_), reads, writes, dsem=sem, join=join)

    def mm(out, lhsT, rhs, start, stop, reads, writes, tp=None):
        if tp is None:
            P.op("pe", lambda e: e.matmul(out, lhsT=lhsT, rhs=rhs, start=start, stop=stop), reads, writes)
        else:
            P.op("pe", lambda e: e.matmul(out, lhsT=lhsT, rhs=rhs, start=start, stop=stop, tile_position=tp), reads, writes)

    def tr(out, in_, idn, reads, writes):
        P.op("pe", lambda e: e.transpose(out=out, in_=in_, identity=idn), reads, writes)

    def tt(eng, out, in0, in1, op, reads, writes):
        P.op(eng, lambda e: e.tensor_tensor(out=out, in0=in0, in1=in1, op=op), reads, writes)

    def ts(eng, out, in0, s1, s2, op0, op1, reads, writes):
        if s2 is None:
            P.op(eng, lambda e: e.tensor_scalar(out=out, in0=in0, scalar1=s1, scalar2=None, op0=op0), reads, writes)
        else:
            P.op(eng, lambda e: e.tensor_scalar(out=out, in0=in0, scalar1=s1, scalar2=s2, op0=op0, op1=op1), reads, writes)

    def stt(eng, out, in0, scalar, in1, op0, op1, reads, writes):
        P.op(eng, lambda e: e.scalar_tensor_tensor(out=out, in0=in0, scalar=scalar, in1=in1, op0=op0, op1=op1), reads, writes)

    def act(out, in_, func, reads, writes, scale=None, bias=None, accum=None):
        kw = {}
        if scale is not None:
            kw["scale"] = scale
        if bias is not None:
            kw["bias"] = bias
        if accum is not None:
            kw["accum_out"] = accum
        P.op("act", lambda e: e.activation(out=out, in_=in_, func=func, **kw), reads, writes)

    def cp(eng, out, in_, reads, writes):
        if eng == "act":
            P.op("act", lambda e: e.copy(out=out, in_=in_), reads, writes)
        else:
            P.op(eng, lambda e: e.tensor_copy(out=out, in_=in_), reads, writes)

    def ms(eng, ap, val, writes):
        P.op(eng, lambda e: e.memset(ap, val), (), writes)

    def recip(out, in_, reads, writes):
        P.op("dve", lambda e: e.reciprocal(out=out, in_=in_), reads, writes)

    def pf(items, loader):
        items = list(items)
        nxt = loader(items[0])
        for i, it in enumerate(items):
            cur = nxt
            if i + 1 < len(items):
                nxt = loader(items[i + 1])
            yield it, cur

    class _Stop(Exception):
        pass

    def ckpt(name):
        if stop == name:
            raise _Stop()

    def dump(name, ap, keys):
        if not dbg:
            return
        t = nc.dram_tensor("dbg_" + name, list(ap.shape), ap.dtype, kind="ExternalOutput").ap()
        P.op("sp", lambda e: e.dma_start(out=t, in_=ap), keys, [], dsem="dbg_" + name)

    def body():
        dma("sp", identf[:], ident, [], ["identf"], "identf")
        bmask_t = WA("bmask", 10368, 128)
        dma("sp", bmask_t[:], bmask, [], ["bmask"], "bmask")
        dma("pool", identb[:], ident, [], ["identb"], "identb")
        ms("dve", onesb[:], 1.0, ["onesb"])
        stG = WA("stG", 11520, 128); st2 = WA("st2", 11648, 128)
        for i, g in enumerate([g_mix, g_cross, g_mem, g_ffn]):
            dma("sp", stG[16 * i:16 * i + 16, :], g.rearrange("(k p) -> k p", p=128), [], ["stG"], "stG", join=True)
        for i, g in enumerate([pool_scale, ssm_d, b_glu]):
            dma("sp", st2[8 * i:8 * i + 8, :], g.rearrange("(k p) -> k p", p=128), [], ["st2"], "st2", join=True)
        dma("sp", invc_t[:], invc, [], ["invc"], "invc")
        b = bank()
        tr(PS[b][:, 0:64], stG[0:64, :], identf[0:64, 0:64], ["stG", "identf"], [("ps", b)])
        tr(PS[b][:, 64:88], st2[0:24, :], identf[0:24, 0:24], ["st2", "identf"], [("ps", b)])
        cp("dve", gv[:].rearrange("p i k -> p (i k)"), PS[b][:, 0:64], [("ps", b)], ["gv"])
        cp("dve", pv[:], PS[b][:, 64:88], [("ps", b)], ["pscale", "dvec", "bglu"])
        LRE, LIM, DL, XR, TH, RR, CC, SS, T1, T2, T3, ARE, AIM, FRE, FIM, DEN, AM1 = range(17)
        stL = [WA("stL0", 11776, 128), WA("stL1", 11904, 128), WA("stL2", 12032, 128)]
        lsT = WA("lsT", 12160, 2)
        dma("sp", stL[0][0:32, :].rearrange("q (t p) -> q t p", t=2), lam_re.rearrange("(q two) p -> q two p", two=2), [], ["stL0"], "stL0")
        dma("sp", stL[1][0:32, :].rearrange("q (t p) -> q t p", t=2), lam_im.rearrange("(q two) p -> q two p", two=2), [], ["stL1"], "stL1")
        dma("sp", lsT[0:32, :], log_step.rearrange("(q two) -> q two", two=2), [], ["lsT"], "lsT")
        cp("dve", stL[2][0:32, :].rearrange("q (t p) -> q t p", t=2), lsT[0:32, :].unsqueeze(2).to_broadcast([32, 2, 64]), ["lsT"], ["stL2"])
        b = bank()
        for i_, col in enumerate((LRE, LIM, DL)):
            tr(PS[b][:, 32 * i_:32 * i_ + 32], stL[i_][0:32, :], identf[0:32, 0:32], ["stL%d" % i_, "identf"], [("ps", b)])
        for i_, col in enumerate((LRE, LIM, DL)):
            cp("dve", sq[:, :, col], PS[b][:, 32 * i_:32 * i_ + 32], [("ps", b)], [("sq", col)])
        XT = [WA("XT0", 0, 2048), WA("XT1", 2048, 2048)]
        HNL = [WA("HN0", 4096, 1024, bf=True), WA("HN1", 5120, 1024, bf=True)]
        for _p in range(2):
            P.alias("HN%da" % _p, [("W", b_) for b_ in range((4096 + 1024 * _p) // WBLK, (4096 + 1024 * _p + 512) // WBLK)])
            P.alias("HN%db" % _p, [("W", b_) for b_ in range((4096 + 1024 * _p + 512) // WBLK, (4096 + 1024 * _p + 1024) // WBLK)])
        nctr = [0]

        def norm_from(xt_ap, xkey, R, gi, dst_fn, dst_keys):
            par = nctr[0] % 2
            nctr[0] += 1
            HN = HNL[par]; hk = "HN%d" % par
            s0, s1, s2 = 4 * par, 4 * par + 1, 4 * par + 2
            k0, k1, k2 = "stat%d" % s0, "stat%d" % s1, "stat%d" % s2
            act(HN[0:R, :], xt_ap, AF.Square, [xkey], [hk + "a", hk + "b", k0], accum=stat[0:R, s0:s0 + 1])
            ts("dve", stat[0:R, s1:s1 + 1], stat[0:R, s0:s0 + 1], 1.0 / D, EPS, ALU.mult, ALU.add, [k0], [k1])
            act(stat[0:R, s1:s1 + 1], stat[0:R, s1:s1 + 1], AF.Sqrt, [k1], [k1])
            recip(stat[0:R, s2:s2 + 1], stat[0:R, s1:s1 + 1], [k1], [k2])
            act(HN[0:R, 0:1024], xt_ap[:, 0:1024], AF.Copy, [xkey, k2], [hk + "a"], scale=stat[0:R, s2:s2 + 1])
            ts("dve", HN[0:R, 1024:2048], xt_ap[:, 1024:2048], stat[0:R, s2:s2 + 1], None, ALU.mult, None, [xkey, k2], [hk + "b"])
            for hb in range(2):
                b = bank()
                pb = PS[b][:].bitcast(BF16)
                for j in range(8):
                    kc = hb * 8 + j
                    tr(pb[:, j * 128:j * 128 + R], HN[0:R, kc * 128:(kc + 1) * 128], identb[0:R, 0:R], [hk + ("a" if hb == 0 else "b"), "identb"], [("ps", b)])
                src3 = pb[:, 0:1024].rearrange("p (k t) -> p k t", k=8)[:, :, 0:R]
                gb = gv[:, gi, hb * 8:hb * 8 + 8].unsqueeze(2).to_broadcast([128, 8, R])
                tt("dve", dst_fn(hb), src3, gb, ALU.mult, [("ps", b), "gv"], dst_keys(hb))

        def norm_tile(src_rows, R, col0, gi, ring_i=0, xin=None):
            if xin is None:
                xt_ap = XT[ring_i][0:R, :]
                xkey = "XT%d" % ring_i
                dma("sp", xt_ap, src_rows, [], [xkey], xkey)
            else:
                xt_ap, xkey = xin
            norm_from(xt_ap, xkey, R, gi, lambda hb: A[:, hb * 8:hb * 8 + 8, col0:col0 + R], lambda hb: [("A", k) for k in range(hb * 8, hb * 8 + 8)])

        stgL = [WA("stg0", 10496, 512), WA("stg1", 11008, 512)]
        for part, src in enumerate([sre, sim]):
            for qb in range(8):
                stg = stgL[qb % 2]; stgk = "stg%d" % (qb % 2)
                dma("sp", stg[0:16, :], src[:, qb * 512:(qb + 1) * 512], [], [stgk], stgk)
                b = bank()
                for j in range(4):
                    tr(PS[b][:, j * 16:(j + 1) * 16], stg[0:16, j * 128:(j + 1) * 128], identf[0:16, 0:16], [stgk, "identf"], [("ps", b)])
                cp("dve", H0[:, part, 4 * qb:4 * qb + 4, :], PS[b][:, 0:64].rearrange("p (q s) -> p q s", q=4), [("ps", b)], ["H0"])
        cp("pool", H0b[:], H0[:], ["H0"], ["H0b"])
        RINGS = {"wr": (6400, 1024, 2), "wo": (8448, 2048, 2), "wkv": (0, 2048, 2), "wkve": (10496, 1024, 2)}
        rctr = {"wr": 0, "wo": 0, "wkv": 0, "wkve": 0}

        def load_w(src_ap, nk, ncols, tag="wr"):
            base, slot, nbuf = RINGS[tag]
            i = rctr[tag] % nbuf
            rctr[tag] += 1
            sz = nk * ncols // 2
            assert sz <= slot
            name = "%s%d" % (tag, i)
            v = WA(name, base + i * slot, sz, bf=True).rearrange("p (k c) -> p k c", k=nk)
            dma("pool", v, src_ap.rearrange("(k p) c -> p k c", p=128), [], [name], name)
            return v, name

        def fm_matmul(wv, wkey, nk, rhs_fn, rhs_keys_fn, c0, ncols):
            b = bank()
            for k in range(nk):
                mm(PS[b][:, 0:ncols], wv[:, k, :], rhs_fn(k)[:, c0:c0 + ncols], k == 0, k == nk - 1, [wkey] + rhs_keys_fn(k), [("ps", b)])
            return b

        Akeys = lambda k: [("A", k)]
        for t_i in range(8):
            norm_tile(xprev[t_i * 128:(t_i + 1) * 128, :], 128, t_i * 128, 0, ring_i=t_i % 2)


        def sqv(i):
            return sq[:, :, i]

        P.alias("sq", [("sq", c_) for c_ in range(24)])

        K = ["sq"]
        import math as _m

        def horner(dst, xcol, cf, eng="dve"):
            ts(eng, sqv(dst), sqv(xcol), float(cf[-1]), float(cf[-2]), ALU.mult, ALU.add, [("sq", xcol)], [("sq", dst)])
            for c_ in reversed(cf[:-2]):
                tt(eng, sqv(dst), sqv(dst), sqv(xcol), ALU.mult, [("sq", dst), ("sq", xcol)], [("sq", dst)])
                ts(eng, sqv(dst), sqv(dst), float(c_), None, ALU.add, None, [("sq", dst)], [("sq", dst)])

        ecf = [1.0 / _m.factorial(i) for i in range(13)]
        ts("dve", sqv(T1), sqv(DL), 0.125, None, ALU.mult, None, [("sq", DL)], [("sq", T1)])
        tt("dve", sqv(T2), sqv(T1), sqv(T1), ALU.mult, [("sq", T1), ("sq", T1)], [("sq", T2)])
        horner(DL, T2, ecf[0::2], "dve")
        horner(T3, T2, ecf[1::2], "pool")
        tt("dve", sqv(T3), sqv(T3), sqv(T1), ALU.mult, [("sq", T3), ("sq", T1)], [("sq", T3)])
        tt("dve", sqv(DL), sqv(DL), sqv(T3), ALU.add, [("sq", DL), ("sq", T3)], [("sq", DL)])
        for _ in range(3):
            tt("dve", sqv(DL), sqv(DL), sqv(DL), ALU.mult, [("sq", DL), ("sq", DL)], [("sq", DL)])
        tt("dve", sqv(XR), sqv(LRE), sqv(DL), ALU.mult, [("sq", LRE), ("sq", DL)], [("sq", XR)])
        tt("dve", sqv(TH), sqv(LIM), sqv(DL), ALU.mult, [("sq", LIM), ("sq", DL)], [("sq", TH)])
        ts("dve", sqv(T1), sqv(TH), 1.0 / 16, None, ALU.mult, None, [("sq", TH)], [("sq", T1)])
        tt("dve", sqv(T2), sqv(T1), sqv(T1), ALU.mult, [("sq", T1), ("sq", T1)], [("sq", T2)])
        sc = [(-1.0) ** i / _m.factorial(2 * i + 1) for i in range(7)]
        cc_ = [(-1.0) ** i / _m.factorial(2 * i) for i in range(8)]
        horner(SS, T2, sc, "dve")
        horner(CC, T2, cc_, "pool")
        tt("dve", sqv(SS), sqv(SS), sqv(T1), ALU.mult, [("sq", SS), ("sq", T1)], [("sq", SS)])
        ts("pool", sqv(AM1), sqv(XR), 0.25, None, ALU.mult, None, [("sq", XR)], [("sq", AM1)])
        horner(RR, AM1, ecf[:9], "pool")
        for _ in range(2):
            tt("pool", sqv(RR), sqv(RR), sqv(RR), ALU.mult, [("sq", RR), ("sq", RR)], [("sq", RR)])
        P.alias("MTE", [("XC", 8)])
        MTE = XC[:, 16384:18432].bitcast(BF16).rearrange("p (k t) -> p k t", k=16)
        KOe = [WA("KOe0", 12544, 128), WA("KOe1", 12672, 128)]
        for mt_i in range(2):
            xkey = "XT%d" % mt_i
            dma("sp", XT[mt_i][:, :], mem[mt_i * 128:(mt_i + 1) * 128, :], [], [xkey], xkey)
            norm_from(XT[mt_i][:, :], xkey, 128, 2, lambda hb: MTE[:, hb * 8:hb * 8 + 8, mt_i * 128:(mt_i + 1) * 128], lambda hb: ["MTE"])
        for which, (wsrc, dsto, dk_) in enumerate([(w_k, mk, "mkd"), (w_v, mv, "mvd")]):
            for cb, (wv, wkey) in pf(range(16), lambda cb: load_w(wsrc[:, cb * 128:(cb + 1) * 128], 16, 128, tag="wkve")):
                for mt_i in range(2):
                    b = bank()
                    for k in range(16):
                        mm(PS[b][:, 0:128], MTE[:, k, mt_i * 128:(mt_i + 1) * 128], wv[:, k, :], k == 0, k == 15, [wkey, "MTE"], [("ps", b)])
                    kk = "KOe%d" % mt_i
                    cp("act", KOe[mt_i], PS[b][:, 0:128], [("ps", b)], [kk])
                    dma("sp", dsto[mt_i * 128:(mt_i + 1) * 128, cb * 128:(cb + 1) * 128], KOe[mt_i], [kk], [dk_], "o_" + kk, join=True)

        for _ in range(4):
            tt("dve", sqv(T1), sqv(CC), sqv(CC), ALU.mult, [("sq", CC), ("sq", CC)], [("sq", T1)])
            tt("dve", sqv(T2), sqv(SS), sqv(SS), ALU.mult, [("sq", SS), ("sq", SS)], [("sq", T2)])
            tt("dve", sqv(T3), sqv(CC), sqv(SS), ALU.mult, [("sq", CC), ("sq", SS)], [("sq", T3)])
            tt("dve", sqv(CC), sqv(T1), sqv(T2), ALU.subtract, [("sq", T1), ("sq", T2)], [("sq", CC)])
            ts("dve", sqv(SS), sqv(T3), 2.0, None, ALU.mult, None, [("sq", T3)], [("sq", SS)])
        tt("dve", sqv(ARE), sqv(RR), sqv(CC), ALU.mult, [("sq", RR), ("sq", CC)], [("sq", ARE)])
        tt("dve", sqv(AIM), sqv(RR), sqv(SS), ALU.mult, [("sq", RR), ("sq", SS)], [("sq", AIM)])
        tt("dve", sqv(T1), sqv(LRE), sqv(LRE), ALU.mult, [("sq", LRE), ("sq", LRE)], [("sq", T1)])
        tt("dve", sqv(T2), sqv(LIM), sqv(LIM), ALU.mult, [("sq", LIM), ("sq", LIM)], [("sq", T2)])
        tt("dve", sqv(DEN), sqv(T1), sqv(T2), ALU.add, [("sq", T1), ("sq", T2)], [("sq", DEN)])
        recip(sqv(DEN), sqv(DEN), [("sq", DEN)], [("sq", DEN)])
        ts("dve", sqv(AM1), sqv(ARE), -1.0, None, ALU.add, None, [("sq", ARE)], [("sq", AM1)])
        tt("dve", sqv(T1), sqv(AM1), sqv(LRE), ALU.mult, [("sq", AM1), ("sq", LRE)], [("sq", T1)])
        tt("dve", sqv(T2), sqv(AIM), sqv(LIM), ALU.mult, [("sq", AIM), ("sq", LIM)], [("sq", T2)])
        tt("dve", sqv(T1), sqv(T1), sqv(T2), ALU.add, [("sq", T1), ("sq", T2)], [("sq", T1)])
        tt("dve", sqv(FRE), sqv(T1), sqv(DEN), ALU.mult, [("sq", T1), ("sq", DEN)], [("sq", FRE)])
        tt("dve", sqv(T1), sqv(AIM), sqv(LRE), ALU.mult, [("sq", AIM), ("sq", LRE)], [("sq", T1)])
        tt("dve", sqv(T2), sqv(AM1), sqv(LIM), ALU.mult, [("sq", AM1), ("sq", LIM)], [("sq", T2)])
        tt("dve", sqv(T1), sqv(T1), sqv(T2), ALU.subtract, [("sq", T1), ("sq", T2)], [("sq", T1)])
        tt("dve", sqv(FIM), sqv(T1), sqv(DEN), ALU.mult, [("sq", T1), ("sq", DEN)], [("sq", FIM)])

        P2R, P2I, P3R, P3I, P4R, P4I, R4 = 17, 18, 19, 20, 21, 22, 23

        def cmul_sq(orr, oi, xr_, xi_, yr, yi):
            tt("dve", sqv(T1), sqv(xr_), sqv(yr), ALU.mult, [("sq", xr_), ("sq", yr)], [("sq", T1)])
            tt("dve", sqv(T2), sqv(xi_), sqv(yi), ALU.mult, [("sq", xi_), ("sq", yi)], [("sq", T2)])
            tt("dve", sqv(T3), sqv(xr_), sqv(yi), ALU.mult, [("sq", xr_), ("sq", yi)], [("sq", T3)])
            tt("dve", sqv(orr), sqv(T1), sqv(T2), ALU.subtract, [("sq", T1), ("sq", T2)], [("sq", orr)])
            tt("dve", sqv(T1), sqv(xi_), sqv(yr), ALU.mult, [("sq", xi_), ("sq", yr)], [("sq", T1)])
            tt("dve", sqv(oi), sqv(T3), sqv(T1), ALU.add, [("sq", T3), ("sq", T1)], [("sq", oi)])

        cmul_sq(P2R, P2I, ARE, AIM, ARE, AIM)
        cmul_sq(P3R, P3I, P2R, P2I, ARE, AIM)
        cmul_sq(P4R, P4I, P2R, P2I, P2R, P2I)
        tt("dve", sqv(R4), sqv(RR), sqv(RR), ALU.mult, [("sq", RR), ("sq", RR)], [("sq", R4)])
        tt("dve", sqv(R4), sqv(R4), sqv(R4), ALU.mult, [("sq", R4), ("sq", R4)], [("sq", R4)])
        for _ in range(2):
            tt("dve", sqv(T1), sqv(CC), sqv(CC), ALU.mult, [("sq", CC), ("sq", CC)], [("sq", T1)])
            tt("dve", sqv(T2), sqv(SS), sqv(SS), ALU.mult, [("sq", SS), ("sq", SS)], [("sq", T2)])
            tt("dve", sqv(T3), sqv(CC), sqv(SS), ALU.mult, [("sq", CC), ("sq", SS)], [("sq", T3)])
            tt("dve", sqv(CC), sqv(T1), sqv(T2), ALU.subtract, [("sq", T1), ("sq", T2)], [("sq", CC)])
            ts("dve", sqv(SS), sqv(T3), 2.0, None, ALU.mult, None, [("sq", T3)], [("sq", SS)])
        PW = {1: (ARE, AIM), 2: (P2R, P2I), 3: (P3R, P3I), 4: (P4R, P4I)}

        RAW = [WA("RAW0", 0, 1024), WA("RAW1", 1024, 1024)]
        BB = [WA("BB0", 2048, 1024), WA("BB1", 3072, 1024)]
        PR = [WA("PR0", 4096, 1024), WA("PR1", 5120, 1024)]
        TM = [WA("TM0", 6144, 1024), WA("TM1", 7168, 1024)]
        CT = [WA("CT0", 8192, 1024), WA("CT1", 9216, 1024)]
        v4 = lambda ap: ap.rearrange("p (c l j) -> p c l j", c=8, l=4)
        v3c = lambda ap: ap.rearrange("p (c j) -> p c j", c=8)
        mvw = lambda col: sq[:, :, col].rearrange("p (c l) -> p c l", l=4).unsqueeze(3).to_broadcast([128, 8, 4, 32])

        def cmul_pad(out, outk, x, xk, mre, mim, neg_im=False):
            tt("dve", v4(TM[0]), v4(x[0]), mvw(mre), ALU.mult, [xk[0]] + K, ["TM0"])
            tt("dve", v4(TM[1]), v4(x[1]), mvw(mim), ALU.mult, [xk[1]] + K, ["TM1"])
            tt("dve", out[0], TM[0], TM[1], ALU.subtract, ["TM0", "TM1"], [outk[0]])
            tt("dve", v4(TM[0]), v4(x[0]), mvw(mim), ALU.mult, [xk[0]] + K, ["TM0"])
            tt("dve", v4(TM[1]), v4(x[1]), mvw(mre), ALU.mult, [xk[1]] + K, ["TM1"])
            if neg_im:
                stt("dve", out[1], TM[0], -1.0, TM[1], ALU.mult, ALU.subtract, ["TM0", "TM1"], [outk[1]])
            else:
                tt("dve", out[1], TM[0], TM[1], ALU.add, ["TM0", "TM1"], [outk[1]])

        for part, src in enumerate((b_re, b_im)):
            ms("dve", RAW[part], 0.0, ["RAW%d" % part])
            for gl in range(8):
                hf = gl % 2
                dma("sp", v3c(RAW[part])[64 * hf:64 * hf + 64, :, 16 * gl:16 * gl + 16], src.rearrange("(c r) p k -> r p c k", r=8)[gl],
                    [], ["RAW%d" % part], "RAW%d" % part, join=True, slow=True)
        cmul_pad(BB, ["BB0", "BB1"], RAW, ["RAW0", "RAW1"], FRE, FIM)
        for i in range(4):
            kpow = 3 - i
            if kpow == 0:
                srcp, srck = BB, ["BB0", "BB1"]
            else:
                cmul_pad(PR, ["PR0", "PR1"], BB, ["BB0", "BB1"], PW[kpow][0], PW[kpow][1])
                srcp, srck = PR, ["PR0", "PR1"]
            for c in range(8):
                b = bank()
                for part in range(2):
                    tr(PS[b][:, part * 128:(part + 1) * 128], v3c(srcp[part])[:, c, :], identf[:], [srck[part], "identf"], [("ps", b)])
                cp("act", WBt[:, c, :, i, :], PS[b][:, 0:256].rearrange("p (a j) -> p a j", a=2), [("ps", b)], ["WB"])
        for part, src in enumerate((c_re, c_im)):
            ms("dve", RAW[part], 0.0, ["RAW%d" % part])
            for gl in range(8):
                hf = gl % 2
                dma("sp", v3c(RAW[part])[16 * gl:16 * gl + 16, :, 64 * hf:64 * hf + 64], src.rearrange("(c r) k p -> r k c p", r=8)[gl],
                    [], ["RAW%d" % part], "RAW%d" % part, join=True)
        for c in range(8):
            b = bank()
            for part in range(2):
                tr(PS[b][:, part * 128:(part + 1) * 128], v3c(RAW[part])[:, c, :], identf[:], ["RAW%d" % part, "identf"], [("ps", b)])
            for part in range(2):
                cp("act", v3c(CT[part])[:, c, :], PS[b][:, part * 128:(part + 1) * 128], [("ps", b)], ["CT%d" % part])
        KF = WA("KF", 10240, 128)
        for kpow in range(5):
            if kpow == 0:
                cp("dve", PR[0], CT[0], ["CT0"], ["PR0"])
                ts("dve", PR[1], CT[1], -1.0, None, ALU.mult, None, ["CT1"], ["PR1"])
            else:
                cmul_pad(PR, ["PR0", "PR1"], CT, ["CT0", "CT1"], PW[kpow][0], PW[kpow][1], neg_im=True)
            if kpow >= 1:
                for part in range(2):
                    cp("act", WCt[:, :, part, kpow - 1, :], v3c(PR[part]), ["PR%d" % part], ["WC"])
            if kpow <= 3:
                for c in range(8):
                    b = bank()
                    mm(PS[b][:, 0:128], v3c(BB[0])[:, c, :], v3c(PR[0])[:, c, :], True, False, ["BB0", "PR0"], [("ps", b)])
                    mm(PS[b][:, 0:128], v3c(BB[1])[:, c, :], v3c(PR[1])[:, c, :], False, True, ["BB1", "PR1"], [("ps", b)])
                    if kpow == 0:
                        tt("dve", KF, PS[b][:, 0:128], bmask_t[:], ALU.mult, [("ps", b), "bmask"], ["KF"])
                        stt("dve", KTt[:, c, 0, :], identf[:], dvec[:, c:c + 1], KF, ALU.mult, ALU.add, ["KF", "identf", "dvec"], ["KT"])
                    else:
                        tt("dve", KTt[:, c, kpow, :], PS[b][:, 0:128], bmask_t[:], ALU.mult, [("ps", b), "bmask"], ["KT"])

        cp("dve", cosT[:, :, 0], sqv(CC), K, ["cosT"])
        cp("dve", sinT[:, :, 0], sqv(SS), K, ["sinT"])
        tmpA = WA("tmpA", 0, 2048).rearrange("p (q j) -> p q j", q=32)
        tmpB = WA("tmpB", 2048, 2048).rearrange("p (q j) -> p q j", q=32)
        m = 1
        while m < 128:
            cm = cosT[:, :, m - 1:m].to_broadcast([128, 32, m])
            sm = sinT[:, :, m - 1:m].to_broadcast([128, 32, m])
            tt("dve", tmpA[:, :, 0:m], cosT[:, :, 0:m], cm, ALU.mult, TAB, ["tmpA"])
            tt("dve", tmpB[:, :, 0:m], sinT[:, :, 0:m], sm, ALU.mult, TAB, ["tmpB"])
            tt("dve", cosT[:, :, m:2 * m], tmpA[:, :, 0:m], tmpB[:, :, 0:m], ALU.subtract, ["tmpA", "tmpB"], ["cosT"])
            tt("dve", tmpA[:, :, 0:m], sinT[:, :, 0:m], cm, ALU.mult, TAB, ["tmpA"])
            tt("dve", tmpB[:, :, 0:m], cosT[:, :, 0:m], sm, ALU.mult, TAB, ["tmpB"])
            tt("dve", sinT[:, :, m:2 * m], tmpA[:, :, 0:m], tmpB[:, :, 0:m], ALU.add, ["tmpA", "tmpB"], ["sinT"])
            m *= 2
        ms("dve", Sst[:], 0.0, [("S", q) for q in range(32)])
        dump("cosT", XC[:, 0:4096], ["cosT"]); dump("sinT", XC[:, 4096:8192], ["sinT"])
        dump("WB", XC[:, 8192:12288], ["WB"]); dump("WC", XC[:, 12288:16384], ["WC"]); dump("KT", XC[:, 16384:18432], ["KT"])
        dump("sq", sq[:], ["sq"]); dump("H0", H0[:], ["H0"])
        ckpt("setup")


        TSd = {}
        for n_i, nm in enumerate(("t1", "t2", "t3", "t4", "t5", "t6", "t7", "t8", "zbr", "zbi", "zr", "zi")):
            TSd[nm] = (WA(nm, 512 * n_i, 512), nm)
        SbL = [WA("Sb0", 8448, 528, bf=True).rearrange("p (l a m) -> p l a m", l=4, a=2),
               WA("Sb1", 9088, 528, bf=True).rearrange("p (l a m) -> p l a m", l=4, a=2)]
        UBL = [WA("UB0", 9728, 544, bf=True), WA("UB1", 10368, 544, bf=True)]
        YT = WA("YT", 11008, 512)
        CM = PSALL[:, 0:2048].rearrange("p (l x) -> p l x", l=4)
        CMK = [("ps", l) for l in range(4)]
        CSR = WA("CSR", 11520, 512); CSI = WA("CSI", 12032, 512)

        def seg_tasks(segs):
            out = []
            n = 0
            for c in range(8):
                for (c0, ncols, sample) in segs:
                    out.append(dict(c=c, c0=c0, ncols=ncols, sample=sample, par=n % 2, first=(c0 == segs[0][0])))
                    n += 1
            return out

        def segA(t):
            c, c0, ncols = t["c"], t["c0"], t["ncols"]
            nb = ncols // 4
            UB = UBL[c % 2]; UBk = "UB%d" % (c % 2)
            for ql in range(4):
                for part in range(2):
                    for i in range(4):
                        mm(PS[ql][:, part * 128:part * 128 + nb], WBt[32 * ql:32 * ql + 32, c, part, i, :], UB[32 * ql:32 * ql + 32, c0 + i:c0 + ncols:4],
                           i == 0, i == 3, ["WB", UBk], [("ps", ql)], tp=(32 * ql, 0))

        def segB1(t):
            nb = t["ncols"] // 4
            v3 = lambda ap: ap[:, 0:4 * nb].rearrange("p (l m) -> p l m", l=4)
            cp("act", v3(CSR), CM[:, :, 0:nb], CMK, ["CSR"])
            cp("act", v3(CSI), CM[:, :, 128:128 + nb], CMK, ["CSI"])

        def segB(t, need_y):
            c, c0, ncols, sample, par = t["c"], t["c0"], t["ncols"], t["sample"], t["par"]
            nb = ncols // 4
            T = TSd
            (t1, t1k), (t2, t2k), (t3, t3k), (t4, t4k) = T["t1"], T["t2"], T["t3"], T["t4"]
            (t5, t5k), (t6, t6k), (t7, t7k), (t8, t8k) = T["t5"], T["t6"], T["t7"], T["t8"]
            (zbr, zbrk), (zbi, zbik), (zr, zrk), (zi, zik) = T["zbr"], T["zbi"], T["zr"], T["zi"]
            Sb = SbL[par]; Sbk = "Sb%d" % par
            v3 = lambda ap: ap[:, 0:4 * nb].rearrange("p (l m) -> p l m", l=4)
            pr = CM[:, :, 0:nb]; pi = CM[:, :, 128:128 + nb]
            if sample:
                cbt = cosT[:, 4 * c:4 * c + 4, 0:1].to_broadcast([128, 4, nb]); sbt = sinT[:, 4 * c:4 * c + 4, 0:1].to_broadcast([128, 4, nb])
            else:
                cbt = cosT[:, 4 * c:4 * c + 4, :]; sbt = sinT[:, 4 * c:4 * c + 4, :]
            SK = [("S", q) for q in range(4 * c, 4 * c + 4)]
            tt("dve", v3(t1), v3(CSR), cbt, ALU.mult, ["CSR", "cosT"], [t1k])
            tt("pool", v3(t3), v3(CSI), cbt, ALU.mult, ["CSI", "cosT"], [t3k])
            tt("dve", v3(t2), v3(CSI), sbt, ALU.mult, ["CSI", "sinT"], [t2k])
            tt("pool", v3(t4), v3(CSR), sbt, ALU.mult, ["CSR", "sinT"], [t4k])
            tt("dve", v3(zbr), v3(t1), v3(t2), ALU.add, [t1k, t2k], [zbrk])
            tt("pool", v3(zbi), v3(t3), v3(t4), ALU.subtract, [t3k, t4k], [zbik])
            r4b = sq[:, 4 * c:4 * c + 4, R4:R4 + 1]
            if sample:
                for part, (zb_, z_, zk, zbk) in enumerate(((zbr, zr, zrk, zbrk), (zbi, zi, zik, zbik))):
                    tx, txk = (t5, t5k) if part == 0 else (t6, t6k)
                    tt("pool", v3(tx), H0[:, part, 4 * c:4 * c + 4, :], r4b.to_broadcast([128, 4, nb]), ALU.mult, ["H0", "sq"], [txk])
                    tt("dve", v3(z_), v3(zb_), v3(tx), ALU.add, [zbk, txk], [zk])
            else:
                if need_y:
                    for part in range(2):
                        cp("act", Sb[:, :, part, 0], Sst[:, part, 4 * c:4 * c + 4], SK, [Sbk])
                for part, (zb_, z_, zk, zbk) in enumerate(((zbr, zr, zrk, zbrk), (zbi, zi, zik, zbik))):
                    for ql in range(4):
                        q = 4 * c + ql
                        rb = sq[:, q, R4:R4 + 1].to_broadcast([128, nb])
                        P.op("dve", lambda e, ql=ql, q=q, rb=rb, z_=z_, zb_=zb_, part=part: e.tensor_tensor_scan(
                            out=v3(z_)[:, ql, :], data0=rb, data1=v3(zb_)[:, ql, :], initial=Sst[:, part, q:q + 1], op0=ALU.mult, op1=ALU.add),
                            [zbk, "sq", ("S", q)], [zk])
                cl = cosT[:, 4 * c:4 * c + 4, nb - 1]; sl_ = sinT[:, 4 * c:4 * c + 4, nb - 1]
                zrl = v3(zr)[:, :, nb - 1]; zil = v3(zi)[:, :, nb - 1]
                tt("dve", tiny[:, 0:4], zrl, cl, ALU.mult, [zrk, "cosT"], ["tiny0"])
                tt("dve", tiny[:, 4:8], zil, sl_, ALU.mult, [zik, "sinT"], ["tiny1"])
                tt("dve", Sst[:, 0, 4 * c:4 * c + 4], tiny[:, 0:4], tiny[:, 4:8], ALU.subtract, ["tiny0", "tiny1"], SK)
                tt("dve", tiny[:, 0:4], zrl, sl_, ALU.mult, [zrk, "sinT"], ["tiny0"])
                tt("dve", tiny[:, 4:8], zil, cl, ALU.mult, [zik, "cosT"], ["tiny1"])
                tt("dve", Sst[:, 1, 4 * c:4 * c + 4], tiny[:, 0:4], tiny[:, 4:8], ALU.add, ["tiny0", "tiny1"], SK)
            if not need_y:
                return
            tt("dve", v3(t5), v3(zr), cbt, ALU.mult, [zrk, "cosT"], [t5k])
            tt("pool", v3(t7), v3(zr), sbt, ALU.mult, [zrk, "sinT"], [t7k])
            tt("dve", v3(t6), v3(zi), sbt, ALU.mult, [zik, "sinT"], [t6k])
            tt("pool", v3(t8), v3(zi), cbt, ALU.mult, [zik, "cosT"], [t8k])
            if sample:
                tt("dve", SF[:, 0, 4 * c:4 * c + 4, :], v3(t5), v3(t6), ALU.subtract, [t5k, t6k], ["SF"])
                tt("pool", SF[:, 1, 4 * c:4 * c + 4, :], v3(t7), v3(t8), ALU.add, [t7k, t8k], ["SF"])
            else:
                tt("dve", Sb[:, :, 0, 1:1 + nb], v3(t5), v3(t6), ALU.subtract, [t5k, t6k], [Sbk])
                tt("pool", Sb[:, :, 1, 1:1 + nb], v3(t7), v3(t8), ALU.add, [t7k, t8k], [Sbk])

        def segC(t):
            c, c0, ncols, sample, par = t["c"], t["c0"], t["ncols"], t["sample"], t["par"]
            nb = ncols // 4
            UB = UBL[c % 2]; UBk = "UB%d" % (c % 2)
            Sb = SbL[par]; Sbk = "Sb%d" % par
            b = bank()
            for i in range(4):
                oc_ = PS[b][:, i:ncols:4]
                for tau in range(i + 1):
                    mm(oc_, KTt[:, c, tau, :], UB[:, c0 + i - tau:c0 + ncols:4], tau == 0, False, ["KT", UBk], [("ps", b)])
                for ql in range(4):
                    for part in range(2):
                        rhs = H0b[:, part, 4 * c + ql, :] if sample else Sb[:, ql, part, 0:nb]
                        mm(PS[b][32 * ql:32 * ql + 32, i:ncols:4], WCt[:, c, part, i, 32 * ql:32 * ql + 32], rhs, False, (ql == 3 and part == 1),
                           ["WC", "H0b" if sample else Sbk], [("ps", b)], tp=(0, 32 * ql))
            t["ybank"] = b

        def segCact(t):
            c, c0, ncols, b = t["c"], t["c0"], t["ncols"], t["ybank"]
            act(B[:, 8 + c, c0:c0 + ncols], PS[b][:, 0:ncols], AF.Gelu_apprx_tanh, [("ps", b)], [("B", 8 + c)])

        def ssm_pass(segs, need_y):
            bpool[0] = [4, 5, 6, 7]
            tasks = seg_tasks(segs)
            wl = {}

            def U(c):
                wv, wkey = wl.pop(c)
                if c + 1 < 8:
                    wl[c + 1] = load_w(w_in[:, 1024 + 128 * (c + 1):1024 + 128 * (c + 2)], 16, 128)
                for (c0, ncols, _) in segs:
                    b = fm_matmul(wv, wkey, 16, lambda k: A[:, k, :], Akeys, c0, ncols)
                    cp("act", UBL[c % 2][:, c0:c0 + ncols], PS[b][:, 0:ncols], [("ps", b)], ["UB%d" % (c % 2)])

            wl[0] = load_w(w_in[:, 1024:1024 + 128], 16, 128)
            U(0)
            segA(tasks[0])
            segB1(tasks[0])
            for j, t in enumerate(tasks):
                if t["first"] and t["c"] + 1 < 8:
                    U(t["c"] + 1)
                if j + 1 < len(tasks):
                    segA(tasks[j + 1])
                segB(t, need_y)
                if need_y:
                    segC(t)
                if j + 1 < len(tasks):
                    segB1(tasks[j + 1])
                if need_y:
                    segCact(t)
            bpool[0] = list(range(8))

        ssm_pass([(0, 512, False), (512, 512, False)], need_y=False)
        for c, (wv, wkey) in pf(range(8), lambda c: load_w(w_in[:, 128 * c:128 * (c + 1)], 16, 128)):
            b = fm_matmul(wv, wkey, 16, lambda k: A[:, k, :], Akeys, 896, 128)
            cp("act", HISTP[:, c, :], PS[b][:, 113:128], [("ps", b)], ["HISTP"])
        dump("Sst_prefix", Sst[:], [("S", q) for q in range(32)]); dump("HISTP", HISTP[:], ["HISTP"])
        ckpt("prefix")

        for t_i, (r0, R) in enumerate(TT):
            src = xp[r0:r0 + R, :] if t_i < 8 else xs[:, :]
            norm_tile(src, R, r0, 0, ring_i=t_i % 2)
        dump("HT", A[:], [("A", k) for k in range(16)])
        ssm_pass([(0, 512, False), (512, 512, False), (1024, 64, True)], need_y=True)

        SO = WA("SO", 0, 512)
        so2 = [WA("so2_0", 512, 512), WA("so2_1", 1024, 512)]
        for part, (dstp, dsts) in enumerate([(srp, srs), (sip, sis)]):
            b = bank()
            tr(PS[b][0:32, 0:128], Sst[:, part, :], identf[:], [("S", q) for q in range(32)] + ["identf"], [("ps", b)])
            cp("act", SO[0:32, 0:128], PS[b][0:32, 0:128], [("ps", b)], ["SO"])
            dma("sp", dstp, SO[0:32, 0:128], ["SO"], [], "o_SO")
            for qb in range(8):
                b = bank()
                for j in range(4):
                    tr(PS[b][0:16, j * 128:(j + 1) * 128], SF[:, part, 4 * qb + j, :], identf[:], ["SF", "identf"], [("ps", b)])
                i = qb % 2
                cp("act", so2[i][0:16, :], PS[b][0:16, :], [("ps", b)], ["so2_%d" % i])
                dma("sp", dsts[:, qb * 512:(qb + 1) * 512], so2[i][0:16, :], ["so2_%d" % i], [], "o_so2_%d" % i)
        dump("G", B[:, 8:16, :], [("B", k) for k in range(8, 16)]); dump("Sst_main", Sst[:], [("S", q) for q in range(32)]); dump("SF", SF[:], ["SF"])
        ckpt("ssm")

        Z = WA("Z", 0, 1040)
        Zs = WA("Zs", 1040, 304).rearrange("p (s t) -> p s t", t=19)
        PT1 = WA("PT1", 1344, 1040); PT2 = WA("PT2", 2384, 1040)
        PTs1 = WA("PTs1", 3424, 304).rearrange("p (s t) -> p s t", t=19)
        PTs2 = WA("PTs2", 3728, 304).rearrange("p (s t) -> p s t", t=19)
        ZC = WA("ZC", 4096, 240)
        PB = WA("PB", 8448, 2048).rearrange("p (h c) -> p h c", h=2)
        PSO = [WA("PSO0", 11520, 128), WA("PSO1", 11648, 128)]
        PBO = [WA("PBO0", 12032, 128), WA("PBO1", 12160, 128)]
        for h in range(2):
            dma("sp", PB[0:120, h, :], pbuf[h * 120:(h + 1) * 120, :], [], ["PB"], "PB", join=True)
        octr = [0]
        for c, (wv, wkey) in pf(range(8), lambda c: load_w(w_in[:, 128 * c:128 * (c + 1)], 16, 128)):
            wg = c // 2
            w = 2 << wg
            for (c0, ncols) in NTL:
                b = fm_matmul(wv, wkey, 16, lambda k: A[:, k, :], Akeys, c0, ncols)
                if c0 < 1024:
                    cp("act", Z[:, 15 + c0:15 + c0 + ncols], PS[b][:, 0:ncols], [("ps", b)], ["Z"])
                else:
                    cp("act", Zs[:, :, 15:19], PS[b][:, 0:64].rearrange("p (s t) -> p s t", t=4), [("ps", b)], ["Zs"])
            cp("dve", Z[:, 0:15], HISTP[:, c, :], ["HISTP"], ["Z"])
            for h in range(2):
                b = bank()
                tr(PS[b][:, 0:120], PB[0:120, h, 128 * c:128 * (c + 1)], identf[0:120, 0:120], ["PB", "identf"], [("ps", b)])
                cp("dve", Zs[:, 8 * h:8 * h + 8, 0:15], PS[b][:, 0:120].rearrange("p (s t) -> p s t", t=15), [("ps", b)], ["Zs"])
            cur, curk, curs, cursk = Z, "Z", Zs, "Zs"
            bufs = [(PT1, "PT1", PTs1, "PTs1"), (PT2, "PT2", PTs2, "PTs2")]
            for i in range(wg + 1):
                sh = 1 << i
                lo = 2 * sh - 1
                nb, nbk, nbs, nbsk = bufs[i % 2]
                tt("pool", nb[:, lo:1039], cur[:, lo:1039], cur[:, lo - sh:1039 - sh], ALU.add, [curk], [nbk])
                tt("pool", nbs[:, :, lo:19], curs[:, :, lo:19], curs[:, :, lo - sh:19 - sh], ALU.add, [cursk], [nbsk])
                cur, curk, curs, cursk = nb, nbk, nbs, nbsk
            stt("dve", B[:, c, 0:1024], cur[:, 15:1039], 1.0 / w, Z[:, 15:1039], ALU.mult, ALU.subtract, [curk, "Z"], [("B", c)])
            tt("dve", YT[:, 0:15], cur[:, 15:30], invc_t[:, wg, :], ALU.mult, [curk, "invc"], ["YT"])
            tt("dve", B[:, c, 0:15], YT[:, 0:15], Z[:, 15:30], ALU.subtract, ["YT", "Z"], [("B", c)])
            stt("dve", B[:, c, 1024:1088].rearrange("p (s t) -> p s t", t=4), curs[:, :, 15:19], 1.0 / w, Zs[:, :, 15:19], ALU.mult, ALU.subtract, [cursk, "Zs"], [("B", c)])
            i = octr[0] % 2
            octr[0] += 1
            b = bank()
            tr(PS[b][0:15, 0:128], Z[:, 1024:1039], identf[:], ["Z", "identf"], [("ps", b)])
            cp("act", PBO[i][0:15, :], PS[b][0:15, 0:128], [("ps", b)], ["PBO%d" % i])
            dma("sp", pbp[:, 128 * c:128 * (c + 1)], PBO[i][0:15, :], ["PBO%d" % i], [], "o_PBO%d" % i)
            cp("pool", ZC.rearrange("p (s t) -> p s t", t=15), Zs[:, :, 4:19], ["Zs"], ["ZC"])
            for h in range(2):
                b = bank()
                tr(PS[b][0:120, 0:128], ZC[:, 120 * h:120 * (h + 1)], identf[:], ["ZC", "identf"], [("ps", b)])
                cp("act", PSO[h][0:120, :], PS[b][0:120, 0:128], [("ps", b)], ["PSO%d" % h])
                dma("sp", pbs[h * 120:(h + 1) * 120, 128 * c:128 * (c + 1)], PSO[h][0:120, :], ["PSO%d" % h], [], "o_PSO%d" % h)
        dump("POOLED", B[:, 0:8, :], [("B", k) for k in range(8)])
        ckpt("pool")

        for wg, (wv, wkey) in pf(range(4), lambda wg: load_w(w_pool[wg], 2, 256)):
            for (c0, ncols) in NTL:
                bs = []
                for oc in range(2):
                    b = bank()
                    for k in range(2):
                        mm(PS[b][:, 0:ncols], wv[:, k, 128 * oc:128 * (oc + 1)], B[:, 2 * wg + k, c0:c0 + ncols], k == 0, k == 1, [wkey, ("B", 2 * wg + k)], [("ps", b)])
                    bs.append(b)
                for oc in range(2):
                    act(B[:, 2 * wg + oc, c0:c0 + ncols], PS[bs[oc]][:, 0:ncols], AF.Copy, [("ps", bs[oc]), "pscale"], [("B", 2 * wg + oc)], scale=pscale[:, 2 * wg + oc:2 * wg + oc + 1])

        for oc, (wv, wkey) in pf(range(8), lambda oc: load_w(w_glu[:, 128 * oc:128 * (oc + 1)], 8, 128)):
            for (c0, ncols) in NTL:
                b = fm_matmul(wv, wkey, 8, lambda k: B[:, 8 + k, :], lambda k: [("B", 8 + k)], c0, ncols)
                act(YT[:, 0:ncols], PS[b][:, 0:ncols], AF.Sigmoid, [("ps", b), "bglu"], ["YT"], bias=bglu[:, oc:oc + 1])
                tt("dve", A[:, 8 + oc, c0:c0 + ncols], B[:, 8 + oc, c0:c0 + ncols], YT[:, 0:ncols], ALU.mult, ["YT", ("B", 8 + oc)], [("A", 8 + oc)])
        dump("MIXP", B[:, 0:8, :], [("B", k) for k in range(8)]); dump("MIXS", A[:, 8:16, :], [("A", k) for k in range(8, 16)])
        ckpt("glu")

        def proj_resid(wsrc, lhs_fn, lhs_keys_fn, nk, first):
            CB = 256
            if first:
                for ti, (r0, R) in enumerate(TT):
                    src = xp[r0:r0 + R, :] if ti < 8 else xs[:, :]
                    dma("sp", Xt(ti)[0:R, :], src, [], [("X", ti)], ("X", ti))
            for cb, (wv, wkey) in pf(range(D // CB), lambda cb: load_w(wsrc[:, cb * CB:(cb + 1) * CB], nk, CB, tag="wo")):
                for ti, (r0, R) in enumerate(TT):
                    b = bank()
                    for k in range(nk):
                        mm(PS[b][0:R, 0:CB], lhs_fn(k)[:, r0:r0 + R], wv[:, k, :], k == 0, k == nk - 1, [wkey] + lhs_keys_fn(k), [("ps", b)])
                    xs_ap = Xt(ti)[0:R, cb * CB:(cb + 1) * CB]
                    tt("dve", xs_ap, xs_ap, PS[b][0:R, 0:CB], ALU.add, [("X", ti), ("ps", b)], [("X", ti)])

        mixfn = lambda k: (B[:, k, :] if k < 8 else A[:, k, :])
        mixkeys = lambda k: ([("B", k)] if k < 8 else [("A", k)])
        proj_resid(w_out, mixfn, mixkeys, 16, True)
        dump("X1", XC[:], [("X", t) for t in range(9)])
        ckpt("wout")

        for ti, (r0, R) in enumerate(TT):
            norm_tile(None, R, r0, 1, xin=(Xt(ti)[0:R, :], ("X", ti)))
        KT = WA("KT", 10496, 2048, bf=True).rearrange("p (k t) -> p k t", k=16)
        VB = WA("VB", 8448, 2048, bf=True).rearrange("p (m d) -> p m d", m=2)
        KBF = WA("KBF", 0, 2048, bf=True).rearrange("p (m d) -> p m d", m=2)
        EX = [WA("EX0", 4096, 256, bf=True), WA("EX1", 4352, 256, bf=True)]
        RD = WA("RD", 4608, 512)
        dma("pool", VB, mv.rearrange("(m p) d -> p m d", p=128), ["mvd"], ["VB"], "VB")
        dma("pool", KBF, mk.rearrange("(m p) d -> p m d", p=128), ["mkd"], ["KBF"], "KBF")
        for g_ in range(4):
            b = bank()
            pb = PS[b][:].bitcast(BF16)
            for kk_ in range(4):
                kc = 4 * g_ + kk_
                for mt_i in range(2):
                    tr(pb[:, kk_ * 256 + mt_i * 128:kk_ * 256 + (mt_i + 1) * 128], KBF[:, mt_i, kc * 128:(kc + 1) * 128], identb[:], ["KBF", "identb"], [("ps", b)])
            cp("act", KT[:, 4 * g_:4 * g_ + 4, :], pb[:, 0:1024].rearrange("p (k m) -> p k m", k=4), [("ps", b)], ["KT"])
        SCL = 512.0 ** -0.5
        NKV = 4
        KS = [WA("KS%d" % i, i * 512, 512, bf=True).rearrange("p (m d) -> p m d", m=2) for i in range(NKV)]
        VS = [WA("VS%d" % i, 2048 + i * 512, 512, bf=True).rearrange("p (m d) -> p m d", m=2) for i in range(NKV)]
        KTSL = [WA("KTS0", 5120, 512, bf=True).rearrange("p (j m) -> p j m", j=4), WA("KTS1", 5632, 512, bf=True).rearrange("p (j m) -> p j m", j=4)]
        PTSL = [WA("PTS0", 6144, 4, bf=True), WA("PTS1", 6272, 4, bf=True)]
        RDSL = [WA("RDS0", 6160, 4), WA("RDS1", 6288, 4)]

        def prompt_unit(blk, h):
            c0 = blk * 512
            for mt_i in range(2):
                b = bank()
                for j in range(4):
                    mm(PS[b][:, :], KT[:, 4 * h + j, mt_i * 128:(mt_i + 1) * 128], B[:, 4 * h + j, c0:c0 + 512], j == 0, j == 3, ["KT", ("B", 4 * h + j)], [("ps", b)])
                act(EX[mt_i], PS[b][:, :], AF.Exp, [("ps", b)], ["EX%d" % mt_i], scale=SCL)
            b = bank()
            for mt_i in range(2):
                mm(PS[b][:, :], onesb[:], EX[mt_i], mt_i == 0, mt_i == 1, ["onesb", "EX%d" % mt_i], [("ps", b)])
            recip(RD, PS[b][:, :], [("ps", b)], ["RD"])
            for j in range(4):
                b = bank()
                for mt_i in range(2):
                    mm(PS[b][:, :], VB[:, mt_i, 512 * h + 128 * j:512 * h + 128 * (j + 1)], EX[mt_i], mt_i == 0, mt_i == 1, ["VB", "EX%d" % mt_i], [("ps", b)])
                tt("dve", B[:, 4 * h + j, c0:c0 + 512], PS[b][:, :], RD, ALU.mult, [("ps", b), "RD"], [("B", 4 * h + j)])

        sh_list = [(s_, h) for h in range(4) for s_ in range(NSQ)]

        def load_kv(n):
            s_, h = sh_list[n]
            i = n % NKV
            dma("pool", KS[i], ck[s_].rearrange("(m p) d -> p m d", p=128)[:, :, 512 * h:512 * (h + 1)], [], ["KS%d" % i], "KS%d" % i)
            dma("pool", VS[i], cv[s_].rearrange("(m p) d -> p m d", p=128)[:, :, 512 * h:512 * (h + 1)], [], ["VS%d" % i], "VS%d" % i)

        nload = [0]

        def ensure_loaded(upto):
            while nload[0] <= min(upto, len(sh_list) - 1):
                load_kv(nload[0])
                nload[0] += 1

        def sampA(n):
            s_, h = sh_list[n]
            i = n % NKV
            pp = n % 2
            b = bank()
            pb = PS[b][:].bitcast(BF16)
            for j in range(4):
                for mt_i in range(2):
                    tr(pb[:, j * 256 + mt_i * 128:j * 256 + (mt_i + 1) * 128], KS[i][:, mt_i, 128 * j:128 * (j + 1)], identb[:], ["KS%d" % i, "identb"], [("ps", b)])
            cp("act", KTSL[pp], pb[:, 0:1024].rearrange("p (j m) -> p j m", j=4), [("ps", b)], ["KTS%d" % pp])

        def sampB(n):
            s_, h = sh_list[n]
            pp = n % 2
            KTS = KTSL[pp]; PTS = PTSL[pp]
            qc = 1024 + 4 * s_
            b = bank()
            for mt_i in range(2):
                for j in range(4):
                    mm(PS[b][:, mt_i * 4:mt_i * 4 + 4], KTS[:, j, mt_i * 128:(mt_i + 1) * 128], B[:, 4 * h + j, qc:qc + 4], j == 0, j == 3, ["KTS%d" % pp, ("B", 4 * h + j)], [("ps", b)])
            act(PTS, PS[b][:, 0:8], AF.Exp, [("ps", b)], ["PTS%d" % pp], scale=SCL)

        def sampC(n):
            s_, h = sh_list[n]
            i = n % NKV
            pp = n % 2
            PTS = PTSL[pp]; RDS = RDSL[pp]
            qc = 1024 + 4 * s_
            b = bank()
            for mt_i in range(2):
                mm(PS[b][:, 0:4], onesb[:], PTS[:, mt_i * 4:mt_i * 4 + 4], mt_i == 0, mt_i == 1, ["onesb", "PTS%d" % pp], [("ps", b)])
            recip(RDS, PS[b][:, 0:4], [("ps", b)], ["RDS%d" % pp])
            b = bank()
            for j in range(4):
                for mt_i in range(2):
                    mm(PS[b][:, j * 4:j * 4 + 4], VS[i][:, mt_i, 128 * j:128 * (j + 1)], PTS[:, mt_i * 4:mt_i * 4 + 4], mt_i == 0, mt_i == 1, ["VS%d" % i, "PTS%d" % pp], [("ps", b)])
            tt("dve", B[:, 4 * h:4 * h + 4, qc:qc + 4], PS[b][:, 0:16].rearrange("p (j t) -> p j t", j=4), RDS.unsqueeze(1).to_broadcast([128, 4, 4]), ALU.mult,
               [("ps", b), "RDS%d" % pp], [("B", 4 * h + j) for j in range(4)])

        def sample_group(n0, cnt):
            sampA(n0)
            for k_ in range(cnt):
                n = n0 + k_
                if k_ + 1 < cnt:
                    sampA(n + 1)
                sampB(n)
                if k_ >= 1:
                    sampC(n - 1)
                    ensure_loaded(n - 1 + NKV)
            sampC(n0 + cnt - 1)
            ensure_loaded(n0 + cnt - 1 + NKV)

        ensure_loaded(NKV - 1)
        n_it = 0
        wq = {0: load_w(w_q[:, 0:128], 16, 128)}
        for h in range(4):
            for oc in range(4 * h, 4 * h + 4):
                wv, wkey = wq.pop(oc)
                if oc + 1 < 16:
                    wq[oc + 1] = load_w(w_q[:, 128 * (oc + 1):128 * (oc + 2)], 16, 128)
                for (c0, ncols) in NTL:
                    b = fm_matmul(wv, wkey, 16, lambda k: A[:, k, :], Akeys, c0, ncols)
                    cp("act", B[:, oc, c0:c0 + ncols], PS[b][:, 0:ncols], [("ps", b)], [("B", oc)])
            for blk in range(2):
                prompt_unit(blk, h)
                sample_group(n_it, 8)
                n_it += 8
        ckpt("attn_s")
        dump("OT", B[:], [("B", k) for k in range(16)])
        proj_resid(w_o, lambda k: B[:, k, :], lambda k: [("B", k)], 16, False)
        dump("X2", XC[:], [("X", t) for t in range(9)])
        ckpt("attn")

        for ti, (r0, R) in enumerate(TT):
            norm_tile(None, R, r0, 3, xin=(Xt(ti)[0:R, :], ("X", ti)))
        GU = [WA("gu%d" % i, i * 1024, 1024, bf=True).rearrange("p (k c) -> p k c", k=16) for i in range(4)]
        DN = [WA("dn%d" % i, 4096 + i * 4096, 4096, bf=True).rearrange("p (k c) -> p k c", k=4) for i in range(2)]
        SG = WA("SG", 12288, 512)
        guc = [0]

        def load_gu(ffc):
            out = []
            for src in (w_gate, w_up):
                i = guc[0] % 4
                guc[0] += 1
                dma("pool", GU[i], src[:, 128 * ffc:128 * (ffc + 1)].rearrange("(k p) c -> p k c", p=128), [], ["gu%d" % i], "gu%d" % i)
                out.append(i)
            return out

        def load_dn(grp):
            i = grp % 2
            dma("pool", DN[i], w_down[512 * grp:512 * (grp + 1), :].rearrange("(k p) c -> p k c", p=128), [], ["dn%d" % i], "dn%d" % i)
            return i

        dn_next = load_dn(0)
        for ffc, (gi_, ui_) in pf(range(44), load_gu):
            grp, fc = ffc // 4, ffc % 4
            slot = (grp % 2) * 4 + fc
            for (c0, ncols) in NTL:
                bg = fm_matmul(GU[gi_], "gu%d" % gi_, 16, lambda k: A[:, k, :], Akeys, c0, ncols)
                bu = fm_matmul(GU[ui_], "gu%d" % ui_, 16, lambda k: A[:, k, :], Akeys, c0, ncols)
                act(SG[:, 0:ncols], PS[bg][:, 0:ncols], AF.Silu, [("ps", bg)], ["SG"])
                tt("dve", B[:, slot, c0:c0 + ncols], SG[:, 0:ncols], PS[bu][:, 0:ncols], ALU.mult, ["SG", ("ps", bu)], [("B", slot)])
            if fc == 3:
                di = dn_next
                if grp + 1 < 11:
                    dn_next = load_dn(grp + 1)
                for ti, (r0, R) in enumerate(TT):
                    for cb in range(4):
                        b = bank()
                        for f2 in range(4):
                            s2 = (grp % 2) * 4 + f2
                            mm(PS[b][0:R, :], B[:, s2, r0:r0 + R], DN[di][:, f2, cb * 512:(cb + 1) * 512], f2 == 0, f2 == 3, ["dn%d" % di, ("B", s2)], [("ps", b)])
                        xs_ap = Xt(ti)[0:R, cb * 512:(cb + 1) * 512]
                        tt("dve", xs_ap, xs_ap, PS[b][0:R, :], ALU.add, [("X", ti), ("ps", b)], [("X", ti)])
        gfin = WA("gfin", 4096, 2048)
        dma("sp", gfin, g_final.partition_broadcast(128), [], ["gfin"], "gfin")
        YO = [WA("YO0", 0, 2048), WA("YO1", 2048, 2048)]
        for ti, (r0, R) in enumerate(TT):
            xk = [("X", ti)]
            xt_ap = Xt(ti)[0:R, :]
            i = ti % 2
            yk = "YO%d" % i
            o_ = 4 * i
            ka, kb, kc_ = "stat%d" % o_, "stat%d" % (o_ + 1), "stat%d" % (o_ + 2)
            act(YO[i][0:R, :], xt_ap, AF.Square, xk, [yk, ka], accum=stat[0:R, o_:o_ + 1])
            ts("dve", stat[0:R, o_ + 1:o_ + 2], stat[0:R, o_:o_ + 1], 1.0 / D, EPS, ALU.mult, ALU.add, [ka], [kb])
            act(stat[0:R, o_ + 1:o_ + 2], stat[0:R, o_ + 1:o_ + 2], AF.Sqrt, [kb], [kb])
            recip(stat[0:R, o_ + 2:o_ + 3], stat[0:R, o_ + 1:o_ + 2], [kb], [kc_])
            stt("dve", YO[i][0:R, :], xt_ap, stat[0:R, o_ + 2:o_ + 3], gfin[0:R, :], ALU.mult, ALU.mult, xk + [kc_, "gfin"], [yk])
            dst = yp[r0:r0 + R, :] if ti < 8 else ys[:, :]
            dma("sp", dst, YO[i][0:R, :], [yk], [], "o_" + yk)

    try:
        body()
    except _Stop:
        pass
    P.emit(nc, st)
    st.close()
    return nc


_NC = None


def make_in_maps(inp):
    f = lambda a: np.ascontiguousarray(np.asarray(a, dtype=np.float32))
    x_prompt = f(inp["x_prompt"]); x_sample = f(inp["x_sample"]); mem_prompt = f(inp["mem_prompt"])
    spb = f(inp["state_pool_buf"]); s_re = f(inp["state_ssm_re"]); s_im = f(inp["state_ssm_im"])
    cmk = f(inp["cache_mem_k"]); cmv = f(inp["cache_mem_v"])
    shared = {
        "g_mix": f(inp["g_mix"][0]), "w_in": f(inp["w_in"][0]), "w_pool": f(inp["w_pool"][0]), "pool_scale": f(inp["pool_scale"][0]),
        "lam_re": f(inp["ssm_lam_re"][0]), "lam_im": f(inp["ssm_lam_im"][0]), "log_step": f(inp["ssm_log_step"][0]),
        "b_re": f(inp["ssm_b_re"][0]), "b_im": f(inp["ssm_b_im"][0]), "c_re": f(inp["ssm_c_re"][0]), "c_im": f(inp["ssm_c_im"][0]),
        "ssm_d": f(inp["ssm_d"][0]), "w_glu": f(inp["w_glu"][0]), "b_glu": f(inp["b_glu"][0]), "w_out": f(inp["w_out"][0]),
        "g_cross": f(inp["g_cross"][0]), "g_mem": f(inp["g_mem"][0]), "w_q": f(inp["w_q"][0]), "w_k": f(inp["w_k"][0]), "w_v": f(inp["w_v"][0]),
        "w_o": f(inp["w_o"][0]), "g_ffn": f(inp["g_ffn"][0]), "w_gate": f(inp["w_gate"][0]), "w_up": f(inp["w_up"][0]), "w_down": f(inp["w_down"][0]),
        "g_final": f(inp["g_final"]), "ident": np.eye(128, dtype=np.float32),
        "bmask": np.kron(np.eye(8, dtype=np.float32), np.ones((16, 16), np.float32)),
    }
    in_maps = []
    for c in range(8):
        b, half = c // 2, c % 2
        m = dict(shared)
        m["xp"] = f(x_prompt[b, half * 1024:(half + 1) * 1024])
        m["xprev"] = f(x_prompt[b, 0:1024]) if half == 1 else np.zeros((1024, D), np.float32)
        m["xs"] = f(x_sample[16 * c:16 * c + 16].reshape(64, D))
        m["mem"] = f(mem_prompt[b])
        m["pbuf"] = f(spb[0, 16 * c:16 * c + 16].reshape(240, 1024))
        m["sre"] = f(s_re[0, 16 * c:16 * c + 16].reshape(16, 4096))
        m["sim"] = f(s_im[0, 16 * c:16 * c + 16].reshape(16, 4096))
        m["ck"] = f(cmk[0, 16 * c:16 * c + 16].reshape(16, 256, D))
        m["cv"] = f(cmv[0, 16 * c:16 * c + 16].reshape(16, 256, D))
        ic = np.zeros((128, 4, 15), np.float32)
        for wg, w in enumerate((2, 4, 8, 16)):
            for t in range(15):
                ic[:, wg, t] = 1.0 / min(half * 1024 + t + 1, w)
        m["invc"] = ic
        in_maps.append(m)
    return in_maps


def kernel(**inp):
    global _NC
    in_maps = make_in_maps(inp)
    if _NC is None:
        _NC = build()
    res = run_bass_kernel_spmd(_NC, in_maps, core_ids=list(range(8))).results
    y_prompt = np.stack([np.concatenate([res[2 * b]["yp"], res[2 * b + 1]["yp"]], 0) for b in range(4)])
    y_sample = np.concatenate([res[c]["ys"].reshape(16, 4, D) for c in range(8)], 0)
    pb_p = np.stack([res[2 * b + 1]["pbp"] for b in range(4)])[None]
    re_p = np.stack([res[2 * b + 1]["srp"].reshape(64, 64) for b in range(4)])[None]
    im_p = np.stack([res[2 * b + 1]["sip"].reshape(64, 64) for b in range(4)])[None]
    mk_p = np.stack([res[2 * b]["mk"].reshape(256, 4, 512) for b in range(4)])[None]
    mv_p = np.stack([res[2 * b]["mv"].reshape(256, 4, 512) for b in range(4)])[None]
    pb_s = np.concatenate([res[c]["pbs"].reshape(16, 15, 1024) for c in range(8)], 0)[None]
    re_s = np.concatenate([res[c]["srs"].reshape(16, 64, 64) for c in range(8)], 0)[None]
    im_s = np.concatenate([res[c]["sis"].reshape(16, 64, 64) for c in range(8)], 0)[None]
    return (y_prompt.astype(np.float32), y_sample.astype(np.float32), pb_p.astype(np.float32), re_p.astype(np.float32), im_p.astype(np.float32),
            mk_p.astype(np.float32), mv_p.astype(np.float32), pb_s.astype(np.float32), re_s.astype(np.float32), im_s.astype(np.float32))
```

```python
import contextlib
import numpy as np
import concourse.bass as bass
import concourse.mybir as mybir
from concourse.bass_utils import run_bass_kernel_spmd

F32 = mybir.dt.float32
BF16 = mybir.dt.bfloat16
AF = mybir.ActivationFunctionType
ALU = mybir.AluOpType

D = 2048
NPR = 1024
NSQ = 16
NS = 64
N = NPR + NS
DFF = 5632
NTL = [(0, 512), (512, 512), (1024, 64)]
TT = [(i * 128, 128) for i in range(8)] + [(1024, 64)]
EPS = 1e-6


class _Op:
    __slots__ = ("eng", "fn", "deps", "dsem", "val", "sem", "need")

    def __init__(self, eng, fn, deps, dsem):
        self.eng = eng
        self.fn = fn
        self.deps = deps
        self.dsem = dsem
        self.val = 0
        self.sem = None
        self.need = False


class Prog:
    ENGS = ("pe", "act", "dve", "pool", "sp")

    def __init__(self):
        self.ops = []
        self.base_w = {}
        self.join_w = {}
        self.readers = {}
        self.aliases = {}
        self.spacer = None

    def alias(self, name, keys):
        self.aliases[name] = list(keys)

    def _expand(self, keys):
        out = []
        for k in keys:
            a = self.aliases.get(k)
            if a is None:
                out.append(k)
            else:
                out.extend(a)
        return out

    def op(self, eng, fn, reads=(), writes=(), dsem=None, join=False):
        idx = len(self.ops)
        reads = self._expand(reads)
        writes = self._expand(writes)
        deps = {}

        def add(d, raw):
            if d is None or d == idx:
                return
            deps[d] = deps.get(d, False) or raw

        for k in reads:
            add(self.base_w.get(k), True)
            for d in self.join_w.get(k, ()):
                add(d, True)
        for k in writes:
            add(self.base_w.get(k), False)
            if not join:
                for d in self.join_w.get(k, ()):
                    add(d, False)
            for d in self.readers.get(k, ()):
                add(d, False)
        for k in reads:
            self.readers.setdefault(k, []).append(idx)
        for k in writes:
            if join:
                self.join_w.setdefault(k, []).append(idx)
            else:
                self.base_w[k] = idx
                self.join_w[k] = []
                self.readers[k] = []
        self.ops.append(_Op(eng, fn, deps, dsem))
        return idx

    def emit(self, nc, stack):
        ops = self.ops
        pos = {}
        cnt = {e: 0 for e in self.ENGS}
        for i, o in enumerate(ops):
            if o.dsem is None:
                pos[i] = cnt[o.eng]
                cnt[o.eng] += 1
        waits = []
        spacers = set()
        for i, o in enumerate(ops):
            w = set()
            best = {}
            for d, raw in o.deps.items():
                od = ops[d]
                if od.dsem is not None:
                    w.add(d)
                    continue
                if o.dsem is None and od.eng == o.eng:
                    if o.eng == "pe":
                        continue
                    if o.eng in ("dve", "act"):
                        if not raw or pos[i] - pos[d] >= 3:
                            continue
                        if o.eng == "dve" and self.spacer is not None:
                            spacers.add(i)
                            continue
                if d > best.get(od.eng, -1):
                    best[od.eng] = d
            w.update(best.values())
            waits.append(w)
            for d in w:
                ops[d].need = True
        esem = {e: stack.enter_context(nc.semaphore("s_" + e)) for e in self.ENGS}
        dsems = {}
        ecount = {e: 0 for e in self.ENGS}
        dcount = {}
        for o in ops:
            if o.dsem is not None:
                if o.dsem not in dsems:
                    dsems[o.dsem] = stack.enter_context(nc.semaphore("d_" + o.dsem))
                    dcount[o.dsem] = 0
                dcount[o.dsem] += 16
                o.sem = dsems[o.dsem]
                o.val = dcount[o.dsem]
                o.need = True
            elif o.need:
                ecount[o.eng] += 1
                o.sem = esem[o.eng]
                o.val = ecount[o.eng]
        block = stack.enter_context(nc.Block())
        final = dict(dcount)

        def run(engname, e):
            waited = {}
            for i, o in enumerate(ops):
                if o.eng != engname:
                    continue
                need = {}
                for d in waits[i]:
                    od = ops[d]
                    if need.get(od.sem, (0, None))[0] < od.val:
                        need[od.sem] = (od.val, od.sem)
                for key, (v, s_) in need.items():
                    if waited.get(key, 0) < v:
                        e.wait_ge(s_, v)
                        waited[key] = v
                if i in spacers:
                    self.spacer(e)
                ins = o.fn(e)
                if o.need:
                    ins.then_inc(o.sem, 16 if o.dsem is not None else 1)
            if engname == "sp":
                for name, v in final.items():
                    e.wait_ge(dsems[name], v)

        @block.tensor
        def _(e):
            run("pe", e)

        @block.scalar
        def _(e):
            run("act", e)

        @block.vector
        def _(e):
            run("dve", e)

        @block.gpsimd
        def _(e):
            run("pool", e)

        @block.sync
        def _(e):
            run("sp", e)


def build(stop=None, dbg=False):
    nc = bass.Bass("TRN2", target_bir_lowering=False)
    st = contextlib.ExitStack()
    P = Prog()

    def din(name, shape):
        return nc.dram_tensor(name, shape, F32, kind="ExternalInput").ap()

    def dout(name, shape):
        return nc.dram_tensor(name, shape, F32, kind="ExternalOutput").ap()

    xp = din("xp", [NPR, D]); xprev = din("xprev", [NPR, D]); xs = din("xs", [NS, D]); mem = din("mem", [256, D])
    pbuf = din("pbuf", [240, 1024]); sre = din("sre", [16, 4096]); sim = din("sim", [16, 4096])
    ck = din("ck", [16, 256, D]); cv = din("cv", [16, 256, D])
    invc = din("invc", [128, 4, 15]); ident = din("ident", [128, 128]); bmask = din("bmask", [128, 128])
    g_mix = din("g_mix", [D]); w_in = din("w_in", [D, D]); w_pool = din("w_pool", [4, 256, 256]); pool_scale = din("pool_scale", [1024])
    lam_re = din("lam_re", [64, 64]); lam_im = din("lam_im", [64, 64]); log_step = din("log_step", [64])
    b_re = din("b_re", [64, 64, 16]); b_im = din("b_im", [64, 64, 16]); c_re = din("c_re", [64, 16, 64]); c_im = din("c_im", [64, 16, 64])
    ssm_d = din("ssm_d", [1024]); w_glu = din("w_glu", [1024, 1024]); b_glu = din("b_glu", [1024]); w_out = din("w_out", [D, D])
    g_cross = din("g_cross", [D]); g_mem = din("g_mem", [D]); w_q = din("w_q", [D, D]); w_k = din("w_k", [D, D]); w_v = din("w_v", [D, D])
    w_o = din("w_o", [D, D]); g_ffn = din("g_ffn", [D]); w_gate = din("w_gate", [D, DFF]); w_up = din("w_up", [D, DFF]); w_down = din("w_down", [DFF, D])
    g_final = din("g_final", [D])
    yp = dout("yp", [NPR, D]); ys = dout("ys", [NS, D]); pbp = dout("pbp", [15, 1024]); srp = dout("srp", [32, 128]); sip = dout("sip", [32, 128])
    mk = dout("mk", [256, D]); mv = dout("mv", [256, D]); pbs = dout("pbs", [240, 1024]); srs = dout("srs", [16, 4096]); sis = dout("sis", [16, 4096])

    def sb(name, shape, dt=F32):
        return st.enter_context(nc.sbuf_tensor(name, shape, dt))

    XC = sb("XC", [128, 9 * D])
    A = sb("A", [128, 16, N], BF16)
    B = sb("B", [128, 16, N], BF16)
    WSZ = 12800
    W = sb("W", [128, WSZ])
    identf = sb("identf", [128, 128]); identb = sb("identb", [128, 128], BF16); onesb = sb("onesb", [128, 128], BF16)
    gv = sb("gv", [128, 4, 16])
    pv = sb("pv", [128, 24]); pscale = pv[:, 0:8]; dvec = pv[:, 8:16]; bglu = pv[:, 16:24]
    invc_t = sb("invc_t", [128, 4, 15])
    sq = sb("sq", [128, 32, 24])
    H0b = sb("H0b", [128, 2, 32, 16], BF16)

    Sst = sb("Sst", [128, 2, 32])
    H0 = sb("H0", [128, 2, 32, 16]); SF = sb("SF", [128, 2, 32, 16])
    HISTP = sb("HISTP", [128, 8, 15])
    stat = sb("stat", [128, 8]); tiny = sb("tiny", [128, 8]); spc = sb("spc", [128, 2])
    P.spacer = None
    PSALL = st.enter_context(nc.psum_tensor("psall", [128, 8 * 512], F32))
    PS = [PSALL[:, i * 512:(i + 1) * 512] for i in range(8)]
    pctr = [0]
    bpool = [list(range(8))]

    def bank():
        b = bpool[0][pctr[0] % len(bpool[0])]
        pctr[0] += 1
        return b

    WBLK = 128

    def WA(name, off, n, bf=False):
        assert off + n <= WSZ, (name, off, n)
        P.alias(name, [("W", b) for b in range(off // WBLK, (off + n - 1) // WBLK + 1)])
        v = W[:, off:off + n]
        return v.bitcast(BF16) if bf else v

    cosT = XC[:, 0:4096].rearrange("p (q j) -> p q j", q=32)
    sinT = XC[:, 4096:8192].rearrange("p (q j) -> p q j", q=32)
    WBt = XC[:, 8192:12288].bitcast(BF16).rearrange("p (c a i j) -> p c a i j", c=8, a=2, i=4)
    WCt = XC[:, 12288:16384].bitcast(BF16).rearrange("p (c a i j) -> p c a i j", c=8, a=2, i=4)
    KTt = XC[:, 16384:18432].bitcast(BF16).rearrange("p (c i j) -> p c i j", c=8, i=4)
    P.alias("cosT", [("XC", 0), ("XC", 1)]); P.alias("sinT", [("XC", 2), ("XC", 3)])
    P.alias("WB", [("XC", 4), ("XC", 5)]); P.alias("WC", [("XC", 6), ("XC", 7)]); P.alias("KT", [("XC", 8)])
    for ti in range(9):
        P.alias(("X", ti), [("XC", ti)])
    TAB = ["cosT", "sinT"]

    def Xt(ti):
        return XC[:, ti * D:(ti + 1) * D]

    def dma(eng, out, in_, reads, writes, buf, join=False, slow=False):
        sem = buf if isinstance(buf, str) else "_".join(str(x) for x in buf)
        if slow:
            return P.op(eng, lambda e: e.dma_start(out=out, in_=in_, allow_slow_non_contiguous=True), reads, writes, dsem=sem, join=join)
        return P.op(eng, lambda e: e.dma_start(out=out, in_=in_), reads, writes, dsem=sem, join=join)

    def mm(out, lhsT, rhs, start, stop, reads, writes, tp=None):
        if tp is None:
            P.op("pe", lambda e: e.matmul(out, lhsT=lhsT, rhs=rhs, start=start, stop=stop), reads, writes)
        else:
            P.op("pe", lambda e: e.matmul(out, lhsT=lhsT, rhs=rhs, start=start, stop=stop, tile_position=tp), reads, writes)

    def tr(out, in_, idn, reads, writes):
        P.op("pe", lambda e: e.transpose(out=out, in_=in_, identity=idn), reads, writes)

    def tt(eng, out, in0, in1, op, reads, writes):
        P.op(eng, lambda e: e.tensor_tensor(out=out, in0=in0, in1=in1, op=op), reads, writes)

    def ts(eng, out, in0, s1, s2, op0, op1, reads, writes):
        if s2 is None:
            P.op(eng, lambda e: e.tensor_scalar(out=out, in0=in0, scalar1=s1, scalar2=None, op0=op0), reads, writes)
        else:
            P.op(eng, lambda e: e.tensor_scalar(out=out, in0=in0, scalar1=s1, scalar2=s2, op0=op0, op1=op1), reads, writes)

    def stt(eng, out, in0, scalar, in1, op0, op1, reads, writes):
        P.op(eng, lambda e: e.scalar_tensor_tensor(out=out, in0=in0, scalar=scalar, in1=in1, op0=op0, op1=op1), reads, writes)

    def act(out, in_, func, reads, writes, scale=None, bias=None, accum=None):
        kw = {}
        if scale is not None:
            kw["scale"] = scale
        if bias is not None:
            kw["bias"] = bias
        if accum is not None:
            kw["accum_out"] = accum
        P.op("act", lambda e: e.activation(out=out, in_=in_, func=func, **kw), reads, writes)

    def cp(eng, out, in_, reads, writes):
        if eng == "act":
            P.op("act", lambda e: e.copy(out=out, in_=in_), reads, writes)
        else:
            P.op(eng, lambda e: e.tensor_copy(out=out, in_=in_), reads, writes)

    def ms(eng, ap, val, writes):
        P.op(eng, lambda e: e.memset(ap, val), (), writes)

    def recip(out, in_, reads, writes):
        P.op("dve", lambda e: e.reciprocal(out=out, in_=in_), reads, writes)

    def pf(items, loader):
        items = list(items)
        nxt = loader(items[0])
        for i, it in enumerate(items):
            cur = nxt
            if i + 1 < len(items):
                nxt = loader(items[i + 1])
            yield it, cur

    class _Stop(Exception):
        pass

    def ckpt(name):
        if stop == name:
            raise _Stop()

    def dump(name, ap, keys):
        if not dbg:
            return
        t = nc.dram_tensor("dbg_" + name, list(ap.shape), ap.dtype, kind="ExternalOutput").ap()
        P.op("sp", lambda e: e.dma_start(out=t, in_=ap), keys, [], dsem="dbg_" + name)

    def body():
        dma("sp", identf[:], ident, [], ["identf"], "identf")
        bmask_t = WA("bmask", 10368, 128)
        dma("sp", bmask_t[:], bmask, [], ["bmask"], "bmask")
        dma("pool", identb[:], ident, [], ["identb"], "identb")
        ms("dve", onesb[:], 1.0, ["onesb"])
        stG = WA("stG", 11520, 128); st2 = WA("st2", 11648, 128)
        for i, g in enumerate([g_mix, g_cross, g_mem, g_ffn]):
            dma("sp", stG[16 * i:16 * i + 16, :], g.rearrange("(k p) -> k p", p=128), [], ["stG"], "stG", join=True)
        for i, g in enumerate([pool_scale, ssm_d, b_glu]):
            dma("sp", st2[8 * i:8 * i + 8, :], g.rearrange("(k p) -> k p", p=128), [], ["st2"], "st2", join=True)
        dma("sp", invc_t[:], invc, [], ["invc"], "invc")
        b = bank()
        tr(PS[b][:, 0:64], stG[0:64, :], identf[0:64, 0:64], ["stG", "identf"], [("ps", b)])
        tr(PS[b][:, 64:88], st2[0:24, :], identf[0:24, 0:24], ["st2", "identf"], [("ps", b)])
        cp("dve", gv[:].rearrange("p i k -> p (i k)"), PS[b][:, 0:64], [("ps", b)], ["gv"])
        cp("dve", pv[:], PS[b][:, 64:88], [("ps", b)], ["pscale", "dvec", "bglu"])
        LRE, LIM, DL, XR, TH, RR, CC, SS, T1, T2, T3, ARE, AIM, FRE, FIM, DEN, AM1 = range(17)
        stL = [WA("stL0", 11776, 128), WA("stL1", 11904, 128), WA("stL2", 12032, 128)]
        lsT = WA("lsT", 12160, 2)
        dma("sp", stL[0][0:32, :].rearrange("q (t p) -> q t p", t=2), lam_re.rearrange("(q two) p -> q two p", two=2), [], ["stL0"], "stL0")
        dma("sp", stL[1][0:32, :].rearrange("q (t p) -> q t p", t=2), lam_im.rearrange("(q two) p -> q two p", two=2), [], ["stL1"], "stL1")
        dma("sp", lsT[0:32, :], log_step.rearrange("(q two) -> q two", two=2), [], ["lsT"], "lsT")
        cp("dve", stL[2][0:32, :].rearrange("q (t p) -> q t p", t=2), lsT[0:32, :].unsqueeze(2).to_broadcast([32, 2, 64]), ["lsT"], ["stL2"])
        b = bank()
        for i_, col in enumerate((LRE, LIM, DL)):
            tr(PS[b][:, 32 * i_:32 * i_ + 32], stL[i_][0:32, :], identf[0:32, 0:32], ["stL%d" % i_, "identf"], [("ps", b)])
        for i_, col in enumerate((LRE, LIM, DL)):
            cp("dve", sq[:, :, col], PS[b][:, 32 * i_:32 * i_ + 32], [("ps", b)], [("sq", col)])
        XT = [WA("XT0", 0, 2048), WA("XT1", 2048, 2048)]
        HNL = [WA("HN0", 4096, 1024, bf=True), WA("HN1", 5120, 1024, bf=True)]
        for _p in range(2):
            P.alias("HN%da" % _p, [("W", b_) for b_ in range((4096 + 1024 * _p) // WBLK, (4096 + 1024 * _p + 512) // WBLK)])
            P.alias("HN%db" % _p, [("W", b_) for b_ in range((4096 + 1024 * _p + 512) // WBLK, (4096 + 1024 * _p + 1024) // WBLK)])
        nctr = [0]

        def norm_from(xt_ap, xkey, R, gi, dst_fn, dst_keys):
            par = nctr[0] % 2
            nctr[0] += 1
            HN = HNL[par]; hk = "HN%d" % par
            s0, s1, s2 = 4 * par, 4 * par + 1, 4 * par + 2
            k0, k1, k2 = "stat%d" % s0, "stat%d" % s1, "stat%d" % s2
            act(HN[0:R, :], xt_ap, AF.Square, [xkey], [hk + "a", hk + "b", k0], accum=stat[0:R, s0:s0 + 1])
            ts("dve", stat[0:R, s1:s1 + 1], stat[0:R, s0:s0 + 1], 1.0 / D, EPS, ALU.mult, ALU.add, [k0], [k1])
            act(stat[0:R, s1:s1 + 1], stat[0:R, s1:s1 + 1], AF.Sqrt, [k1], [k1])
            recip(stat[0:R, s2:s2 + 1], stat[0:R, s1:s1 + 1], [k1], [k2])
            act(HN[0:R, 0:1024], xt_ap[:, 0:1024], AF.Copy, [xkey, k2], [hk + "a"], scale=stat[0:R, s2:s2 + 1])
            ts("dve", HN[0:R, 1024:2048], xt_ap[:, 1024:2048], stat[0:R, s2:s2 + 1], None, ALU.mult, None, [xkey, k2], [hk + "b"])
            for hb in range(2):
                b = bank()
                pb = PS[b][:].bitcast(BF16)
                for j in range(8):
                    kc = hb * 8 + j
                    tr(pb[:, j * 128:j * 128 + R], HN[0:R, kc * 128:(kc + 1) * 128], identb[0:R, 0:R], [hk + ("a" if hb == 0 else "b"), "identb"], [("ps", b)])
                src3 = pb[:, 0:1024].rearrange("p (k t) -> p k t", k=8)[:, :, 0:R]
                gb = gv[:, gi, hb * 8:hb * 8 + 8].unsqueeze(2).to_broadcast([128, 8, R])
                tt("dve", dst_fn(hb), src3, gb, ALU.mult, [("ps", b), "gv"], dst_keys(hb))

        def norm_tile(src_rows, R, col0, gi, ring_i=0, xin=None):
            if xin is None:
                xt_ap = XT[ring_i][0:R, :]
                xkey = "XT%d" % ring_i
                dma("sp", xt_ap, src_rows, [], [xkey], xkey)
            else:
                xt_ap, xkey = xin
            norm_from(xt_ap, xkey, R, gi, lambda hb: A[:, hb * 8:hb * 8 + 8, col0:col0 + R], lambda hb: [("A", k) for k in range(hb * 8, hb * 8 + 8)])

        stgL = [WA("stg0", 10496, 512), WA("stg1", 11008, 512)]
        for part, src in enumerate([sre, sim]):
            for qb in range(8):
                stg = stgL[qb % 2]; stgk = "stg%d" % (qb % 2)
                dma("sp", stg[0:16, :], src[:, qb * 512:(qb + 1) * 512], [], [stgk], stgk)
                b = bank()
                for j in range(4):
                    tr(PS[b][:, j * 16:(j + 1) * 16], stg[0:16, j * 128:(j + 1) * 128], identf[0:16, 0:16], [stgk, "identf"], [("ps", b)])
                cp("dve", H0[:, part, 4 * qb:4 * qb + 4, :], PS[b][:, 0:64].rearrange("p (q s) -> p q s", q=4), [("ps", b)], ["H0"])
        cp("pool", H0b[:], H0[:], ["H0"], ["H0b"])
        for t_i in range(8):
            norm_tile(xprev[t_i * 128:(t_i + 1) * 128, :], 128, t_i * 128, 0, ring_i=t_i % 2)


        def sqv(i):
            return sq[:, :, i]

        P.alias("sq", [("sq", c_) for c_ in range(24)])

        K = ["sq"]
        import math as _m

        def horner(dst, xcol, cf, eng="dve"):
            ts(eng, sqv(dst), sqv(xcol), float(cf[-1]), float(cf[-2]), ALU.mult, ALU.add, [("sq", xcol)], [("sq", dst)])
            for c_ in reversed(cf[:-2]):
                tt(eng, sqv(dst), sqv(dst), sqv(xcol), ALU.mult, [("sq", dst), ("sq", xcol)], [("sq", dst)])
                ts(eng, sqv(dst), sqv(dst), float(c_), None, ALU.add, None, [("sq", dst)], [("sq", dst)])

        ecf = [1.0 / _m.factorial(i) for i in range(13)]
        ts("dve", sqv(T1), sqv(DL), 0.125, None, ALU.mult, None, [("sq", DL)], [("sq", T1)])
        tt("dve", sqv(T2), sqv(T1), sqv(T1), ALU.mult, [("sq", T1), ("sq", T1)], [("sq", T2)])
        horner(DL, T2, ecf[0::2], "dve")
        horner(T3, T2, ecf[1::2], "pool")
        tt("dve", sqv(T3), sqv(T3), sqv(T1), ALU.mult, [("sq", T3), ("sq", T1)], [("sq", T3)])
        tt("dve", sqv(DL), sqv(DL), sqv(T3), ALU.add, [("sq", DL), ("sq", T3)], [("sq", DL)])
        for _ in range(3):
            tt("dve", sqv(DL), sqv(DL), sqv(DL), ALU.mult, [("sq", DL), ("sq", DL)], [("sq", DL)])
        tt("dve", sqv(XR), sqv(LRE), sqv(DL), ALU.mult, [("sq", LRE), ("sq", DL)], [("sq", XR)])
        tt("dve", sqv(TH), sqv(LIM), sqv(DL), ALU.mult, [("sq", LIM), ("sq", DL)], [("sq", TH)])
        ts("dve", sqv(T1), sqv(TH), 1.0 / 16, None, ALU.mult, None, [("sq", TH)], [("sq", T1)])
        tt("dve", sqv(T2), sqv(T1), sqv(T1), ALU.mult, [("sq", T1), ("sq", T1)], [("sq", T2)])
        sc = [(-1.0) ** i / _m.factorial(2 * i + 1) for i in range(7)]
        cc_ = [(-1.0) ** i / _m.factorial(2 * i) for i in range(8)]
        horner(SS, T2, sc, "dve")
        horner(CC, T2, cc_, "pool")
        tt("dve", sqv(SS), sqv(SS), sqv(T1), ALU.mult, [("sq", SS), ("sq", T1)], [("sq", SS)])
        ts("pool", sqv(AM1), sqv(XR), 0.25, None, ALU.mult, None, [("sq", XR)], [("sq", AM1)])
        horner(RR, AM1, ecf[:9], "pool")
        for _ in range(2):
            tt("pool", sqv(RR), sqv(RR), sqv(RR), ALU.mult, [("sq", RR), ("sq", RR)], [("sq", RR)])
        for _ in range(4):
            tt("dve", sqv(T1), sqv(CC), sqv(CC), ALU.mult, [("sq", CC), ("sq", CC)], [("sq", T1)])
            tt("dve", sqv(T2), sqv(SS), sqv(SS), ALU.mult, [("sq", SS), ("sq", SS)], [("sq", T2)])
            tt("dve", sqv(T3), sqv(CC), sqv(SS), ALU.mult, [("sq", CC), ("sq", SS)], [("sq", T3)])
            tt("dve", sqv(CC), sqv(T1), sqv(T2), ALU.subtract, [("sq", T1), ("sq", T2)], [("sq", CC)])
            ts("dve", sqv(SS), sqv(T3), 2.0, None, ALU.mult, None, [("sq", T3)], [("sq", SS)])
        tt("dve", sqv(ARE), sqv(RR), sqv(CC), ALU.mult, [("sq", RR), ("sq", CC)], [("sq", ARE)])
        tt("dve", sqv(AIM), sqv(RR), sqv(SS), ALU.mult, [("sq", RR), ("sq", SS)], [("sq", AIM)])
        tt("dve", sqv(T1), sqv(LRE), sqv(LRE), ALU.mult, [("sq", LRE), ("sq", LRE)], [("sq", T1)])
        tt("dve", sqv(T2), sqv(LIM), sqv(LIM), ALU.mult, [("sq", LIM), ("sq", LIM)], [("sq", T2)])
        tt("dve", sqv(DEN), sqv(T1), sqv(T2), ALU.add, [("sq", T1), ("sq", T2)], [("sq", DEN)])
        recip(sqv(DEN), sqv(DEN), [("sq", DEN)], [("sq", DEN)])
        ts("dve", sqv(AM1), sqv(ARE), -1.0, None, ALU.add, None, [("sq", ARE)], [("sq", AM1)])
        tt("dve", sqv(T1), sqv(AM1), sqv(LRE), ALU.mult, [("sq", AM1), ("sq", LRE)], [("sq", T1)])
        tt("dve", sqv(T2), sqv(AIM), sqv(LIM), ALU.mult, [("sq", AIM), ("sq", LIM)], [("sq", T2)])
        tt("dve", sqv(T1), sqv(T1), sqv(T2), ALU.add, [("sq", T1), ("sq", T2)], [("sq", T1)])
        tt("dve", sqv(FRE), sqv(T1), sqv(DEN), ALU.mult, [("sq", T1), ("sq", DEN)], [("sq", FRE)])
        tt("dve", sqv(T1), sqv(AIM), sqv(LRE), ALU.mult, [("sq", AIM), ("sq", LRE)], [("sq", T1)])
        tt("dve", sqv(T2), sqv(AM1), sqv(LIM), ALU.mult, [("sq", AM1), ("sq", LIM)], [("sq", T2)])
        tt("dve", sqv(T1), sqv(T1), sqv(T2), ALU.subtract, [("sq", T1), ("sq", T2)], [("sq", T1)])
        tt("dve", sqv(FIM), sqv(T1), sqv(DEN), ALU.mult, [("sq", T1), ("sq", DEN)], [("sq", FIM)])

        P2R, P2I, P3R, P3I, P4R, P4I, R4 = 17, 18, 19, 20, 21, 22, 23

        def cmul_sq(orr, oi, xr_, xi_, yr, yi):
            tt("dve", sqv(T1), sqv(xr_), sqv(yr), ALU.mult, [("sq", xr_), ("sq", yr)], [("sq", T1)])
            tt("dve", sqv(T2), sqv(xi_), sqv(yi), ALU.mult, [("sq", xi_), ("sq", yi)], [("sq", T2)])
            tt("dve", sqv(T3), sqv(xr_), sqv(yi), ALU.mult, [("sq", xr_), ("sq", yi)], [("sq", T3)])
            tt("dve", sqv(orr), sqv(T1), sqv(T2), ALU.subtract, [("sq", T1), ("sq", T2)], [("sq", orr)])
            tt("dve", sqv(T1), sqv(xi_), sqv(yr), ALU.mult, [("sq", xi_), ("sq", yr)], [("sq", T1)])
            tt("dve", sqv(oi), sqv(T3), sqv(T1), ALU.add, [("sq", T3), ("sq", T1)], [("sq", oi)])

        cmul_sq(P2R, P2I, ARE, AIM, ARE, AIM)
        cmul_sq(P3R, P3I, P2R, P2I, ARE, AIM)
        cmul_sq(P4R, P4I, P2R, P2I, P2R, P2I)
        tt("dve", sqv(R4), sqv(RR), sqv(RR), ALU.mult, [("sq", RR), ("sq", RR)], [("sq", R4)])
        tt("dve", sqv(R4), sqv(R4), sqv(R4), ALU.mult, [("sq", R4), ("sq", R4)], [("sq", R4)])
        for _ in range(2):
            tt("dve", sqv(T1), sqv(CC), sqv(CC), ALU.mult, [("sq", CC), ("sq", CC)], [("sq", T1)])
            tt("dve", sqv(T2), sqv(SS), sqv(SS), ALU.mult, [("sq", SS), ("sq", SS)], [("sq", T2)])
            tt("dve", sqv(T3), sqv(CC), sqv(SS), ALU.mult, [("sq", CC), ("sq", SS)], [("sq", T3)])
            tt("dve", sqv(CC), sqv(T1), sqv(T2), ALU.subtract, [("sq", T1), ("sq", T2)], [("sq", CC)])
            ts("dve", sqv(SS), sqv(T3), 2.0, None, ALU.mult, None, [("sq", T3)], [("sq", SS)])
        PW = {1: (ARE, AIM), 2: (P2R, P2I), 3: (P3R, P3I), 4: (P4R, P4I)}

        RAW = [WA("RAW0", 0, 1024), WA("RAW1", 1024, 1024)]
        BB = [WA("BB0", 2048, 1024), WA("BB1", 3072, 1024)]
        PR = [WA("PR0", 4096, 1024), WA("PR1", 5120, 1024)]
        TM = [WA("TM0", 6144, 1024), WA("TM1", 7168, 1024)]
        CT = [WA("CT0", 8192, 1024), WA("CT1", 9216, 1024)]
        v4 = lambda ap: ap.rearrange("p (c l j) -> p c l j", c=8, l=4)
        v3c = lambda ap: ap.rearrange("p (c j) -> p c j", c=8)
        mvw = lambda col: sq[:, :, col].rearrange("p (c l) -> p c l", l=4).unsqueeze(3).to_broadcast([128, 8, 4, 32])

        def cmul_pad(out, outk, x, xk, mre, mim, neg_im=False):
            tt("dve", v4(TM[0]), v4(x[0]), mvw(mre), ALU.mult, [xk[0]] + K, ["TM0"])
            tt("dve", v4(TM[1]), v4(x[1]), mvw(mim), ALU.mult, [xk[1]] + K, ["TM1"])
            tt("dve", out[0], TM[0], TM[1], ALU.subtract, ["TM0", "TM1"], [outk[0]])
            tt("dve", v4(TM[0]), v4(x[0]), mvw(mim), ALU.mult, [xk[0]] + K, ["TM0"])
            tt("dve", v4(TM[1]), v4(x[1]), mvw(mre), ALU.mult, [xk[1]] + K, ["TM1"])
            if neg_im:
                stt("dve", out[1], TM[0], -1.0, TM[1], ALU.mult, ALU.subtract, ["TM0", "TM1"], [outk[1]])
            else:
                tt("dve", out[1], TM[0], TM[1], ALU.add, ["TM0", "TM1"], [outk[1]])

        for part, src in enumerate((b_re, b_im)):
            ms("pool", RAW[part], 0.0, ["RAW%d" % part])
            for gl in range(8):
                hf = gl % 2
                dma("sp", v3c(RAW[part])[64 * hf:64 * hf + 64, :, 16 * gl:16 * gl + 16], src.rearrange("(c r) p k -> r p c k", r=8)[gl],
                    [], ["RAW%d" % part], "RAW%d" % part, join=True, slow=True)
        cmul_pad(BB, ["BB0", "BB1"], RAW, ["RAW0", "RAW1"], FRE, FIM)
        for i in range(4):
            kpow = 3 - i
            if kpow == 0:
                srcp, srck = BB, ["BB0", "BB1"]
            else:
                cmul_pad(PR, ["PR0", "PR1"], BB, ["BB0", "BB1"], PW[kpow][0], PW[kpow][1])
                srcp, srck = PR, ["PR0", "PR1"]
            for c in range(8):
                b = bank()
                for part in range(2):
                    tr(PS[b][:, part * 128:(part + 1) * 128], v3c(srcp[part])[:, c, :], identf[:], [srck[part], "identf"], [("ps", b)])
                cp("act", WBt[:, c, :, i, :], PS[b][:, 0:256].rearrange("p (a j) -> p a j", a=2), [("ps", b)], ["WB"])
        for part, src in enumerate((c_re, c_im)):
            ms("pool", RAW[part], 0.0, ["RAW%d" % part])
            for gl in range(8):
                hf = gl % 2
                dma("sp", v3c(RAW[part])[16 * gl:16 * gl + 16, :, 64 * hf:64 * hf + 64], src.rearrange("(c r) k p -> r k c p", r=8)[gl],
                    [], ["RAW%d" % part], "RAW%d" % part, join=True)
        for c in range(8):
            b = bank()
            for part in range(2):
                tr(PS[b][:, part * 128:(part + 1) * 128], v3c(RAW[part])[:, c, :], identf[:], ["RAW%d" % part, "identf"], [("ps", b)])
            for part in range(2):
                cp("act", v3c(CT[part])[:, c, :], PS[b][:, part * 128:(part + 1) * 128], [("ps", b)], ["CT%d" % part])
        KF = WA("KF", 10240, 128)
        for kpow in range(5):
            if kpow == 0:
                cp("dve", PR[0], CT[0], ["CT0"], ["PR0"])
                ts("dve", PR[1], CT[1], -1.0, None, ALU.mult, None, ["CT1"], ["PR1"])
            else:
                cmul_pad(PR, ["PR0", "PR1"], CT, ["CT0", "CT1"], PW[kpow][0], PW[kpow][1], neg_im=True)
            if kpow >= 1:
                for part in range(2):
                    cp("act", WCt[:, :, part, kpow - 1, :], v3c(PR[part]), ["PR%d" % part], ["WC"])
            if kpow <= 3:
                for c in range(8):
                    b = bank()
                    mm(PS[b][:, 0:128], v3c(BB[0])[:, c, :], v3c(PR[0])[:, c, :], True, False, ["BB0", "PR0"], [("ps", b)])
                    mm(PS[b][:, 0:128], v3c(BB[1])[:, c, :], v3c(PR[1])[:, c, :], False, True, ["BB1", "PR1"], [("ps", b)])
                    if kpow == 0:
                        tt("dve", KF, PS[b][:, 0:128], bmask_t[:], ALU.mult, [("ps", b), "bmask"], ["KF"])
                        stt("dve", KTt[:, c, 0, :], identf[:], dvec[:, c:c + 1], KF, ALU.mult, ALU.add, ["KF", "identf", "dvec"], ["KT"])
                    else:
                        tt("dve", KTt[:, c, kpow, :], PS[b][:, 0:128], bmask_t[:], ALU.mult, [("ps", b), "bmask"], ["KT"])

        cp("dve", cosT[:, :, 0], sqv(CC), K, ["cosT"])
        cp("dve", sinT[:, :, 0], sqv(SS), K, ["sinT"])
        tmpA = WA("tmpA", 0, 2048).rearrange("p (q j) -> p q j", q=32)
        tmpB = WA("tmpB", 2048, 2048).rearrange("p (q j) -> p q j", q=32)
        m = 1
        while m < 128:
            cm = cosT[:, :, m - 1:m].to_broadcast([128, 32, m])
            sm = sinT[:, :, m - 1:m].to_broadcast([128, 32, m])
            tt("dve", tmpA[:, :, 0:m], cosT[:, :, 0:m], cm, ALU.mult, TAB, ["tmpA"])
            tt("dve", tmpB[:, :, 0:m], sinT[:, :, 0:m], sm, ALU.mult, TAB, ["tmpB"])
            tt("dve", cosT[:, :, m:2 * m], tmpA[:, :, 0:m], tmpB[:, :, 0:m], ALU.subtract, ["tmpA", "tmpB"], ["cosT"])
            tt("dve", tmpA[:, :, 0:m], sinT[:, :, 0:m], cm, ALU.mult, TAB, ["tmpA"])
            tt("dve", tmpB[:, :, 0:m], cosT[:, :, 0:m], sm, ALU.mult, TAB, ["tmpB"])
            tt("dve", sinT[:, :, m:2 * m], tmpA[:, :, 0:m], tmpB[:, :, 0:m], ALU.add, ["tmpA", "tmpB"], ["sinT"])
            m *= 2
        ms("dve", Sst[:], 0.0, [("S", q) for q in range(32)])
        dump("cosT", XC[:, 0:4096], ["cosT"]); dump("sinT", XC[:, 4096:8192], ["sinT"])
        dump("WB", XC[:, 8192:12288], ["WB"]); dump("WC", XC[:, 12288:16384], ["WC"]); dump("KT", XC[:, 16384:18432], ["KT"])
        dump("sq", sq[:], ["sq"]); dump("H0", H0[:], ["H0"])
        ckpt("setup")

        RINGS = {"wr": (6400, 1024, 2), "wo": (8448, 2048, 2), "wkv": (0, 2048, 2)}
        rctr = {"wr": 0, "wo": 0, "wkv": 0}

        def load_w(src_ap, nk, ncols, tag="wr"):
            base, slot, nbuf = RINGS[tag]
            i = rctr[tag] % nbuf
            rctr[tag] += 1
            sz = nk * ncols // 2
            assert sz <= slot
            name = "%s%d" % (tag, i)
            v = WA(name, base + i * slot, sz, bf=True).rearrange("p (k c) -> p k c", k=nk)
            dma("pool", v, src_ap.rearrange("(k p) c -> p k c", p=128), [], [name], name)
            return v, name

        def fm_matmul(wv, wkey, nk, rhs_fn, rhs_keys_fn, c0, ncols):
            b = bank()
            for k in range(nk):
                mm(PS[b][:, 0:ncols], wv[:, k, :], rhs_fn(k)[:, c0:c0 + ncols], k == 0, k == nk - 1, [wkey] + rhs_keys_fn(k), [("ps", b)])
            return b

        Akeys = lambda k: [("A", k)]

        TSd = {}
        for n_i, nm in enumerate(("t1", "t2", "t3", "t4", "t5", "t6", "t7", "t8", "zbr", "zbi", "zr", "zi")):
            TSd[nm] = (WA(nm, 512 * n_i, 512), nm)
        SbL = [WA("Sb0", 8448, 528, bf=True).rearrange("p (l a m) -> p l a m", l=4, a=2),
               WA("Sb1", 9088, 528, bf=True).rearrange("p (l a m) -> p l a m", l=4, a=2)]
        UBL = [WA("UB0", 9728, 544, bf=True), WA("UB1", 10368, 544, bf=True)]
        YT = WA("YT", 11008, 512)
        CM = PSALL[:, 0:2048].rearrange("p (l x) -> p l x", l=4)
        CMK = [("ps", l) for l in range(4)]
        CSR = WA("CSR", 11520, 512); CSI = WA("CSI", 12032, 512)

        def seg_tasks(segs):
            out = []
            n = 0
            for c in range(8):
                for (c0, ncols, sample) in segs:
                    out.append(dict(c=c, c0=c0, ncols=ncols, sample=sample, par=n % 2, first=(c0 == segs[0][0])))
                    n += 1
            return out

        def segA(t):
            c, c0, ncols = t["c"], t["c0"], t["ncols"]
            nb = ncols // 4
            UB = UBL[c % 2]; UBk = "UB%d" % (c % 2)
            for ql in range(4):
                for part in range(2):
                    for i in range(4):
                        mm(PS[ql][:, part * 128:part * 128 + nb], WBt[32 * ql:32 * ql + 32, c, part, i, :], UB[32 * ql:32 * ql + 32, c0 + i:c0 + ncols:4],
                           i == 0, i == 3, ["WB", UBk], [("ps", ql)], tp=(32 * ql, 0))

        def segB1(t):
            nb = t["ncols"] // 4
            v3 = lambda ap: ap[:, 0:4 * nb].rearrange("p (l m) -> p l m", l=4)
            cp("act", v3(CSR), CM[:, :, 0:nb], CMK, ["CSR"])
            cp("act", v3(CSI), CM[:, :, 128:128 + nb], CMK, ["CSI"])

        def segB(t, need_y):
            c, c0, ncols, sample, par = t["c"], t["c0"], t["ncols"], t["sample"], t["par"]
            nb = ncols // 4
            T = TSd
            (t1, t1k), (t2, t2k), (t3, t3k), (t4, t4k) = T["t1"], T["t2"], T["t3"], T["t4"]
            (t5, t5k), (t6, t6k), (t7, t7k), (t8, t8k) = T["t5"], T["t6"], T["t7"], T["t8"]
            (zbr, zbrk), (zbi, zbik), (zr, zrk), (zi, zik) = T["zbr"], T["zbi"], T["zr"], T["zi"]
            Sb = SbL[par]; Sbk = "Sb%d" % par
            v3 = lambda ap: ap[:, 0:4 * nb].rearrange("p (l m) -> p l m", l=4)
            pr = CM[:, :, 0:nb]; pi = CM[:, :, 128:128 + nb]
            if sample:
                cbt = cosT[:, 4 * c:4 * c + 4, 0:1].to_broadcast([128, 4, nb]); sbt = sinT[:, 4 * c:4 * c + 4, 0:1].to_broadcast([128, 4, nb])
            else:
                cbt = cosT[:, 4 * c:4 * c + 4, :]; sbt = sinT[:, 4 * c:4 * c + 4, :]
            SK = [("S", q) for q in range(4 * c, 4 * c + 4)]
            tt("dve", v3(t1), v3(CSR), cbt, ALU.mult, ["CSR", "cosT"], [t1k])
            tt("pool", v3(t3), v3(CSI), cbt, ALU.mult, ["CSI", "cosT"], [t3k])
            tt("dve", v3(t2), v3(CSI), sbt, ALU.mult, ["CSI", "sinT"], [t2k])
            tt("pool", v3(t4), v3(CSR), sbt, ALU.mult, ["CSR", "sinT"], [t4k])
            tt("dve", v3(zbr), v3(t1), v3(t2), ALU.add, [t1k, t2k], [zbrk])
            tt("pool", v3(zbi), v3(t3), v3(t4), ALU.subtract, [t3k, t4k], [zbik])
            r4b = sq[:, 4 * c:4 * c + 4, R4:R4 + 1]
            if sample:
                for part, (zb_, z_, zk, zbk) in enumerate(((zbr, zr, zrk, zbrk), (zbi, zi, zik, zbik))):
                    tx, txk = (t5, t5k) if part == 0 else (t6, t6k)
                    tt("pool", v3(tx), H0[:, part, 4 * c:4 * c + 4, :], r4b.to_broadcast([128, 4, nb]), ALU.mult, ["H0", "sq"], [txk])
                    tt("dve", v3(z_), v3(zb_), v3(tx), ALU.add, [zbk, txk], [zk])
            else:
                if need_y:
                    for part in range(2):
                        cp("act", Sb[:, :, part, 0], Sst[:, part, 4 * c:4 * c + 4], SK, [Sbk])
                for part, (zb_, z_, zk, zbk) in enumerate(((zbr, zr, zrk, zbrk), (zbi, zi, zik, zbik))):
                    for ql in range(4):
                        q = 4 * c + ql
                        rb = sq[:, q, R4:R4 + 1].to_broadcast([128, nb])
                        P.op("dve", lambda e, ql=ql, q=q, rb=rb, z_=z_, zb_=zb_, part=part: e.tensor_tensor_scan(
                            out=v3(z_)[:, ql, :], data0=rb, data1=v3(zb_)[:, ql, :], initial=Sst[:, part, q:q + 1], op0=ALU.mult, op1=ALU.add),
                            [zbk, "sq", ("S", q)], [zk])
                cl = cosT[:, 4 * c:4 * c + 4, nb - 1]; sl_ = sinT[:, 4 * c:4 * c + 4, nb - 1]
                zrl = v3(zr)[:, :, nb - 1]; zil = v3(zi)[:, :, nb - 1]
                tt("dve", tiny[:, 0:4], zrl, cl, ALU.mult, [zrk, "cosT"], ["tiny0"])
                tt("dve", tiny[:, 4:8], zil, sl_, ALU.mult, [zik, "sinT"], ["tiny1"])
                tt("dve", Sst[:, 0, 4 * c:4 * c + 4], tiny[:, 0:4], tiny[:, 4:8], ALU.subtract, ["tiny0", "tiny1"], SK)
                tt("dve", tiny[:, 0:4], zrl, sl_, ALU.mult, [zrk, "sinT"], ["tiny0"])
                tt("dve", tiny[:, 4:8], zil, cl, ALU.mult, [zik, "cosT"], ["tiny1"])
                tt("dve", Sst[:, 1, 4 * c:4 * c + 4], tiny[:, 0:4], tiny[:, 4:8], ALU.add, ["tiny0", "tiny1"], SK)
            if not need_y:
                return
            tt("dve", v3(t5), v3(zr), cbt, ALU.mult, [zrk, "cosT"], [t5k])
            tt("pool", v3(t7), v3(zr), sbt, ALU.mult, [zrk, "sinT"], [t7k])
            tt("dve", v3(t6), v3(zi), sbt, ALU.mult, [zik, "sinT"], [t6k])
            tt("pool", v3(t8), v3(zi), cbt, ALU.mult, [zik, "cosT"], [t8k])
            if sample:
                tt("dve", SF[:, 0, 4 * c:4 * c + 4, :], v3(t5), v3(t6), ALU.subtract, [t5k, t6k], ["SF"])
                tt("pool", SF[:, 1, 4 * c:4 * c + 4, :], v3(t7), v3(t8), ALU.add, [t7k, t8k], ["SF"])
            else:
                tt("dve", Sb[:, :, 0, 1:1 + nb], v3(t5), v3(t6), ALU.subtract, [t5k, t6k], [Sbk])
                tt("pool", Sb[:, :, 1, 1:1 + nb], v3(t7), v3(t8), ALU.add, [t7k, t8k], [Sbk])

        def segC(t):
            c, c0, ncols, sample, par = t["c"], t["c0"], t["ncols"], t["sample"], t["par"]
            nb = ncols // 4
            UB = UBL[c % 2]; UBk = "UB%d" % (c % 2)
            Sb = SbL[par]; Sbk = "Sb%d" % par
            b = bank()
            for i in range(4):
                oc_ = PS[b][:, i:ncols:4]
                for tau in range(i + 1):
                    mm(oc_, KTt[:, c, tau, :], UB[:, c0 + i - tau:c0 + ncols:4], tau == 0, False, ["KT", UBk], [("ps", b)])
                for ql in range(4):
                    for part in range(2):
                        rhs = H0b[:, part, 4 * c + ql, :] if sample else Sb[:, ql, part, 0:nb]
                        mm(PS[b][32 * ql:32 * ql + 32, i:ncols:4], WCt[:, c, part, i, 32 * ql:32 * ql + 32], rhs, False, (ql == 3 and part == 1),
                           ["WC", "H0b" if sample else Sbk], [("ps", b)], tp=(0, 32 * ql))
            t["ybank"] = b

        def segCact(t):
            c, c0, ncols, b = t["c"], t["c0"], t["ncols"], t["ybank"]
            act(B[:, 8 + c, c0:c0 + ncols], PS[b][:, 0:ncols], AF.Gelu_apprx_tanh, [("ps", b)], [("B", 8 + c)])

        def ssm_pass(segs, need_y):
            bpool[0] = [4, 5, 6, 7]
            tasks = seg_tasks(segs)
            wl = {}

            def U(c):
                wv, wkey = wl.pop(c)
                if c + 1 < 8:
                    wl[c + 1] = load_w(w_in[:, 1024 + 128 * (c + 1):1024 + 128 * (c + 2)], 16, 128)
                for (c0, ncols, _) in segs:
                    b = fm_matmul(wv, wkey, 16, lambda k: A[:, k, :], Akeys, c0, ncols)
                    cp("act", UBL[c % 2][:, c0:c0 + ncols], PS[b][:, 0:ncols], [("ps", b)], ["UB%d" % (c % 2)])

            wl[0] = load_w(w_in[:, 1024:1024 + 128], 16, 128)
            U(0)
            segA(tasks[0])
            segB1(tasks[0])
            for j, t in enumerate(tasks):
                if t["first"] and t["c"] + 1 < 8:
                    U(t["c"] + 1)
                if j + 1 < len(tasks):
                    segA(tasks[j + 1])
                segB(t, need_y)
                if need_y:
                    segC(t)
                if j + 1 < len(tasks):
                    segB1(tasks[j + 1])
                if need_y:
                    segCact(t)
            bpool[0] = list(range(8))

        ssm_pass([(0, 512, False), (512, 512, False)], need_y=False)
        for c, (wv, wkey) in pf(range(8), lambda c: load_w(w_in[:, 128 * c:128 * (c + 1)], 16, 128)):
            b = fm_matmul(wv, wkey, 16, lambda k: A[:, k, :], Akeys, 896, 128)
            cp("act", HISTP[:, c, :], PS[b][:, 113:128], [("ps", b)], ["HISTP"])
        dump("Sst_prefix", Sst[:], [("S", q) for q in range(32)]); dump("HISTP", HISTP[:], ["HISTP"])
        ckpt("prefix")

        for t_i, (r0, R) in enumerate(TT):
            src = xp[r0:r0 + R, :] if t_i < 8 else xs[:, :]
            norm_tile(src, R, r0, 0, ring_i=t_i % 2)
        dump("HT", A[:], [("A", k) for k in range(16)])
        ssm_pass([(0, 512, False), (512, 512, False), (1024, 64, True)], need_y=True)

        SO = WA("SO", 0, 512)
        so2 = [WA("so2_0", 512, 512), WA("so2_1", 1024, 512)]
        for part, (dstp, dsts) in enumerate([(srp, srs), (sip, sis)]):
            b = bank()
            tr(PS[b][0:32, 0:128], Sst[:, part, :], identf[:], [("S", q) for q in range(32)] + ["identf"], [("ps", b)])
            cp("act", SO[0:32, 0:128], PS[b][0:32, 0:128], [("ps", b)], ["SO"])
            dma("sp", dstp, SO[0:32, 0:128], ["SO"], [], "o_SO")
            for qb in range(8):
                b = bank()
                for j in range(4):
                    tr(PS[b][0:16, j * 128:(j + 1) * 128], SF[:, part, 4 * qb + j, :], identf[:], ["SF", "identf"], [("ps", b)])
                i = qb % 2
                cp("act", so2[i][0:16, :], PS[b][0:16, :], [("ps", b)], ["so2_%d" % i])
                dma("sp", dsts[:, qb * 512:(qb + 1) * 512], so2[i][0:16, :], ["so2_%d" % i], [], "o_so2_%d" % i)
        dump("G", B[:, 8:16, :], [("B", k) for k in range(8, 16)]); dump("Sst_main", Sst[:], [("S", q) for q in range(32)]); dump("SF", SF[:], ["SF"])
        ckpt("ssm")

        Z = WA("Z", 0, 1040)
        Zs = WA("Zs", 1040, 304).rearrange("p (s t) -> p s t", t=19)
        PT1 = WA("PT1", 1344, 1040); PT2 = WA("PT2", 2384, 1040)
        PTs1 = WA("PTs1", 3424, 304).rearrange("p (s t) -> p s t", t=19)
        PTs2 = WA("PTs2", 3728, 304).rearrange("p (s t) -> p s t", t=19)
        ZC = WA("ZC", 4096, 240)
        PB = WA("PB", 8448, 2048).rearrange("p (h c) -> p h c", h=2)
        PSO = [WA("PSO0", 11520, 128), WA("PSO1", 11648, 128)]
        PBO = [WA("PBO0", 12032, 128), WA("PBO1", 12160, 128)]
        for h in range(2):
            dma("sp", PB[0:120, h, :], pbuf[h * 120:(h + 1) * 120, :], [], ["PB"], "PB", join=True)
        octr = [0]
        for c, (wv, wkey) in pf(range(8), lambda c: load_w(w_in[:, 128 * c:128 * (c + 1)], 16, 128)):
            wg = c // 2
            w = 2 << wg
            for (c0, ncols) in NTL:
                b = fm_matmul(wv, wkey, 16, lambda k: A[:, k, :], Akeys, c0, ncols)
                if c0 < 1024:
                    cp("act", Z[:, 15 + c0:15 + c0 + ncols], PS[b][:, 0:ncols], [("ps", b)], ["Z"])
                else:
                    cp("act", Zs[:, :, 15:19], PS[b][:, 0:64].rearrange("p (s t) -> p s t", t=4), [("ps", b)], ["Zs"])
            cp("dve", Z[:, 0:15], HISTP[:, c, :], ["HISTP"], ["Z"])
            for h in range(2):
                b = bank()
                tr(PS[b][:, 0:120], PB[0:120, h, 128 * c:128 * (c + 1)], identf[0:120, 0:120], ["PB", "identf"], [("ps", b)])
                cp("dve", Zs[:, 8 * h:8 * h + 8, 0:15], PS[b][:, 0:120].rearrange("p (s t) -> p s t", t=15), [("ps", b)], ["Zs"])
            cur, curk, curs, cursk = Z, "Z", Zs, "Zs"
            bufs = [(PT1, "PT1", PTs1, "PTs1"), (PT2, "PT2", PTs2, "PTs2")]
            for i in range(wg + 1):
                sh = 1 << i
                lo = 2 * sh - 1
                nb, nbk, nbs, nbsk = bufs[i % 2]
                tt("pool", nb[:, lo:1039], cur[:, lo:1039], cur[:, lo - sh:1039 - sh], ALU.add, [curk], [nbk])
                tt("pool", nbs[:, :, lo:19], curs[:, :, lo:19], curs[:, :, lo - sh:19 - sh], ALU.add, [cursk], [nbsk])
                cur, curk, curs, cursk = nb, nbk, nbs, nbsk
            stt("dve", B[:, c, 0:1024], cur[:, 15:1039], 1.0 / w, Z[:, 15:1039], ALU.mult, ALU.subtract, [curk, "Z"], [("B", c)])
            tt("dve", YT[:, 0:15], cur[:, 15:30], invc_t[:, wg, :], ALU.mult, [curk, "invc"], ["YT"])
            tt("dve", B[:, c, 0:15], YT[:, 0:15], Z[:, 15:30], ALU.subtract, ["YT", "Z"], [("B", c)])
            stt("dve", B[:, c, 1024:1088].rearrange("p (s t) -> p s t", t=4), curs[:, :, 15:19], 1.0 / w, Zs[:, :, 15:19], ALU.mult, ALU.subtract, [cursk, "Zs"], [("B", c)])
            i = octr[0] % 2
            octr[0] += 1
            b = bank()
            tr(PS[b][0:15, 0:128], Z[:, 1024:1039], identf[:], ["Z", "identf"], [("ps", b)])
            cp("act", PBO[i][0:15, :], PS[b][0:15, 0:128], [("ps", b)], ["PBO%d" % i])
            dma("sp", pbp[:, 128 * c:128 * (c + 1)], PBO[i][0:15, :], ["PBO%d" % i], [], "o_PBO%d" % i)
            cp("pool", ZC.rearrange("p (s t) -> p s t", t=15), Zs[:, :, 4:19], ["Zs"], ["ZC"])
            for h in range(2):
                b = bank()
                tr(PS[b][0:120, 0:128], ZC[:, 120 * h:120 * (h + 1)], identf[:], ["ZC", "identf"], [("ps", b)])
                cp("act", PSO[h][0:120, :], PS[b][0:120, 0:128], [("ps", b)], ["PSO%d" % h])
                dma("sp", pbs[h * 120:(h + 1) * 120, 128 * c:128 * (c + 1)], PSO[h][0:120, :], ["PSO%d" % h], [], "o_PSO%d" % h)
        dump("POOLED", B[:, 0:8, :], [("B", k) for k in range(8)])
        ckpt("pool")

        for wg, (wv, wkey) in pf(range(4), lambda wg: load_w(w_pool[wg], 2, 256)):
            for (c0, ncols) in NTL:
                bs = []
                for oc in range(2):
                    b = bank()
                    for k in range(2):
                        mm(PS[b][:, 0:ncols], wv[:, k, 128 * oc:128 * (oc + 1)], B[:, 2 * wg + k, c0:c0 + ncols], k == 0, k == 1, [wkey, ("B", 2 * wg + k)], [("ps", b)])
                    bs.append(b)
                for oc in range(2):
                    act(B[:, 2 * wg + oc, c0:c0 + ncols], PS[bs[oc]][:, 0:ncols], AF.Copy, [("ps", bs[oc]), "pscale"], [("B", 2 * wg + oc)], scale=pscale[:, 2 * wg + oc:2 * wg + oc + 1])

        for oc, (wv, wkey) in pf(range(8), lambda oc: load_w(w_glu[:, 128 * oc:128 * (oc + 1)], 8, 128)):
            for (c0, ncols) in NTL:
                b = fm_matmul(wv, wkey, 8, lambda k: B[:, 8 + k, :], lambda k: [("B", 8 + k)], c0, ncols)
                act(YT[:, 0:ncols], PS[b][:, 0:ncols], AF.Sigmoid, [("ps", b), "bglu"], ["YT"], bias=bglu[:, oc:oc + 1])
                tt("dve", A[:, 8 + oc, c0:c0 + ncols], B[:, 8 + oc, c0:c0 + ncols], YT[:, 0:ncols], ALU.mult, ["YT", ("B", 8 + oc)], [("A", 8 + oc)])
        dump("MIXP", B[:, 0:8, :], [("B", k) for k in range(8)]); dump("MIXS", A[:, 8:16, :], [("A", k) for k in range(8, 16)])
        ckpt("glu")

        def proj_resid(wsrc, lhs_fn, lhs_keys_fn, nk, first):
            CB = 256
            if first:
                for ti, (r0, R) in enumerate(TT):
                    src = xp[r0:r0 + R, :] if ti < 8 else xs[:, :]
                    dma("sp", Xt(ti)[0:R, :], src, [], [("X", ti)], ("X", ti))
            for cb, (wv, wkey) in pf(range(D // CB), lambda cb: load_w(wsrc[:, cb * CB:(cb + 1) * CB], nk, CB, tag="wo")):
                for ti, (r0, R) in enumerate(TT):
                    b = bank()
                    for k in range(nk):
                        mm(PS[b][0:R, 0:CB], lhs_fn(k)[:, r0:r0 + R], wv[:, k, :], k == 0, k == nk - 1, [wkey] + lhs_keys_fn(k), [("ps", b)])
                    xs_ap = Xt(ti)[0:R, cb * CB:(cb + 1) * CB]
                    tt("dve", xs_ap, xs_ap, PS[b][0:R, 0:CB], ALU.add, [("X", ti), ("ps", b)], [("X", ti)])

        mixfn = lambda k: (B[:, k, :] if k < 8 else A[:, k, :])
        mixkeys = lambda k: ([("B", k)] if k < 8 else [("A", k)])
        proj_resid(w_out, mixfn, mixkeys, 16, True)
        dump("X1", XC[:], [("X", t) for t in range(9)])
        ckpt("wout")

        for ti, (r0, R) in enumerate(TT):
            norm_tile(None, R, r0, 1, xin=(Xt(ti)[0:R, :], ("X", ti)))
        for oc, (wv, wkey) in pf(range(16), lambda oc: load_w(w_q[:, 128 * oc:128 * (oc + 1)], 16, 128)):
            for (c0, ncols) in NTL:
                b = fm_matmul(wv, wkey, 16, lambda k: A[:, k, :], Akeys, c0, ncols)
                cp("act", B[:, oc, c0:c0 + ncols], PS[b][:, 0:ncols], [("ps", b)], [("B", oc)])
        ckpt("attn_q")
        MT = WA("MT", 8448, 2048, bf=True).rearrange("p (k t) -> p k t", k=16)
        KT = WA("KT", 10496, 2048, bf=True).rearrange("p (k t) -> p k t", k=16)
        VB = WA("VB", 6400, 2048, bf=True).rearrange("p (m d) -> p m d", m=2)
        KO = [WA("KO0", 4096, 512), WA("KO1", 4608, 512)]
        EX = [WA("EX0", 5120, 256, bf=True), WA("EX1", 5376, 256, bf=True)]
        RD = WA("RD", 5632, 512)
        for mt_i in range(2):
            xkey = "XT%d" % mt_i
            dma("sp", XT[mt_i][:, :], mem[mt_i * 128:(mt_i + 1) * 128, :], [], [xkey], xkey)
            norm_from(XT[mt_i][:, :], xkey, 128, 2, lambda hb: MT[:, hb * 8:hb * 8 + 8, mt_i * 128:(mt_i + 1) * 128], lambda hb: ["MT"])
        ckpt("kv0")
        for which, (wsrc, dsto) in enumerate([(w_k, mk), (w_v, mv)]):
            for cb, (wv, wkey) in pf(range(8), lambda cb: load_w(wsrc[:, cb * 256:(cb + 1) * 256], 16, 256, tag="wkv")):
                for mt_i in range(2):
                    b = bank()
                    for k in range(16):
                        mm(PS[b][:, 0:256], MT[:, k, mt_i * 128:(mt_i + 1) * 128], wv[:, k, :], k == 0, k == 15, [wkey, "MT"], [("ps", b)])
                    kk = "KO%d" % mt_i
                    cp("act", KO[mt_i][:, 0:256], PS[b][:, 0:256], [("ps", b)], [kk])
                    if which == 1:
                        cp("dve", VB[:, mt_i, cb * 256:(cb + 1) * 256], KO[mt_i][:, 0:256], [kk], ["VB"])
                    dma("sp", dsto[mt_i * 128:(mt_i + 1) * 128, cb * 256:(cb + 1) * 256], KO[mt_i][:, 0:256], [kk], [], "o_" + kk)
                if which == 0:
                    for j in range(2):
                        b = bank()
                        for k in range(16):
                            mm(PS[b][:, 0:256], wv[:, k, 128 * j:128 * (j + 1)], MT[:, k, :], k == 0, k == 15, [wkey, "MT"], [("ps", b)])
                        cp("act", KT[:, 2 * cb + j, :], PS[b][:, 0:256], [("ps", b)], ["KT"])
        ckpt("attn_kv")
        SCL = 512.0 ** -0.5
        NKV = 4
        KS = [WA("KS%d" % i, i * 512, 512, bf=True).rearrange("p (m d) -> p m d", m=2) for i in range(NKV)]
        VS = [WA("VS%d" % i, 2048 + i * 512, 512, bf=True).rearrange("p (m d) -> p m d", m=2) for i in range(NKV)]
        KTSL = [WA("KTS0", 8448, 512, bf=True).rearrange("p (j m) -> p j m", j=4), WA("KTS1", 8960, 512, bf=True).rearrange("p (j m) -> p j m", j=4)]
        PTSL = [WA("PTS0", 9472, 4, bf=True), WA("PTS1", 9600, 4, bf=True)]
        RDSL = [WA("RDS0", 9728, 4), WA("RDS1", 9856, 4)]

        def prompt_unit(blk, h):
            c0 = blk * 512
            for mt_i in range(2):
                b = bank()
                for j in range(4):
                    mm(PS[b][:, :], KT[:, 4 * h + j, mt_i * 128:(mt_i + 1) * 128], B[:, 4 * h + j, c0:c0 + 512], j == 0, j == 3, ["KT", ("B", 4 * h + j)], [("ps", b)])
                act(EX[mt_i], PS[b][:, :], AF.Exp, [("ps", b)], ["EX%d" % mt_i], scale=SCL)
            b = bank()
            for mt_i in range(2):
                mm(PS[b][:, :], onesb[:], EX[mt_i], mt_i == 0, mt_i == 1, ["onesb", "EX%d" % mt_i], [("ps", b)])
            recip(RD, PS[b][:, :], [("ps", b)], ["RD"])
            for j in range(4):
                b = bank()
                for mt_i in range(2):
                    mm(PS[b][:, :], VB[:, mt_i, 512 * h + 128 * j:512 * h + 128 * (j + 1)], EX[mt_i], mt_i == 0, mt_i == 1, ["VB", "EX%d" % mt_i], [("ps", b)])
                tt("dve", A[:, 4 * h + j, c0:c0 + 512], PS[b][:, :], RD, ALU.mult, [("ps", b), "RD"], [("A", 4 * h + j)])

        sh_list = [(s_, h) for s_ in range(NSQ) for h in range(4)]

        def load_kv(n):
            s_, h = sh_list[n]
            i = n % NKV
            dma("pool", KS[i], ck[s_].rearrange("(m p) d -> p m d", p=128)[:, :, 512 * h:512 * (h + 1)], [], ["KS%d" % i], "KS%d" % i)
            dma("pool", VS[i], cv[s_].rearrange("(m p) d -> p m d", p=128)[:, :, 512 * h:512 * (h + 1)], [], ["VS%d" % i], "VS%d" % i)

        def sampA(n):
            s_, h = sh_list[n]
            i = n % NKV
            pp = n % 2
            b = bank()
            pb = PS[b][:].bitcast(BF16)
            for j in range(4):
                for mt_i in range(2):
                    tr(pb[:, j * 256 + mt_i * 128:j * 256 + (mt_i + 1) * 128], KS[i][:, mt_i, 128 * j:128 * (j + 1)], identb[:], ["KS%d" % i, "identb"], [("ps", b)])
            cp("act", KTSL[pp], pb[:, 0:1024].rearrange("p (j m) -> p j m", j=4), [("ps", b)], ["KTS%d" % pp])

        def sampB(n):
            s_, h = sh_list[n]
            pp = n % 2
            KTS = KTSL[pp]; PTS = PTSL[pp]
            qc = 1024 + 4 * s_
            b = bank()
            for mt_i in range(2):
                for j in range(4):
                    mm(PS[b][:, mt_i * 4:mt_i * 4 + 4], KTS[:, j, mt_i * 128:(mt_i + 1) * 128], B[:, 4 * h + j, qc:qc + 4], j == 0, j == 3, ["KTS%d" % pp, ("B", 4 * h + j)], [("ps", b)])
            act(PTS, PS[b][:, 0:8], AF.Exp, [("ps", b)], ["PTS%d" % pp], scale=SCL)

        def sampC(n):
            s_, h = sh_list[n]
            i = n % NKV
            pp = n % 2
            PTS = PTSL[pp]; RDS = RDSL[pp]
            qc = 1024 + 4 * s_
            b = bank()
            for mt_i in range(2):
                mm(PS[b][:, 0:4], onesb[:], PTS[:, mt_i * 4:mt_i * 4 + 4], mt_i == 0, mt_i == 1, ["onesb", "PTS%d" % pp], [("ps", b)])
            recip(RDS, PS[b][:, 0:4], [("ps", b)], ["RDS%d" % pp])
            b = bank()
            for j in range(4):
                for mt_i in range(2):
                    mm(PS[b][:, j * 4:j * 4 + 4], VS[i][:, mt_i, 128 * j:128 * (j + 1)], PTS[:, mt_i * 4:mt_i * 4 + 4], mt_i == 0, mt_i == 1, ["VS%d" % i, "PTS%d" % pp], [("ps", b)])
            tt("dve", A[:, 4 * h:4 * h + 4, qc:qc + 4], PS[b][:, 0:16].rearrange("p (j t) -> p j t", j=4), RDS.unsqueeze(1).to_broadcast([128, 4, 4]), ALU.mult,
               [("ps", b), "RDS%d" % pp], [("A", 4 * h + j) for j in range(4)])

        nload = [0]

        def ensure_loaded(upto):
            while nload[0] <= min(upto, len(sh_list) - 1):
                load_kv(nload[0])
                nload[0] += 1

        ensure_loaded(NKV - 1)

        def sample_group(n0, cnt):
            sampA(n0)
            for k_ in range(cnt):
                n = n0 + k_
                if k_ + 1 < cnt:
                    sampA(n + 1)
                sampB(n)
                if k_ >= 1:
                    sampC(n - 1)
                    ensure_loaded(n - 1 + NKV)
            sampC(n0 + cnt - 1)
            ensure_loaded(n0 + cnt - 1 + NKV)

        n_it = 0
        for u in range(8):
            prompt_unit(u // 4, u % 4)
            sample_group(n_it, 8)
            n_it += 8
        ckpt("attn_s")
        dump("QT", B[:], [("B", k) for k in range(16)]); dump("OT", A[:], [("A", k) for k in range(16)])
        proj_resid(w_o, lambda k: A[:, k, :], Akeys, 16, False)
        dump("X2", XC[:], [("X", t) for t in range(9)])
        ckpt("attn")

        for ti, (r0, R) in enumerate(TT):
            norm_tile(None, R, r0, 3, xin=(Xt(ti)[0:R, :], ("X", ti)))
        GU = [WA("gu%d" % i, i * 1024, 1024, bf=True).rearrange("p (k c) -> p k c", k=16) for i in range(4)]
        DN = [WA("dn%d" % i, 4096 + i * 4096, 4096, bf=True).rearrange("p (k c) -> p k c", k=4) for i in range(2)]
        SG = WA("SG", 12288, 512)
        guc = [0]

        def load_gu(ffc):
            out = []
            for src in (w_gate, w_up):
                i = guc[0] % 4
                guc[0] += 1
                dma("pool", GU[i], src[:, 128 * ffc:128 * (ffc + 1)].rearrange("(k p) c -> p k c", p=128), [], ["gu%d" % i], "gu%d" % i)
                out.append(i)
            return out

        def load_dn(grp):
            i = grp % 2
            dma("pool", DN[i], w_down[512 * grp:512 * (grp + 1), :].rearrange("(k p) c -> p k c", p=128), [], ["dn%d" % i], "dn%d" % i)
            return i

        dn_next = load_dn(0)
        for ffc, (gi_, ui_) in pf(range(44), load_gu):
            grp, fc = ffc // 4, ffc % 4
            slot = (grp % 2) * 4 + fc
            for (c0, ncols) in NTL:
                bg = fm_matmul(GU[gi_], "gu%d" % gi_, 16, lambda k: A[:, k, :], Akeys, c0, ncols)
                bu = fm_matmul(GU[ui_], "gu%d" % ui_, 16, lambda k: A[:, k, :], Akeys, c0, ncols)
                act(SG[:, 0:ncols], PS[bg][:, 0:ncols], AF.Silu, [("ps", bg)], ["SG"])
                tt("dve", B[:, slot, c0:c0 + ncols], SG[:, 0:ncols], PS[bu][:, 0:ncols], ALU.mult, ["SG", ("ps", bu)], [("B", slot)])
            if fc == 3:
                di = dn_next
                if grp + 1 < 11:
                    dn_next = load_dn(grp + 1)
                for ti, (r0, R) in enumerate(TT):
                    for cb in range(4):
                        b = bank()
                        for f2 in range(4):
                            s2 = (grp % 2) * 4 + f2
                            mm(PS[b][0:R, :], B[:, s2, r0:r0 + R], DN[di][:, f2, cb * 512:(cb + 1) * 512], f2 == 0, f2 == 3, ["dn%d" % di, ("B", s2)], [("ps", b)])
                        xs_ap = Xt(ti)[0:R, cb * 512:(cb + 1) * 512]
                        tt("dve", xs_ap, xs_ap, PS[b][0:R, :], ALU.add, [("X", ti), ("ps", b)], [("X", ti)])
        gfin = WA("gfin", 4096, 2048)
        dma("sp", gfin, g_final.partition_broadcast(128), [], ["gfin"], "gfin")
        YO = [WA("YO0", 0, 2048), WA("YO1", 2048, 2048)]
        for ti, (r0, R) in enumerate(TT):
            xk = [("X", ti)]
            xt_ap = Xt(ti)[0:R, :]
            i = ti % 2
            yk = "YO%d" % i
            o_ = 4 * i
            ka, kb, kc_ = "stat%d" % o_, "stat%d" % (o_ + 1), "stat%d" % (o_ + 2)
            act(YO[i][0:R, :], xt_ap, AF.Square, xk, [yk, ka], accum=stat[0:R, o_:o_ + 1])
            ts("dve", stat[0:R, o_ + 1:o_ + 2], stat[0:R, o_:o_ + 1], 1.0 / D, EPS, ALU.mult, ALU.add, [ka], [kb])
            act(stat[0:R, o_ + 1:o_ + 2], stat[0:R, o_ + 1:o_ + 2], AF.Sqrt, [kb], [kb])
            recip(stat[0:R, o_ + 2:o_ + 3], stat[0:R, o_ + 1:o_ + 2], [kb], [kc_])
            stt("dve", YO[i][0:R, :], xt_ap, stat[0:R, o_ + 2:o_ + 3], gfin[0:R, :], ALU.mult, ALU.mult, xk + [kc_, "gfin"], [yk])
            dst = yp[r0:r0 + R, :] if ti < 8 else ys[:, :]
            dma("sp", dst, YO[i][0:R, :], [yk], [], "o_" + yk)

    try:
        body()
    except _Stop:
        pass
    P.emit(nc, st)
    st.close()
    return nc


_NC = None


def make_in_maps(inp):
    f = lambda a: np.ascontiguousarray(np.asarray(a, dtype=np.float32))
    x_prompt = f(inp["x_prompt"]); x_sample = f(inp["x_sample"]); mem_prompt = f(inp["mem_prompt"])
    spb = f(inp["state_pool_buf"]); s_re = f(inp["state_ssm_re"]); s_im = f(inp["state_ssm_im"])
    cmk = f(inp["cache_mem_k"]); cmv = f(inp["cache_mem_v"])
    shared = {
        "g_mix": f(inp["g_mix"][0]), "w_in": f(inp["w_in"][0]), "w_pool": f(inp["w_pool"][0]), "pool_scale": f(inp["pool_scale"][0]),
        "lam_re": f(inp["ssm_lam_re"][0]), "lam_im": f(inp["ssm_lam_im"][0]), "log_step": f(inp["ssm_log_step"][0]),
        "b_re": f(inp["ssm_b_re"][0]), "b_im": f(inp["ssm_b_im"][0]), "c_re": f(inp["ssm_c_re"][0]), "c_im": f(inp["ssm_c_im"][0]),
        "ssm_d": f(inp["ssm_d"][0]), "w_glu": f(inp["w_glu"][0]), "b_glu": f(inp["b_glu"][0]), "w_out": f(inp["w_out"][0]),
        "g_cross": f(inp["g_cross"][0]), "g_mem": f(inp["g_mem"][0]), "w_q": f(inp["w_q"][0]), "w_k": f(inp["w_k"][0]), "w_v": f(inp["w_v"][0]),
        "w_o": f(inp["w_o"][0]), "g_ffn": f(inp["g_ffn"][0]), "w_gate": f(inp["w_gate"][0]), "w_up": f(inp["w_up"][0]), "w_down": f(inp["w_down"][0]),
        "g_final": f(inp["g_final"]), "ident": np.eye(128, dtype=np.float32),
        "bmask": np.kron(np.eye(8, dtype=np.float32), np.ones((16, 16), np.float32)),
    }
    in_maps = []
    for c in range(8):
        b, half = c // 2, c % 2
        m = dict(shared)
        m["xp"] = f(x_prompt[b, half * 1024:(half + 1) * 1024])
        m["xprev"] = f(x_prompt[b, 0:1024]) if half == 1 else np.zeros((1024, D), np.float32)
        m["xs"] = f(x_sample[16 * c:16 * c + 16].reshape(64, D))
        m["mem"] = f(mem_prompt[b])
        m["pbuf"] = f(spb[0, 16 * c:16 * c + 16].reshape(240, 1024))
        m["sre"] = f(s_re[0, 16 * c:16 * c + 16].reshape(16, 4096))
        m["sim"] = f(s_im[0, 16 * c:16 * c + 16].reshape(16, 4096))
        m["ck"] = f(cmk[0, 16 * c:16 * c + 16].reshape(16, 256, D))
        m["cv"] = f(cmv[0, 16 * c:16 * c + 16].reshape(16, 256, D))
        ic = np.zeros((128, 4, 15), np.float32)
        for wg, w in enumerate((2, 4, 8, 16)):
            for t in range(15):
                ic[:, wg, t] = 1.0 / min(half * 1024 + t + 1, w)
        m["invc"] = ic
        in_maps.append(m)
    return in_maps


def kernel(**inp):
    global _NC
    in_maps = make_in_maps(inp)
    if _NC is None:
        _NC = build()
    res = run_bass_kernel_spmd(_NC, in_maps, core_ids=list(range(8))).results
    y_prompt = np.stack([np.concatenate([res[2 * b]["yp"], res[2 * b + 1]["yp"]], 0) for b in range(4)])
    y_sample = np.concatenate([res[c]["ys"].reshape(16, 4, D) for c in range(8)], 0)
    pb_p = np.stack([res[2 * b + 1]["pbp"] for b in range(4)])[None]
    re_p = np.stack([res[2 * b + 1]["srp"].reshape(64, 64) for b in range(4)])[None]
    im_p = np.stack([res[2 * b + 1]["sip"].reshape(64, 64) for b in range(4)])[None]
    mk_p = np.stack([res[2 * b]["mk"].reshape(256, 4, 512) for b in range(4)])[None]
    mv_p = np.stack([res[2 * b]["mv"].reshape(256, 4, 512) for b in range(4)])[None]
    pb_s = np.concatenate([res[c]["pbs"].reshape(16, 15, 1024) for c in range(8)], 0)[None]
    re_s = np.concatenate([res[c]["srs"].reshape(16, 64, 64) for c in range(8)], 0)[None]
    im_s = np.concatenate([res[c]["sis"].reshape(16, 64, 64) for c in range(8)], 0)[None]
    return (y_prompt.astype(np.float32), y_sample.astype(np.float32), pb_p.astype(np.float32), re_p.astype(np.float32), im_p.astype(np.float32),
            mk_p.astype(np.float32), mv_p.astype(np.float32), pb_s.astype(np.float32), re_s.astype(np.float32), im_s.astype(np.float32))
```

```python
import contextlib
import numpy as np
import concourse.bass as bass
import concourse.mybir as mybir
from concourse.bass_utils import run_bass_kernel_spmd

F32 = mybir.dt.float32
BF16 = mybir.dt.bfloat16
AF = mybir.ActivationFunctionType
ALU = mybir.AluOpType

D = 2048
NPR = 1024
NSQ = 16
NS = 64
N = NPR + NS
DFF = 5632
NTL = [(0, 512), (512, 512), (1024, 64)]
TT = [(i * 128, 128) for i in range(8)] + [(1024, 64)]
EPS = 1e-6


class _Op:
    __slots__ = ("eng", "fn", "deps", "dsem", "val", "sem", "need")

    def __init__(self, eng, fn, deps, dsem):
        self.eng = eng
        self.fn = fn
        self.deps = deps
        self.dsem = dsem
        self.val = 0
        self.sem = None
        self.need = False


class Prog:
    ENGS = ("pe", "act", "dve", "pool", "sp")

    def __init__(self):
        self.ops = []
        self.base_w = {}
        self.join_w = {}
        self.readers = {}
        self.aliases = {}
        self.spacer = None

    def alias(self, name, keys):
        self.aliases[name] = list(keys)

    def _expand(self, keys):
        out = []
        for k in keys:
            a = self.aliases.get(k)
            if a is None:
                out.append(k)
            else:
                out.extend(a)
        return out

    def op(self, eng, fn, reads=(), writes=(), dsem=None, join=False):
        idx = len(self.ops)
        reads = self._expand(reads)
        writes = self._expand(writes)
        deps = {}

        def add(d, raw):
            if d is None or d == idx:
                return
            deps[d] = deps.get(d, False) or raw

        for k in reads:
            add(self.base_w.get(k), True)
            for d in self.join_w.get(k, ()):
                add(d, True)
        for k in writes:
            add(self.base_w.get(k), False)
            if not join:
                for d in self.join_w.get(k, ()):
                    add(d, False)
            for d in self.readers.get(k, ()):
                add(d, False)
        for k in reads:
            self.readers.setdefault(k, []).append(idx)
        for k in writes:
            if join:
                self.join_w.setdefault(k, []).append(idx)
            else:
                self.base_w[k] = idx
                self.join_w[k] = []
                self.readers[k] = []
        self.ops.append(_Op(eng, fn, deps, dsem))
        return idx

    def emit(self, nc, stack):
        ops = self.ops
        pos = {}
        cnt = {e: 0 for e in self.ENGS}
        for i, o in enumerate(ops):
            if o.dsem is None:
                pos[i] = cnt[o.eng]
                cnt[o.eng] += 1
        waits = []
        spacers = set()
        for i, o in enumerate(ops):
            w = set()
            best = {}
            for d, raw in o.deps.items():
                od = ops[d]
                if od.dsem is not None:
                    w.add(d)
                    continue
                if o.dsem is None and od.eng == o.eng:
                    if o.eng == "pe":
                        continue
                    if o.eng in ("dve", "act"):
                        if not raw or pos[i] - pos[d] >= 3:
                            continue
                        if o.eng == "dve" and self.spacer is not None:
                            spacers.add(i)
                            continue
                if d > best.get(od.eng, -1):
                    best[od.eng] = d
            w.update(best.values())
            waits.append(w)
            for d in w:
                ops[d].need = True
        esem = {e: stack.enter_context(nc.semaphore("s_" + e)) for e in self.ENGS}
        dsems = {}
        ecount = {e: 0 for e in self.ENGS}
        dcount = {}
        for o in ops:
            if o.dsem is not None:
                if o.dsem not in dsems:
                    dsems[o.dsem] = stack.enter_context(nc.semaphore("d_" + o.dsem))
                    dcount[o.dsem] = 0
                dcount[o.dsem] += 16
                o.sem = dsems[o.dsem]
                o.val = dcount[o.dsem]
                o.need = True
            elif o.need:
                ecount[o.eng] += 1
                o.sem = esem[o.eng]
                o.val = ecount[o.eng]
        block = stack.enter_context(nc.Block())
        final = dict(dcount)

        def run(engname, e):
            waited = {}
            for i, o in enumerate(ops):
                if o.eng != engname:
                    continue
                need = {}
                for d in waits[i]:
                    od = ops[d]
                    if need.get(od.sem, (0, None))[0] < od.val:
                        need[od.sem] = (od.val, od.sem)
                for key, (v, s_) in need.items():
                    if waited.get(key, 0) < v:
                        e.wait_ge(s_, v)
                        waited[key] = v
                if i in spacers:
                    self.spacer(e)
                ins = o.fn(e)
                if o.need:
                    ins.then_inc(o.sem, 16 if o.dsem is not None else 1)
            if engname == "sp":
                for name, v in final.items():
                    e.wait_ge(dsems[name], v)

        @block.tensor
        def _(e):
            run("pe", e)

        @block.scalar
        def _(e):
            run("act", e)

        @block.vector
        def _(e):
            run("dve", e)

        @block.gpsimd
        def _(e):
            run("pool", e)

        @block.sync
        def _(e):
            run("sp", e)


def build(stop=None, dbg=False):
    nc = bass.Bass("TRN2", target_bir_lowering=False)
    st = contextlib.ExitStack()
    P = Prog()

    def din(name, shape):
        return nc.dram_tensor(name, shape, F32, kind="ExternalInput").ap()

    def dout(name, shape):
        return nc.dram_tensor(name, shape, F32, kind="ExternalOutput").ap()

    xp = din("xp", [NPR, D]); xprev = din("xprev", [NPR, D]); xs = din("xs", [NS, D]); mem = din("mem", [256, D])
    pbuf = din("pbuf", [240, 1024]); sre = din("sre", [16, 4096]); sim = din("sim", [16, 4096])
    ck = din("ck", [16, 256, D]); cv = din("cv", [16, 256, D])
    invc = din("invc", [128, 4, 15]); ident = din("ident", [128, 128]); bmask = din("bmask", [128, 128])
    g_mix = din("g_mix", [D]); w_in = din("w_in", [D, D]); w_pool = din("w_pool", [4, 256, 256]); pool_scale = din("pool_scale", [1024])
    lam_re = din("lam_re", [64, 64]); lam_im = din("lam_im", [64, 64]); log_step = din("log_step", [64])
    b_re = din("b_re", [64, 64, 16]); b_im = din("b_im", [64, 64, 16]); c_re = din("c_re", [64, 16, 64]); c_im = din("c_im", [64, 16, 64])
    ssm_d = din("ssm_d", [1024]); w_glu = din("w_glu", [1024, 1024]); b_glu = din("b_glu", [1024]); w_out = din("w_out", [D, D])
    g_cross = din("g_cross", [D]); g_mem = din("g_mem", [D]); w_q = din("w_q", [D, D]); w_k = din("w_k", [D, D]); w_v = din("w_v", [D, D])
    w_o = din("w_o", [D, D]); g_ffn = din("g_ffn", [D]); w_gate = din("w_gate", [D, DFF]); w_up = din("w_up", [D, DFF]); w_down = din("w_down", [DFF, D])
    g_final = din("g_final", [D])
    yp = dout("yp", [NPR, D]); ys = dout("ys", [NS, D]); pbp = dout("pbp", [15, 1024]); srp = dout("srp", [32, 128]); sip = dout("sip", [32, 128])
    mk = dout("mk", [256, D]); mv = dout("mv", [256, D]); pbs = dout("pbs", [240, 1024]); srs = dout("srs", [16, 4096]); sis = dout("sis", [16, 4096])

    def sb(name, shape, dt=F32):
        return st.enter_context(nc.sbuf_tensor(name, shape, dt))

    XC = sb("XC", [128, 9 * D])
    A = sb("A", [128, 16, N], BF16)
    B = sb("B", [128, 16, N], BF16)
    WSZ = 12800
    W = sb("W", [128, WSZ])
    identf = sb("identf", [128, 128]); identb = sb("identb", [128, 128], BF16); onesb = sb("onesb", [128, 128], BF16)
    gv = sb("gv", [128, 4, 16])
    pv = sb("pv", [128, 24]); pscale = pv[:, 0:8]; dvec = pv[:, 8:16]; bglu = pv[:, 16:24]
    invc_t = sb("invc_t", [128, 4, 15])
    sq = sb("sq", [128, 32, 24])
    H0b = sb("H0b", [128, 2, 32, 16], BF16)

    Sst = sb("Sst", [128, 2, 32])
    H0 = sb("H0", [128, 2, 32, 16]); SF = sb("SF", [128, 2, 32, 16])
    HISTP = sb("HISTP", [128, 8, 15])
    stat = sb("stat", [128, 8]); tiny = sb("tiny", [128, 8]); spc = sb("spc", [128, 2])
    P.spacer = None
    PSALL = st.enter_context(nc.psum_tensor("psall", [128, 8 * 512], F32))
    PS = [PSALL[:, i * 512:(i + 1) * 512] for i in range(8)]
    pctr = [0]
    bpool = [list(range(8))]

    def bank():
        b = bpool[0][pctr[0] % len(bpool[0])]
        pctr[0] += 1
        return b

    WBLK = 128

    def WA(name, off, n, bf=False):
        assert off + n <= WSZ, (name, off, n)
        P.alias(name, [("W", b) for b in range(off // WBLK, (off + n - 1) // WBLK + 1)])
        v = W[:, off:off + n]
        return v.bitcast(BF16) if bf else v

    cosT = XC[:, 0:4096].rearrange("p (q j) -> p q j", q=32)
    sinT = XC[:, 4096:8192].rearrange("p (q j) -> p q j", q=32)
    WBt = XC[:, 8192:12288].bitcast(BF16).rearrange("p (c a i j) -> p c a i j", c=8, a=2, i=4)
    WCt = XC[:, 12288:16384].bitcast(BF16).rearrange("p (c a i j) -> p c a i j", c=8, a=2, i=4)
    KTt = XC[:, 16384:18432].bitcast(BF16).rearrange("p (c i j) -> p c i j", c=8, i=4)
    P.alias("cosT", [("XC", 0), ("XC", 1)]); P.alias("sinT", [("XC", 2), ("XC", 3)])
    P.alias("WB", [("XC", 4), ("XC", 5)]); P.alias("WC", [("XC", 6), ("XC", 7)]); P.alias("KT", [("XC", 8)])
    for ti in range(9):
        P.alias(("X", ti), [("XC", ti)])
    TAB = ["cosT", "sinT"]

    def Xt(ti):
        return XC[:, ti * D:(ti + 1) * D]

    def dma(eng, out, in_, reads, writes, buf, join=False, slow=False):
        sem = buf if isinstance(buf, str) else "_".join(str(x) for x in buf)
        if slow:
            return P.op(eng, lambda e: e.dma_start(out=out, in_=in_, allow_slow_non_contiguous=True), reads, writes, dsem=sem, join=join)
        return P.op(eng, lambda e: e.dma_start(out=out, in_=in_), reads, writes, dsem=sem, join=join)

    def mm(out, lhsT, rhs, start, stop, reads, writes, tp=None):
        if tp is None:
            P.op("pe", lambda e: e.matmul(out, lhsT=lhsT, rhs=rhs, start=start, stop=stop), reads, writes)
        else:
            P.op("pe", lambda e: e.matmul(out, lhsT=lhsT, rhs=rhs, start=start, stop=stop, tile_position=tp), reads, writes)

    def tr(out, in_, idn, reads, writes):
        P.op("pe", lambda e: e.transpose(out=out, in_=in_, identity=idn), reads, writes)

    def tt(eng, out, in0, in1, op, reads, writes):
        P.op(eng, lambda e: e.tensor_tensor(out=out, in0=in0, in1=in1, op=op), reads, writes)

    def ts(eng, out, in0, s1, s2, op0, op1, reads, writes):
        if s2 is None:
            P.op(eng, lambda e: e.tensor_scalar(out=out, in0=in0, scalar1=s1, scalar2=None, op0=op0), reads, writes)
        else:
            P.op(eng, lambda e: e.tensor_scalar(out=out, in0=in0, scalar1=s1, scalar2=s2, op0=op0, op1=op1), reads, writes)

    def stt(eng, out, in0, scalar, in1, op0, op1, reads, writes):
        P.op(eng, lambda e: e.scalar_tensor_tensor(out=out, in0=in0, scalar=scalar, in1=in1, op0=op0, op1=op1), reads, writes)

    def act(out, in_, func, reads, writes, scale=None, bias=None, accum=None):
        kw = {}
        if scale is not None:
            kw["scale"] = scale
        if bias is not None:
            kw["bias"] = bias
        if accum is not None:
            kw["accum_out"] = accum
        P.op("act", lambda e: e.activation(out=out, in_=in_, func=func, **kw), reads, writes)

    def cp(eng, out, in_, reads, writes):
        if eng == "act":
            P.op("act", lambda e: e.copy(out=out, in_=in_), reads, writes)
        else:
            P.op(eng, lambda e: e.tensor_copy(out=out, in_=in_), reads, writes)

    def ms(eng, ap, val, writes):
        P.op(eng, lambda e: e.memset(ap, val), (), writes)

    def recip(out, in_, reads, writes):
        P.op("dve", lambda e: e.reciprocal(out=out, in_=in_), reads, writes)

    def pf(items, loader):
        items = list(items)
        nxt = loader(items[0])
        for i, it in enumerate(items):
            cur = nxt
            if i + 1 < len(items):
                nxt = loader(items[i + 1])
            yield it, cur

    class _Stop(Exception):
        pass

    def ckpt(name):
        if stop == name:
            raise _Stop()

    def dump(name, ap, keys):
        if not dbg:
            return
        t = nc.dram_tensor("dbg_" + name, list(ap.shape), ap.dtype, kind="ExternalOutput").ap()
        P.op("sp", lambda e: e.dma_start(out=t, in_=ap), keys, [], dsem="dbg_" + name)

    def body():
        dma("sp", identf[:], ident, [], ["identf"], "identf")
        bmask_t = WA("bmask", 10368, 128)
        dma("sp", bmask_t[:], bmask, [], ["bmask"], "bmask")
        dma("pool", identb[:], ident, [], ["identb"], "identb")
        ms("dve", onesb[:], 1.0, ["onesb"])
        stG = WA("stG", 11520, 128); st2 = WA("st2", 11648, 128)
        for i, g in enumerate([g_mix, g_cross, g_mem, g_ffn]):
            dma("sp", stG[16 * i:16 * i + 16, :], g.rearrange("(k p) -> k p", p=128), [], ["stG"], "stG", join=True)
        for i, g in enumerate([pool_scale, ssm_d, b_glu]):
            dma("sp", st2[8 * i:8 * i + 8, :], g.rearrange("(k p) -> k p", p=128), [], ["st2"], "st2", join=True)
        dma("sp", invc_t[:], invc, [], ["invc"], "invc")
        b = bank()
        tr(PS[b][:, 0:64], stG[0:64, :], identf[0:64, 0:64], ["stG", "identf"], [("ps", b)])
        tr(PS[b][:, 64:88], st2[0:24, :], identf[0:24, 0:24], ["st2", "identf"], [("ps", b)])
        cp("dve", gv[:].rearrange("p i k -> p (i k)"), PS[b][:, 0:64], [("ps", b)], ["gv"])
        cp("dve", pv[:], PS[b][:, 64:88], [("ps", b)], ["pscale", "dvec", "bglu"])
        LRE, LIM, DL, XR, TH, RR, CC, SS, T1, T2, T3, ARE, AIM, FRE, FIM, DEN, AM1 = range(17)
        stL = [WA("stL0", 11776, 128), WA("stL1", 11904, 128), WA("stL2", 12032, 128)]
        lsT = WA("lsT", 12160, 2)
        dma("sp", stL[0][0:32, :].rearrange("q (t p) -> q t p", t=2), lam_re.rearrange("(q two) p -> q two p", two=2), [], ["stL0"], "stL0")
        dma("sp", stL[1][0:32, :].rearrange("q (t p) -> q t p", t=2), lam_im.rearrange("(q two) p -> q two p", two=2), [], ["stL1"], "stL1")
        dma("sp", lsT[0:32, :], log_step.rearrange("(q two) -> q two", two=2), [], ["lsT"], "lsT")
        cp("dve", stL[2][0:32, :].rearrange("q (t p) -> q t p", t=2), lsT[0:32, :].unsqueeze(2).to_broadcast([32, 2, 64]), ["lsT"], ["stL2"])
        b = bank()
        for i_, col in enumerate((LRE, LIM, DL)):
            tr(PS[b][:, 32 * i_:32 * i_ + 32], stL[i_][0:32, :], identf[0:32, 0:32], ["stL%d" % i_, "identf"], [("ps", b)])
        for i_, col in enumerate((LRE, LIM, DL)):
            cp("dve", sq[:, :, col], PS[b][:, 32 * i_:32 * i_ + 32], [("ps", b)], [("sq", col)])
        XT = [WA("XT0", 0, 2048), WA("XT1", 2048, 2048)]
        HNL = [WA("HN0", 4096, 1024, bf=True), WA("HN1", 5120, 1024, bf=True)]
        for _p in range(2):
            P.alias("HN%da" % _p, [("W", b_) for b_ in range((4096 + 1024 * _p) // WBLK, (4096 + 1024 * _p + 512) // WBLK)])
            P.alias("HN%db" % _p, [("W", b_) for b_ in range((4096 + 1024 * _p + 512) // WBLK, (4096 + 1024 * _p + 1024) // WBLK)])
        nctr = [0]

        def norm_from(xt_ap, xkey, R, gi, dst_fn, dst_keys):
            par = nctr[0] % 2
            nctr[0] += 1
            HN = HNL[par]; hk = "HN%d" % par
            s0, s1, s2 = 4 * par, 4 * par + 1, 4 * par + 2
            k0, k1, k2 = "stat%d" % s0, "stat%d" % s1, "stat%d" % s2
            act(HN[0:R, :], xt_ap, AF.Square, [xkey], [hk + "a", hk + "b", k0], accum=stat[0:R, s0:s0 + 1])
            ts("dve", stat[0:R, s1:s1 + 1], stat[0:R, s0:s0 + 1], 1.0 / D, EPS, ALU.mult, ALU.add, [k0], [k1])
            act(stat[0:R, s1:s1 + 1], stat[0:R, s1:s1 + 1], AF.Sqrt, [k1], [k1])
            recip(stat[0:R, s2:s2 + 1], stat[0:R, s1:s1 + 1], [k1], [k2])
            act(HN[0:R, 0:1024], xt_ap[:, 0:1024], AF.Copy, [xkey, k2], [hk + "a"], scale=stat[0:R, s2:s2 + 1])
            ts("dve", HN[0:R, 1024:2048], xt_ap[:, 1024:2048], stat[0:R, s2:s2 + 1], None, ALU.mult, None, [xkey, k2], [hk + "b"])
            for hb in range(2):
                b = bank()
                pb = PS[b][:].bitcast(BF16)
                for j in range(8):
                    kc = hb * 8 + j
                    tr(pb[:, j * 128:j * 128 + R], HN[0:R, kc * 128:(kc + 1) * 128], identb[0:R, 0:R], [hk + ("a" if hb == 0 else "b"), "identb"], [("ps", b)])
                src3 = pb[:, 0:1024].rearrange("p (k t) -> p k t", k=8)[:, :, 0:R]
                gb = gv[:, gi, hb * 8:hb * 8 + 8].unsqueeze(2).to_broadcast([128, 8, R])
                tt("dve", dst_fn(hb), src3, gb, ALU.mult, [("ps", b), "gv"], dst_keys(hb))

        def norm_tile(src_rows, R, col0, gi, ring_i=0, xin=None):
            if xin is None:
                xt_ap = XT[ring_i][0:R, :]
                xkey = "XT%d" % ring_i
                dma("sp", xt_ap, src_rows, [], [xkey], xkey)
            else:
                xt_ap, xkey = xin
            norm_from(xt_ap, xkey, R, gi, lambda hb: A[:, hb * 8:hb * 8 + 8, col0:col0 + R], lambda hb: [("A", k) for k in range(hb * 8, hb * 8 + 8)])

        stgL = [WA("stg0", 10496, 512), WA("stg1", 11008, 512)]
        for part, src in enumerate([sre, sim]):
            for qb in range(8):
                stg = stgL[qb % 2]; stgk = "stg%d" % (qb % 2)
                dma("sp", stg[0:16, :], src[:, qb * 512:(qb + 1) * 512], [], [stgk], stgk)
                b = bank()
                for j in range(4):
                    tr(PS[b][:, j * 16:(j + 1) * 16], stg[0:16, j * 128:(j + 1) * 128], identf[0:16, 0:16], [stgk, "identf"], [("ps", b)])
                cp("dve", H0[:, part, 4 * qb:4 * qb + 4, :], PS[b][:, 0:64].rearrange("p (q s) -> p q s", q=4), [("ps", b)], ["H0"])
        cp("pool", H0b[:], H0[:], ["H0"], ["H0b"])
        for t_i in range(8):
            norm_tile(xprev[t_i * 128:(t_i + 1) * 128, :], 128, t_i * 128, 0, ring_i=t_i % 2)


        def sqv(i):
            return sq[:, :, i]

        P.alias("sq", [("sq", c_) for c_ in range(24)])

        K = ["sq"]
        import math as _m

        def horner(dst, xcol, cf, eng="dve"):
            ts(eng, sqv(dst), sqv(xcol), float(cf[-1]), float(cf[-2]), ALU.mult, ALU.add, [("sq", xcol)], [("sq", dst)])
            for c_ in reversed(cf[:-2]):
                tt(eng, sqv(dst), sqv(dst), sqv(xcol), ALU.mult, [("sq", dst), ("sq", xcol)], [("sq", dst)])
                ts(eng, sqv(dst), sqv(dst), float(c_), None, ALU.add, None, [("sq", dst)], [("sq", dst)])

        ecf = [1.0 / _m.factorial(i) for i in range(13)]
        ts("dve", sqv(T1), sqv(DL), 0.125, None, ALU.mult, None, [("sq", DL)], [("sq", T1)])
        tt("dve", sqv(T2), sqv(T1), sqv(T1), ALU.mult, [("sq", T1), ("sq", T1)], [("sq", T2)])
        horner(DL, T2, ecf[0::2], "dve")
        horner(T3, T2, ecf[1::2], "pool")
        tt("dve", sqv(T3), sqv(T3), sqv(T1), ALU.mult, [("sq", T3), ("sq", T1)], [("sq", T3)])
        tt("dve", sqv(DL), sqv(DL), sqv(T3), ALU.add, [("sq", DL), ("sq", T3)], [("sq", DL)])
        for _ in range(3):
            tt("dve", sqv(DL), sqv(DL), sqv(DL), ALU.mult, [("sq", DL), ("sq", DL)], [("sq", DL)])
        tt("dve", sqv(XR), sqv(LRE), sqv(DL), ALU.mult, [("sq", LRE), ("sq", DL)], [("sq", XR)])
        tt("dve", sqv(TH), sqv(LIM), sqv(DL), ALU.mult, [("sq", LIM), ("sq", DL)], [("sq", TH)])
        ts("dve", sqv(T1), sqv(TH), 1.0 / 16, None, ALU.mult, None, [("sq", TH)], [("sq", T1)])
        tt("dve", sqv(T2), sqv(T1), sqv(T1), ALU.mult, [("sq", T1), ("sq", T1)], [("sq", T2)])
        sc = [(-1.0) ** i / _m.factorial(2 * i + 1) for i in range(7)]
        cc_ = [(-1.0) ** i / _m.factorial(2 * i) for i in range(8)]
        horner(SS, T2, sc, "dve")
        horner(CC, T2, cc_, "pool")
        tt("dve", sqv(SS), sqv(SS), sqv(T1), ALU.mult, [("sq", SS), ("sq", T1)], [("sq", SS)])
        ts("pool", sqv(AM1), sqv(XR), 0.25, None, ALU.mult, None, [("sq", XR)], [("sq", AM1)])
        horner(RR, AM1, ecf[:9], "pool")
        for _ in range(2):
            tt("pool", sqv(RR), sqv(RR), sqv(RR), ALU.mult, [("sq", RR), ("sq", RR)], [("sq", RR)])
        for _ in range(4):
            tt("dve", sqv(T1), sqv(CC), sqv(CC), ALU.mult, [("sq", CC), ("sq", CC)], [("sq", T1)])
            tt("dve", sqv(T2), sqv(SS), sqv(SS), ALU.mult, [("sq", SS), ("sq", SS)], [("sq", T2)])
            tt("dve", sqv(T3), sqv(CC), sqv(SS), ALU.mult, [("sq", CC), ("sq", SS)], [("sq", T3)])
            tt("dve", sqv(CC), sqv(T1), sqv(T2), ALU.subtract, [("sq", T1), ("sq", T2)], [("sq", CC)])
            ts("dve", sqv(SS), sqv(T3), 2.0, None, ALU.mult, None, [("sq", T3)], [("sq", SS)])
        tt("dve", sqv(ARE), sqv(RR), sqv(CC), ALU.mult, [("sq", RR), ("sq", CC)], [("sq", ARE)])
        tt("dve", sqv(AIM), sqv(RR), sqv(SS), ALU.mult, [("sq", RR), ("sq", SS)], [("sq", AIM)])
        tt("dve", sqv(T1), sqv(LRE), sqv(LRE), ALU.mult, [("sq", LRE), ("sq", LRE)], [("sq", T1)])
        tt("dve", sqv(T2), sqv(LIM), sqv(LIM), ALU.mult, [("sq", LIM), ("sq", LIM)], [("sq", T2)])
        tt("dve", sqv(DEN), sqv(T1), sqv(T2), ALU.add, [("sq", T1), ("sq", T2)], [("sq", DEN)])
        recip(sqv(DEN), sqv(DEN), [("sq", DEN)], [("sq", DEN)])
        ts("dve", sqv(AM1), sqv(ARE), -1.0, None, ALU.add, None, [("sq", ARE)], [("sq", AM1)])
        tt("dve", sqv(T1), sqv(AM1), sqv(LRE), ALU.mult, [("sq", AM1), ("sq", LRE)], [("sq", T1)])
        tt("dve", sqv(T2), sqv(AIM), sqv(LIM), ALU.mult, [("sq", AIM), ("sq", LIM)], [("sq", T2)])
        tt("dve", sqv(T1), sqv(T1), sqv(T2), ALU.add, [("sq", T1), ("sq", T2)], [("sq", T1)])
        tt("dve", sqv(FRE), sqv(T1), sqv(DEN), ALU.mult, [("sq", T1), ("sq", DEN)], [("sq", FRE)])
        tt("dve", sqv(T1), sqv(AIM), sqv(LRE), ALU.mult, [("sq", AIM), ("sq", LRE)], [("sq", T1)])
        tt("dve", sqv(T2), sqv(AM1), sqv(LIM), ALU.mult, [("sq", AM1), ("sq", LIM)], [("sq", T2)])
        tt("dve", sqv(T1), sqv(T1), sqv(T2), ALU.subtract, [("sq", T1), ("sq", T2)], [("sq", T1)])
        tt("dve", sqv(FIM), sqv(T1), sqv(DEN), ALU.mult, [("sq", T1), ("sq", DEN)], [("sq", FIM)])

        P2R, P2I, P3R, P3I, P4R, P4I, R4 = 17, 18, 19, 20, 21, 22, 23

        def cmul_sq(orr, oi, xr_, xi_, yr, yi):
            tt("dve", sqv(T1), sqv(xr_), sqv(yr), ALU.mult, [("sq", xr_), ("sq", yr)], [("sq", T1)])
            tt("dve", sqv(T2), sqv(xi_), sqv(yi), ALU.mult, [("sq", xi_), ("sq", yi)], [("sq", T2)])
            tt("dve", sqv(T3), sqv(xr_), sqv(yi), ALU.mult, [("sq", xr_), ("sq", yi)], [("sq", T3)])
            tt("dve", sqv(orr), sqv(T1), sqv(T2), ALU.subtract, [("sq", T1), ("sq", T2)], [("sq", orr)])
            tt("dve", sqv(T1), sqv(xi_), sqv(yr), ALU.mult, [("sq", xi_), ("sq", yr)], [("sq", T1)])
            tt("dve", sqv(oi), sqv(T3), sqv(T1), ALU.add, [("sq", T3), ("sq", T1)], [("sq", oi)])

        cmul_sq(P2R, P2I, ARE, AIM, ARE, AIM)
        cmul_sq(P3R, P3I, P2R, P2I, ARE, AIM)
        cmul_sq(P4R, P4I, P2R, P2I, P2R, P2I)
        tt("dve", sqv(R4), sqv(RR), sqv(RR), ALU.mult, [("sq", RR), ("sq", RR)], [("sq", R4)])
        tt("dve", sqv(R4), sqv(R4), sqv(R4), ALU.mult, [("sq", R4), ("sq", R4)], [("sq", R4)])
        for _ in range(2):
            tt("dve", sqv(T1), sqv(CC), sqv(CC), ALU.mult, [("sq", CC), ("sq", CC)], [("sq", T1)])
            tt("dve", sqv(T2), sqv(SS), sqv(SS), ALU.mult, [("sq", SS), ("sq", SS)], [("sq", T2)])
            tt("dve", sqv(T3), sqv(CC), sqv(SS), ALU.mult, [("sq", CC), ("sq", SS)], [("sq", T3)])
            tt("dve", sqv(CC), sqv(T1), sqv(T2), ALU.subtract, [("sq", T1), ("sq", T2)], [("sq", CC)])
            ts("dve", sqv(SS), sqv(T3), 2.0, None, ALU.mult, None, [("sq", T3)], [("sq", SS)])
        PW = {1: (ARE, AIM), 2: (P2R, P2I), 3: (P3R, P3I), 4: (P4R, P4I)}

        RAW = [WA("RAW0", 0, 1024), WA("RAW1", 1024, 1024)]
        BB = [WA("BB0", 2048, 1024), WA("BB1", 3072, 1024)]
        PR = [WA("PR0", 4096, 1024), WA("PR1", 5120, 1024)]
        TM = [WA("TM0", 6144, 1024), WA("TM1", 7168, 1024)]
        CT = [WA("CT0", 8192, 1024), WA("CT1", 9216, 1024)]
        v4 = lambda ap: ap.rearrange("p (c l j) -> p c l j", c=8, l=4)
        v3c = lambda ap: ap.rearrange("p (c j) -> p c j", c=8)
        mvw = lambda col: sq[:, :, col].rearrange("p (c l) -> p c l", l=4).unsqueeze(3).to_broadcast([128, 8, 4, 32])

        def cmul_pad(out, outk, x, xk, mre, mim, neg_im=False):
            tt("dve", v4(TM[0]), v4(x[0]), mvw(mre), ALU.mult, [xk[0]] + K, ["TM0"])
            tt("dve", v4(TM[1]), v4(x[1]), mvw(mim), ALU.mult, [xk[1]] + K, ["TM1"])
            tt("dve", out[0], TM[0], TM[1], ALU.subtract, ["TM0", "TM1"], [outk[0]])
            tt("dve", v4(TM[0]), v4(x[0]), mvw(mim), ALU.mult, [xk[0]] + K, ["TM0"])
            tt("dve", v4(TM[1]), v4(x[1]), mvw(mre), ALU.mult, [xk[1]] + K, ["TM1"])
            if neg_im:
                stt("dve", out[1], TM[0], -1.0, TM[1], ALU.mult, ALU.subtract, ["TM0", "TM1"], [outk[1]])
            else:
                tt("dve", out[1], TM[0], TM[1], ALU.add, ["TM0", "TM1"], [outk[1]])

        for part, src in enumerate((b_re, b_im)):
            ms("pool", RAW[part], 0.0, ["RAW%d" % part])
            for gl in range(8):
                hf = gl % 2
                dma("sp", v3c(RAW[part])[64 * hf:64 * hf + 64, :, 16 * gl:16 * gl + 16], src.rearrange("(c r) p k -> r p c k", r=8)[gl],
                    [], ["RAW%d" % part], "RAW%d" % part, join=True, slow=True)
        cmul_pad(BB, ["BB0", "BB1"], RAW, ["RAW0", "RAW1"], FRE, FIM)
        for i in range(4):
            kpow = 3 - i
            if kpow == 0:
                srcp, srck = BB, ["BB0", "BB1"]
            else:
                cmul_pad(PR, ["PR0", "PR1"], BB, ["BB0", "BB1"], PW[kpow][0], PW[kpow][1])
                srcp, srck = PR, ["PR0", "PR1"]
            for c in range(8):
                b = bank()
                for part in range(2):
                    tr(PS[b][:, part * 128:(part + 1) * 128], v3c(srcp[part])[:, c, :], identf[:], [srck[part], "identf"], [("ps", b)])
                cp("act", WBt[:, c, :, i, :], PS[b][:, 0:256].rearrange("p (a j) -> p a j", a=2), [("ps", b)], ["WB"])
        for part, src in enumerate((c_re, c_im)):
            ms("pool", RAW[part], 0.0, ["RAW%d" % part])
            for gl in range(8):
                hf = gl % 2
                dma("sp", v3c(RAW[part])[16 * gl:16 * gl + 16, :, 64 * hf:64 * hf + 64], src.rearrange("(c r) k p -> r k c p", r=8)[gl],
                    [], ["RAW%d" % part], "RAW%d" % part, join=True)
        for c in range(8):
            b = bank()
            for part in range(2):
                tr(PS[b][:, part * 128:(part + 1) * 128], v3c(RAW[part])[:, c, :], identf[:], ["RAW%d" % part, "identf"], [("ps", b)])
            for part in range(2):
                cp("act", v3c(CT[part])[:, c, :], PS[b][:, part * 128:(part + 1) * 128], [("ps", b)], ["CT%d" % part])
        KF = WA("KF", 10240, 128)
        for kpow in range(5):
            if kpow == 0:
                cp("dve", PR[0], CT[0], ["CT0"], ["PR0"])
                ts("dve", PR[1], CT[1], -1.0, None, ALU.mult, None, ["CT1"], ["PR1"])
            else:
                cmul_pad(PR, ["PR0", "PR1"], CT, ["CT0", "CT1"], PW[kpow][0], PW[kpow][1], neg_im=True)
            if kpow >= 1:
                for part in range(2):
                    cp("act", WCt[:, :, part, kpow - 1, :], v3c(PR[part]), ["PR%d" % part], ["WC"])
            if kpow <= 3:
                for c in range(8):
                    b = bank()
                    mm(PS[b][:, 0:128], v3c(BB[0])[:, c, :], v3c(PR[0])[:, c, :], True, False, ["BB0", "PR0"], [("ps", b)])
                    mm(PS[b][:, 0:128], v3c(BB[1])[:, c, :], v3c(PR[1])[:, c, :], False, True, ["BB1", "PR1"], [("ps", b)])
                    if kpow == 0:
                        tt("dve", KF, PS[b][:, 0:128], bmask_t[:], ALU.mult, [("ps", b), "bmask"], ["KF"])
                        stt("dve", KTt[:, c, 0, :], identf[:], dvec[:, c:c + 1], KF, ALU.mult, ALU.add, ["KF", "identf", "dvec"], ["KT"])
                    else:
                        tt("dve", KTt[:, c, kpow, :], PS[b][:, 0:128], bmask_t[:], ALU.mult, [("ps", b), "bmask"], ["KT"])

        cp("dve", cosT[:, :, 0], sqv(CC), K, ["cosT"])
        cp("dve", sinT[:, :, 0], sqv(SS), K, ["sinT"])
        tmpA = WA("tmpA", 0, 2048).rearrange("p (q j) -> p q j", q=32)
        tmpB = WA("tmpB", 2048, 2048).rearrange("p (q j) -> p q j", q=32)
        m = 1
        while m < 128:
            cm = cosT[:, :, m - 1:m].to_broadcast([128, 32, m])
            sm = sinT[:, :, m - 1:m].to_broadcast([128, 32, m])
            tt("dve", tmpA[:, :, 0:m], cosT[:, :, 0:m], cm, ALU.mult, TAB, ["tmpA"])
            tt("dve", tmpB[:, :, 0:m], sinT[:, :, 0:m], sm, ALU.mult, TAB, ["tmpB"])
            tt("dve", cosT[:, :, m:2 * m], tmpA[:, :, 0:m], tmpB[:, :, 0:m], ALU.subtract, ["tmpA", "tmpB"], ["cosT"])
            tt("dve", tmpA[:, :, 0:m], sinT[:, :, 0:m], cm, ALU.mult, TAB, ["tmpA"])
            tt("dve", tmpB[:, :, 0:m], cosT[:, :, 0:m], sm, ALU.mult, TAB, ["tmpB"])
            tt("dve", sinT[:, :, m:2 * m], tmpA[:, :, 0:m], tmpB[:, :, 0:m], ALU.add, ["tmpA", "tmpB"], ["sinT"])
            m *= 2
        ms("dve", Sst[:], 0.0, [("S", q) for q in range(32)])
        dump("cosT", XC[:, 0:4096], ["cosT"]); dump("sinT", XC[:, 4096:8192], ["sinT"])
        dump("WB", XC[:, 8192:12288], ["WB"]); dump("WC", XC[:, 12288:16384], ["WC"]); dump("KT", XC[:, 16384:18432], ["KT"])
        dump("sq", sq[:], ["sq"]); dump("H0", H0[:], ["H0"])
        ckpt("setup")

        RINGS = {"wr": (6400, 1024, 2), "wo": (8448, 2048, 2), "wkv": (0, 2048, 2)}
        rctr = {"wr": 0, "wo": 0, "wkv": 0}

        def load_w(src_ap, nk, ncols, tag="wr"):
            base, slot, nbuf = RINGS[tag]
            i = rctr[tag] % nbuf
            rctr[tag] += 1
            sz = nk * ncols // 2
            assert sz <= slot
            name = "%s%d" % (tag, i)
            v = WA(name, base + i * slot, sz, bf=True).rearrange("p (k c) -> p k c", k=nk)
            dma("pool", v, src_ap.rearrange("(k p) c -> p k c", p=128), [], [name], name)
            return v, name

        def fm_matmul(wv, wkey, nk, rhs_fn, rhs_keys_fn, c0, ncols):
            b = bank()
            for k in range(nk):
                mm(PS[b][:, 0:ncols], wv[:, k, :], rhs_fn(k)[:, c0:c0 + ncols], k == 0, k == nk - 1, [wkey] + rhs_keys_fn(k), [("ps", b)])
            return b

        Akeys = lambda k: [("A", k)]

        TSd = {}
        for n_i, nm in enumerate(("t1", "t2", "t3", "t4", "t5", "t6", "t7", "t8", "zbr", "zbi", "zr", "zi")):
            TSd[nm] = (WA(nm, 512 * n_i, 512), nm)
        SbL = [WA("Sb0", 8448, 528, bf=True).rearrange("p (l a m) -> p l a m", l=4, a=2),
               WA("Sb1", 9088, 528, bf=True).rearrange("p (l a m) -> p l a m", l=4, a=2)]
        UBL = [WA("UB0", 9728, 544, bf=True), WA("UB1", 10368, 544, bf=True)]
        YT = WA("YT", 11008, 512)
        CM = PSALL[:, 0:2048].rearrange("p (l x) -> p l x", l=4)
        CMK = [("ps", l) for l in range(4)]
        CSR = WA("CSR", 11520, 512); CSI = WA("CSI", 12032, 512)

        def seg_tasks(segs):
            out = []
            n = 0
            for c in range(8):
                for (c0, ncols, sample) in segs:
                    out.append(dict(c=c, c0=c0, ncols=ncols, sample=sample, par=n % 2, first=(c0 == segs[0][0])))
                    n += 1
            return out

        def segA(t):
            c, c0, ncols = t["c"], t["c0"], t["ncols"]
            nb = ncols // 4
            UB = UBL[c % 2]; UBk = "UB%d" % (c % 2)
            for ql in range(4):
                for part in range(2):
                    for i in range(4):
                        mm(PS[ql][:, part * 128:part * 128 + nb], WBt[32 * ql:32 * ql + 32, c, part, i, :], UB[32 * ql:32 * ql + 32, c0 + i:c0 + ncols:4],
                           i == 0, i == 3, ["WB", UBk], [("ps", ql)], tp=(32 * ql, 0))

        def segB1(t):
            nb = t["ncols"] // 4
            v3 = lambda ap: ap[:, 0:4 * nb].rearrange("p (l m) -> p l m", l=4)
            cp("act", v3(CSR), CM[:, :, 0:nb], CMK, ["CSR"])
            cp("act", v3(CSI), CM[:, :, 128:128 + nb], CMK, ["CSI"])

        def segB(t, need_y):
            c, c0, ncols, sample, par = t["c"], t["c0"], t["ncols"], t["sample"], t["par"]
            nb = ncols // 4
            T = TSd
            (t1, t1k), (t2, t2k), (t3, t3k), (t4, t4k) = T["t1"], T["t2"], T["t3"], T["t4"]
            (t5, t5k), (t6, t6k), (t7, t7k), (t8, t8k) = T["t5"], T["t6"], T["t7"], T["t8"]
            (zbr, zbrk), (zbi, zbik), (zr, zrk), (zi, zik) = T["zbr"], T["zbi"], T["zr"], T["zi"]
            Sb = SbL[par]; Sbk = "Sb%d" % par
            v3 = lambda ap: ap[:, 0:4 * nb].rearrange("p (l m) -> p l m", l=4)
            pr = CM[:, :, 0:nb]; pi = CM[:, :, 128:128 + nb]
            if sample:
                cbt = cosT[:, 4 * c:4 * c + 4, 0:1].to_broadcast([128, 4, nb]); sbt = sinT[:, 4 * c:4 * c + 4, 0:1].to_broadcast([128, 4, nb])
            else:
                cbt = cosT[:, 4 * c:4 * c + 4, :]; sbt = sinT[:, 4 * c:4 * c + 4, :]
            SK = [("S", q) for q in range(4 * c, 4 * c + 4)]
            tt("dve", v3(t1), v3(CSR), cbt, ALU.mult, ["CSR", "cosT"], [t1k])
            tt("pool", v3(t3), v3(CSI), cbt, ALU.mult, ["CSI", "cosT"], [t3k])
            tt("dve", v3(t2), v3(CSI), sbt, ALU.mult, ["CSI", "sinT"], [t2k])
            tt("pool", v3(t4), v3(CSR), sbt, ALU.mult, ["CSR", "sinT"], [t4k])
            tt("dve", v3(zbr), v3(t1), v3(t2), ALU.add, [t1k, t2k], [zbrk])
            tt("pool", v3(zbi), v3(t3), v3(t4), ALU.subtract, [t3k, t4k], [zbik])
            r4b = sq[:, 4 * c:4 * c + 4, R4:R4 + 1]
            if sample:
                for part, (zb_, z_, zk, zbk) in enumerate(((zbr, zr, zrk, zbrk), (zbi, zi, zik, zbik))):
                    tx, txk = (t5, t5k) if part == 0 else (t6, t6k)
                    tt("pool", v3(tx), H0[:, part, 4 * c:4 * c + 4, :], r4b.to_broadcast([128, 4, nb]), ALU.mult, ["H0", "sq"], [txk])
                    tt("dve", v3(z_), v3(zb_), v3(tx), ALU.add, [zbk, txk], [zk])
            else:
                if need_y:
                    for part in range(2):
                        cp("act", Sb[:, :, part, 0], Sst[:, part, 4 * c:4 * c + 4], SK, [Sbk])
                for part, (zb_, z_, zk, zbk) in enumerate(((zbr, zr, zrk, zbrk), (zbi, zi, zik, zbik))):
                    for ql in range(4):
                        q = 4 * c + ql
                        rb = sq[:, q, R4:R4 + 1].to_broadcast([128, nb])
                        P.op("dve", lambda e, ql=ql, q=q, rb=rb, z_=z_, zb_=zb_, part=part: e.tensor_tensor_scan(
                            out=v3(z_)[:, ql, :], data0=rb, data1=v3(zb_)[:, ql, :], initial=Sst[:, part, q:q + 1], op0=ALU.mult, op1=ALU.add),
                            [zbk, "sq", ("S", q)], [zk])
                cl = cosT[:, 4 * c:4 * c + 4, nb - 1]; sl_ = sinT[:, 4 * c:4 * c + 4, nb - 1]
                zrl = v3(zr)[:, :, nb - 1]; zil = v3(zi)[:, :, nb - 1]
                tt("dve", tiny[:, 0:4], zrl, cl, ALU.mult, [zrk, "cosT"], ["tiny0"])
                tt("dve", tiny[:, 4:8], zil, sl_, ALU.mult, [zik, "sinT"], ["tiny1"])
                tt("dve", Sst[:, 0, 4 * c:4 * c + 4], tiny[:, 0:4], tiny[:, 4:8], ALU.subtract, ["tiny0", "tiny1"], SK)
                tt("dve", tiny[:, 0:4], zrl, sl_, ALU.mult, [zrk, "sinT"], ["tiny0"])
                tt("dve", tiny[:, 4:8], zil, cl, ALU.mult, [zik, "cosT"], ["tiny1"])
                tt("dve", Sst[:, 1, 4 * c:4 * c + 4], tiny[:, 0:4], tiny[:, 4:8], ALU.add, ["tiny0", "tiny1"], SK)
            if not need_y:
                return
            tt("dve", v3(t5), v3(zr), cbt, ALU.mult, [zrk, "cosT"], [t5k])
            tt("pool", v3(t7), v3(zr), sbt, ALU.mult, [zrk, "sinT"], [t7k])
            tt("dve", v3(t6), v3(zi), sbt, ALU.mult, [zik, "sinT"], [t6k])
            tt("pool", v3(t8), v3(zi), cbt, ALU.mult, [zik, "cosT"], [t8k])
            if sample:
                tt("dve", SF[:, 0, 4 * c:4 * c + 4, :], v3(t5), v3(t6), ALU.subtract, [t5k, t6k], ["SF"])
                tt("pool", SF[:, 1, 4 * c:4 * c + 4, :], v3(t7), v3(t8), ALU.add, [t7k, t8k], ["SF"])
            else:
                tt("dve", Sb[:, :, 0, 1:1 + nb], v3(t5), v3(t6), ALU.subtract, [t5k, t6k], [Sbk])
                tt("pool", Sb[:, :, 1, 1:1 + nb], v3(t7), v3(t8), ALU.add, [t7k, t8k], [Sbk])

        def segC(t):
            c, c0, ncols, sample, par = t["c"], t["c0"], t["ncols"], t["sample"], t["par"]
            nb = ncols // 4
            UB = UBL[c % 2]; UBk = "UB%d" % (c % 2)
            Sb = SbL[par]; Sbk = "Sb%d" % par
            b = bank()
            for i in range(4):
                oc_ = PS[b][:, i:ncols:4]
                for tau in range(i + 1):
                    mm(oc_, KTt[:, c, tau, :], UB[:, c0 + i - tau:c0 + ncols:4], tau == 0, False, ["KT", UBk], [("ps", b)])
                for ql in range(4):
                    for part in range(2):
                        rhs = H0b[:, part, 4 * c + ql, :] if sample else Sb[:, ql, part, 0:nb]
                        mm(PS[b][32 * ql:32 * ql + 32, i:ncols:4], WCt[:, c, part, i, 32 * ql:32 * ql + 32], rhs, False, (ql == 3 and part == 1),
                           ["WC", "H0b" if sample else Sbk], [("ps", b)], tp=(0, 32 * ql))
            t["ybank"] = b

        def segCact(t):
            c, c0, ncols, b = t["c"], t["c0"], t["ncols"], t["ybank"]
            act(B[:, 8 + c, c0:c0 + ncols], PS[b][:, 0:ncols], AF.Gelu_apprx_tanh, [("ps", b)], [("B", 8 + c)])

        def ssm_pass(segs, need_y):
            bpool[0] = [4, 5, 6, 7]
            tasks = seg_tasks(segs)
            wl = {}

            def U(c):
                wv, wkey = wl.pop(c)
                if c + 1 < 8:
                    wl[c + 1] = load_w(w_in[:, 1024 + 128 * (c + 1):1024 + 128 * (c + 2)], 16, 128)
                for (c0, ncols, _) in segs:
                    b = fm_matmul(wv, wkey, 16, lambda k: A[:, k, :], Akeys, c0, ncols)
                    cp("act", UBL[c % 2][:, c0:c0 + ncols], PS[b][:, 0:ncols], [("ps", b)], ["UB%d" % (c % 2)])

            wl[0] = load_w(w_in[:, 1024:1024 + 128], 16, 128)
            U(0)
            segA(tasks[0])
            segB1(tasks[0])
            for j, t in enumerate(tasks):
                if t["first"] and t["c"] + 1 < 8:
                    U(t["c"] + 1)
                if j + 1 < len(tasks):
                    segA(tasks[j + 1])
                segB(t, need_y)
                if need_y:
                    segC(t)
                if j + 1 < len(tasks):
                    segB1(tasks[j + 1])
                if need_y:
                    segCact(t)
            bpool[0] = list(range(8))

        ssm_pass([(0, 512, False), (512, 512, False)], need_y=False)
        for c, (wv, wkey) in pf(range(8), lambda c: load_w(w_in[:, 128 * c:128 * (c + 1)], 16, 128)):
            b = fm_matmul(wv, wkey, 16, lambda k: A[:, k, :], Akeys, 896, 128)
            cp("act", HISTP[:, c, :], PS[b][:, 113:128], [("ps", b)], ["HISTP"])
        dump("Sst_prefix", Sst[:], [("S", q) for q in range(32)]); dump("HISTP", HISTP[:], ["HISTP"])
        ckpt("prefix")

        for t_i, (r0, R) in enumerate(TT):
            src = xp[r0:r0 + R, :] if t_i < 8 else xs[:, :]
            norm_tile(src, R, r0, 0, ring_i=t_i % 2)
        dump("HT", A[:], [("A", k) for k in range(16)])
        ssm_pass([(0, 512, False), (512, 512, False), (1024, 64, True)], need_y=True)

        SO = WA("SO", 0, 512)
        so2 = [WA("so2_0", 512, 512), WA("so2_1", 1024, 512)]
        for part, (dstp, dsts) in enumerate([(srp, srs), (sip, sis)]):
            b = bank()
            tr(PS[b][0:32, 0:128], Sst[:, part, :], identf[:], [("S", q) for q in range(32)] + ["identf"], [("ps", b)])
            cp("act", SO[0:32, 0:128], PS[b][0:32, 0:128], [("ps", b)], ["SO"])
            dma("sp", dstp, SO[0:32, 0:128], ["SO"], [], "o_SO")
            for qb in range(8):
                b = bank()
                for j in range(4):
                    tr(PS[b][0:16, j * 128:(j + 1) * 128], SF[:, part, 4 * qb + j, :], identf[:], ["SF", "identf"], [("ps", b)])
                i = qb % 2
                cp("act", so2[i][0:16, :], PS[b][0:16, :], [("ps", b)], ["so2_%d" % i])
                dma("sp", dsts[:, qb * 512:(qb + 1) * 512], so2[i][0:16, :], ["so2_%d" % i], [], "o_so2_%d" % i)
        dump("G", B[:, 8:16, :], [("B", k) for k in range(8, 16)]); dump("Sst_main", Sst[:], [("S", q) for q in range(32)]); dump("SF", SF[:], ["SF"])
        ckpt("ssm")

        Z = WA("Z", 0, 1040)
        Zs = WA("Zs", 1040, 304).rearrange("p (s t) -> p s t", t=19)
        PT1 = WA("PT1", 1344, 1040); PT2 = WA("PT2", 2384, 1040)
        PTs1 = WA("PTs1", 3424, 304).rearrange("p (s t) -> p s t", t=19)
        PTs2 = WA("PTs2", 3728, 304).rearrange("p (s t) -> p s t", t=19)
        ZC = WA("ZC", 4096, 240)
        PB = WA("PB", 8448, 2048).rearrange("p (h c) -> p h c", h=2)
        PSO = [WA("PSO0", 11520, 128), WA("PSO1", 11648, 128)]
        PBO = [WA("PBO0", 12032, 128), WA("PBO1", 12160, 128)]
        for h in range(2):
            dma("sp", PB[0:120, h, :], pbuf[h * 120:(h + 1) * 120, :], [], ["PB"], "PB", join=True)
        octr = [0]
        for c, (wv, wkey) in pf(range(8), lambda c: load_w(w_in[:, 128 * c:128 * (c + 1)], 16, 128)):
            wg = c // 2
            w = 2 << wg
            for (c0, ncols) in NTL:
                b = fm_matmul(wv, wkey, 16, lambda k: A[:, k, :], Akeys, c0, ncols)
                if c0 < 1024:
                    cp("act", Z[:, 15 + c0:15 + c0 + ncols], PS[b][:, 0:ncols], [("ps", b)], ["Z"])
                else:
                    cp("act", Zs[:, :, 15:19], PS[b][:, 0:64].rearrange("p (s t) -> p s t", t=4), [("ps", b)], ["Zs"])
            cp("dve", Z[:, 0:15], HISTP[:, c, :], ["HISTP"], ["Z"])
            for h in range(2):
                b = bank()
                tr(PS[b][:, 0:120], PB[0:120, h, 128 * c:128 * (c + 1)], identf[0:120, 0:120], ["PB", "identf"], [("ps", b)])
                cp("dve", Zs[:, 8 * h:8 * h + 8, 0:15], PS[b][:, 0:120].rearrange("p (s t) -> p s t", t=15), [("ps", b)], ["Zs"])
            cur, curk, curs, cursk = Z, "Z", Zs, "Zs"
            bufs = [(PT1, "PT1", PTs1, "PTs1"), (PT2, "PT2", PTs2, "PTs2")]
            for i in range(wg + 1):
                sh = 1 << i
                lo = 2 * sh - 1
                nb, nbk, nbs, nbsk = bufs[i % 2]
                tt("pool", nb[:, lo:1039], cur[:, lo:1039], cur[:, lo - sh:1039 - sh], ALU.add, [curk], [nbk])
                tt("pool", nbs[:, :, lo:19], curs[:, :, lo:19], curs[:, :, lo - sh:19 - sh], ALU.add, [cursk], [nbsk])
                cur, curk, curs, cursk = nb, nbk, nbs, nbsk
            stt("dve", B[:, c, 0:1024], cur[:, 15:1039], 1.0 / w, Z[:, 15:1039], ALU.mult, ALU.subtract, [curk, "Z"], [("B", c)])
            tt("dve", YT[:, 0:15], cur[:, 15:30], invc_t[:, wg, :], ALU.mult, [curk, "invc"], ["YT"])
            tt("dve", B[:, c, 0:15], YT[:, 0:15], Z[:, 15:30], ALU.subtract, ["YT", "Z"], [("B", c)])
            stt("dve", B[:, c, 1024:1088].rearrange("p (s t) -> p s t", t=4), curs[:, :, 15:19], 1.0 / w, Zs[:, :, 15:19], ALU.mult, ALU.subtract, [cursk, "Zs"], [("B", c)])
            i = octr[0] % 2
            octr[0] += 1
            b = bank()
            tr(PS[b][0:15, 0:128], Z[:, 1024:1039], identf[:], ["Z", "identf"], [("ps", b)])
            cp("act", PBO[i][0:15, :], PS[b][0:15, 0:128], [("ps", b)], ["PBO%d" % i])
            dma("sp", pbp[:, 128 * c:128 * (c + 1)], PBO[i][0:15, :], ["PBO%d" % i], [], "o_PBO%d" % i)
            cp("pool", ZC.rearrange("p (s t) -> p s t", t=15), Zs[:, :, 4:19], ["Zs"], ["ZC"])
            for h in range(2):
                b = bank()
                tr(PS[b][0:120, 0:128], ZC[:, 120 * h:120 * (h + 1)], identf[:], ["ZC", "identf"], [("ps", b)])
                cp("act", PSO[h][0:120, :], PS[b][0:120, 0:128], [("ps", b)], ["PSO%d" % h])
                dma("sp", pbs[h * 120:(h + 1) * 120, 128 * c:128 * (c + 1)], PSO[h][0:120, :], ["PSO%d" % h], [], "o_PSO%d" % h)
        dump("POOLED", B[:, 0:8, :], [("B", k) for k in range(8)])
        ckpt("pool")

        for wg, (wv, wkey) in pf(range(4), lambda wg: load_w(w_pool[wg], 2, 256)):
            for (c0, ncols) in NTL:
                bs = []
                for oc in range(2):
                    b = bank()
                    for k in range(2):
                        mm(PS[b][:, 0:ncols], wv[:, k, 128 * oc:128 * (oc + 1)], B[:, 2 * wg + k, c0:c0 + ncols], k == 0, k == 1, [wkey, ("B", 2 * wg + k)], [("ps", b)])
                    bs.append(b)
                for oc in range(2):
                    act(B[:, 2 * wg + oc, c0:c0 + ncols], PS[bs[oc]][:, 0:ncols], AF.Copy, [("ps", bs[oc]), "pscale"], [("B", 2 * wg + oc)], scale=pscale[:, 2 * wg + oc:2 * wg + oc + 1])

        for oc, (wv, wkey) in pf(range(8), lambda oc: load_w(w_glu[:, 128 * oc:128 * (oc + 1)], 8, 128)):
            for (c0, ncols) in NTL:
                b = fm_matmul(wv, wkey, 8, lambda k: B[:, 8 + k, :], lambda k: [("B", 8 + k)], c0, ncols)
                act(YT[:, 0:ncols], PS[b][:, 0:ncols], AF.Sigmoid, [("ps", b), "bglu"], ["YT"], bias=bglu[:, oc:oc + 1])
                tt("dve", A[:, 8 + oc, c0:c0 + ncols], B[:, 8 + oc, c0:c0 + ncols], YT[:, 0:ncols], ALU.mult, ["YT", ("B", 8 + oc)], [("A", 8 + oc)])
        dump("MIXP", B[:, 0:8, :], [("B", k) for k in range(8)]); dump("MIXS", A[:, 8:16, :], [("A", k) for k in range(8, 16)])
        ckpt("glu")

        def proj_resid(wsrc, lhs_fn, lhs_keys_fn, nk, first):
            CB = 256
            if first:
                for ti, (r0, R) in enumerate(TT):
                    src = xp[r0:r0 + R, :] if ti < 8 else xs[:, :]
                    dma("sp", Xt(ti)[0:R, :], src, [], [("X", ti)], ("X", ti))
            for cb, (wv, wkey) in pf(range(D // CB), lambda cb: load_w(wsrc[:, cb * CB:(cb + 1) * CB], nk, CB, tag="wo")):
                for ti, (r0, R) in enumerate(TT):
                    b = bank()
                    for k in range(nk):
                        mm(PS[b][0:R, 0:CB], lhs_fn(k)[:, r0:r0 + R], wv[:, k, :], k == 0, k == nk - 1, [wkey] + lhs_keys_fn(k), [("ps", b)])
                    xs_ap = Xt(ti)[0:R, cb * CB:(cb + 1) * CB]
                    tt("dve", xs_ap, xs_ap, PS[b][0:R, 0:CB], ALU.add, [("X", ti), ("ps", b)], [("X", ti)])

        mixfn = lambda k: (B[:, k, :] if k < 8 else A[:, k, :])
        mixkeys = lambda k: ([("B", k)] if k < 8 else [("A", k)])
        proj_resid(w_out, mixfn, mixkeys, 16, True)
        dump("X1", XC[:], [("X", t) for t in range(9)])
        ckpt("wout")

        for ti, (r0, R) in enumerate(TT):
            norm_tile(None, R, r0, 1, xin=(Xt(ti)[0:R, :], ("X", ti)))
        for oc, (wv, wkey) in pf(range(16), lambda oc: load_w(w_q[:, 128 * oc:128 * (oc + 1)], 16, 128)):
            for (c0, ncols) in NTL:
                b = fm_matmul(wv, wkey, 16, lambda k: A[:, k, :], Akeys, c0, ncols)
                cp("act", B[:, oc, c0:c0 + ncols], PS[b][:, 0:ncols], [("ps", b)], [("B", oc)])
        ckpt("attn_q")
        MT = WA("MT", 8448, 2048, bf=True).rearrange("p (k t) -> p k t", k=16)
        KT = WA("KT", 10496, 2048, bf=True).rearrange("p (k t) -> p k t", k=16)
        VB = WA("VB", 6400, 2048, bf=True).rearrange("p (m d) -> p m d", m=2)
        KO = [WA("KO0", 4096, 512), WA("KO1", 4608, 512)]
        EX = [WA("EX0", 5120, 256, bf=True), WA("EX1", 5376, 256, bf=True)]
        RD = WA("RD", 5632, 512)
        for mt_i in range(2):
            xkey = "XT%d" % mt_i
            dma("sp", XT[mt_i][:, :], mem[mt_i * 128:(mt_i + 1) * 128, :], [], [xkey], xkey)
            norm_from(XT[mt_i][:, :], xkey, 128, 2, lambda hb: MT[:, hb * 8:hb * 8 + 8, mt_i * 128:(mt_i + 1) * 128], lambda hb: ["MT"])
        ckpt("kv0")
        for which, (wsrc, dsto) in enumerate([(w_k, mk), (w_v, mv)]):
            for cb, (wv, wkey) in pf(range(8), lambda cb: load_w(wsrc[:, cb * 256:(cb + 1) * 256], 16, 256, tag="wkv")):
                for mt_i in range(2):
                    b = bank()
                    for k in range(16):
                        mm(PS[b][:, 0:256], MT[:, k, mt_i * 128:(mt_i + 1) * 128], wv[:, k, :], k == 0, k == 15, [wkey, "MT"], [("ps", b)])
                    kk = "KO%d" % mt_i
                    cp("act", KO[mt_i][:, 0:256], PS[b][:, 0:256], [("ps", b)], [kk])
                    if which == 1:
                        cp("dve", VB[:, mt_i, cb * 256:(cb + 1) * 256], KO[mt_i][:, 0:256], [kk], ["VB"])
                    dma("sp", dsto[mt_i * 128:(mt_i + 1) * 128, cb * 256:(cb + 1) * 256], KO[mt_i][:, 0:256], [kk], [], "o_" + kk)
                if which == 0:
                    for j in range(2):
                        b = bank()
                        for k in range(16):
                            mm(PS[b][:, 0:256], wv[:, k, 128 * j:128 * (j + 1)], MT[:, k, :], k == 0, k == 15, [wkey, "MT"], [("ps", b)])
                        cp("act", KT[:, 2 * cb + j, :], PS[b][:, 0:256], [("ps", b)], ["KT"])
        ckpt("attn_kv")
        SCL = 512.0 ** -0.5
        NKV = 4
        KS = [WA("KS%d" % i, i * 512, 512, bf=True).rearrange("p (m d) -> p m d", m=2) for i in range(NKV)]
        VS = [WA("VS%d" % i, 2048 + i * 512, 512, bf=True).rearrange("p (m d) -> p m d", m=2) for i in range(NKV)]
        KTSL = [WA("KTS0", 8448, 512, bf=True).rearrange("p (j m) -> p j m", j=4), WA("KTS1", 8960, 512, bf=True).rearrange("p (j m) -> p j m", j=4)]
        PTSL = [WA("PTS0", 9472, 4, bf=True), WA("PTS1", 9600, 4, bf=True)]
        RDSL = [WA("RDS0", 9728, 4), WA("RDS1", 9856, 4)]

        def prompt_unit(blk, h):
            c0 = blk * 512
            for mt_i in range(2):
                b = bank()
                for j in range(4):
                    mm(PS[b][:, :], KT[:, 4 * h + j, mt_i * 128:(mt_i + 1) * 128], B[:, 4 * h + j, c0:c0 + 512], j == 0, j == 3, ["KT", ("B", 4 * h + j)], [("ps", b)])
                act(EX[mt_i], PS[b][:, :], AF.Exp, [("ps", b)], ["EX%d" % mt_i], scale=SCL)
            b = bank()
            for mt_i in range(2):
                mm(PS[b][:, :], onesb[:], EX[mt_i], mt_i == 0, mt_i == 1, ["onesb", "EX%d" % mt_i], [("ps", b)])
            recip(RD, PS[b][:, :], [("ps", b)], ["RD"])
            for j in range(4):
                b = bank()
                for mt_i in range(2):
                    mm(PS[b][:, :], VB[:, mt_i, 512 * h + 128 * j:512 * h + 128 * (j + 1)], EX[mt_i], mt_i == 0, mt_i == 1, ["VB", "EX%d" % mt_i], [("ps", b)])
                tt("dve", A[:, 4 * h + j, c0:c0 + 512], PS[b][:, :], RD, ALU.mult, [("ps", b), "RD"], [("A", 4 * h + j)])

        sh_list = [(s_, h) for s_ in range(NSQ) for h in range(4)]

        def load_kv(n):
            s_, h = sh_list[n]
            i = n % NKV
            dma("pool", KS[i], ck[s_].rearrange("(m p) d -> p m d", p=128)[:, :, 512 * h:512 * (h + 1)], [], ["KS%d" % i], "KS%d" % i)
            dma("pool", VS[i], cv[s_].rearrange("(m p) d -> p m d", p=128)[:, :, 512 * h:512 * (h + 1)], [], ["VS%d" % i], "VS%d" % i)

        def sampA(n):
            s_, h = sh_list[n]
            i = n % NKV
            pp = n % 2
            b = bank()
            pb = PS[b][:].bitcast(BF16)
            for j in range(4):
                for mt_i in range(2):
                    tr(pb[:, j * 256 + mt_i * 128:j * 256 + (mt_i + 1) * 128], KS[i][:, mt_i, 128 * j:128 * (j + 1)], identb[:], ["KS%d" % i, "identb"], [("ps", b)])
            cp("act", KTSL[pp], pb[:, 0:1024].rearrange("p (j m) -> p j m", j=4), [("ps", b)], ["KTS%d" % pp])

        def sampB(n):
            s_, h = sh_list[n]
            pp = n % 2
            KTS = KTSL[pp]; PTS = PTSL[pp]
            qc = 1024 + 4 * s_
            b = bank()
            for mt_i in range(2):
                for j in range(4):
                    mm(PS[b][:, mt_i * 4:mt_i * 4 + 4], KTS[:, j, mt_i * 128:(mt_i + 1) * 128], B[:, 4 * h + j, qc:qc + 4], j == 0, j == 3, ["KTS%d" % pp, ("B", 4 * h + j)], [("ps", b)])
            act(PTS, PS[b][:, 0:8], AF.Exp, [("ps", b)], ["PTS%d" % pp], scale=SCL)

        def sampC(n):
            s_, h = sh_list[n]
            i = n % NKV
            pp = n % 2
            PTS = PTSL[pp]; RDS = RDSL[pp]
            qc = 1024 + 4 * s_
            b = bank()
            for mt_i in range(2):
                mm(PS[b][:, 0:4], onesb[:], PTS[:, mt_i * 4:mt_i * 4 + 4], mt_i == 0, mt_i == 1, ["onesb", "PTS%d" % pp], [("ps", b)])
            recip(RDS, PS[b][:, 0:4], [("ps", b)], ["RDS%d" % pp])
            b = bank()
            for j in range(4):
                for mt_i in range(2):
                    mm(PS[b][:, j * 4:j * 4 + 4], VS[i][:, mt_i, 128 * j:128 * (j + 1)], PTS[:, mt_i * 4:mt_i * 4 + 4], mt_i == 0, mt_i == 1, ["VS%d" % i, "PTS%d" % pp], [("ps", b)])
            tt("dve", A[:, 4 * h:4 * h + 4, qc:qc + 4], PS[b][:, 0:16].rearrange("p (j t) -> p j t", j=4), RDS.unsqueeze(1).to_broadcast([128, 4, 4]), ALU.mult,
               [("ps", b), "RDS%d" % pp], [("A", 4 * h + j) for j in range(4)])

        nload = [0]

        def ensure_loaded(upto):
            while nload[0] <= min(upto, len(sh_list) - 1):
                load_kv(nload[0])
                nload[0] += 1

        ensure_loaded(NKV - 1)

        def sample_group(n0, cnt):
            sampA(n0)
            for k_ in range(cnt):
                n = n0 + k_
                if k_ + 1 < cnt:
                    sampA(n + 1)
                sampB(n)
                if k_ >= 1:
                    sampC(n - 1)
                    ensure_loaded(n - 1 + NKV)
            sampC(n0 + cnt - 1)
            ensure_loaded(n0 + cnt - 1 + NKV)

        n_it = 0
        for u in range(8):
            prompt_unit(u // 4, u % 4)
            sample_group(n_it, 8)
            n_it += 8
        ckpt("attn_s")
        dump("QT", B[:], [("B", k) for k in range(16)]); dump("OT", A[:], [("A", k) for k in range(16)])
        proj_resid(w_o, lambda k: A[:, k, :], Akeys, 16, False)
        dump("X2", XC[:], [("X", t) for t in range(9)])
        ckpt("attn")

        for ti, (r0, R) in enumerate(TT):
            norm_tile(None, R, r0, 3, xin=(Xt(ti)[0:R, :], ("X", ti)))
        GU = [WA("gu%d" % i, i * 1024, 1024, bf=True).rearrange("p (k c) -> p k c", k=16) for i in range(4)]
        DN = [WA("dn%d" % i, 4096 + i * 4096, 4096, bf=True).rearrange("p (k c) -> p k c", k=4) for i in range(2)]
        SG = WA("SG", 12288, 512)
        guc = [0]

        def load_gu(ffc):
            out = []
            for src in (w_gate, w_up):
                i = guc[0] % 4
                guc[0] += 1
                dma("pool", GU[i], src[:, 128 * ffc:128 * (ffc + 1)].rearrange("(k p) c -> p k c", p=128), [], ["gu%d" % i], "gu%d" % i)
                out.append(i)
            return out

        def load_dn(grp):
            i = grp % 2
            dma("pool", DN[i], w_down[512 * grp:512 * (grp + 1), :].rearrange("(k p) c -> p k c", p=128), [], ["dn%d" % i], "dn%d" % i)
            return i

        dn_next = load_dn(0)
        for ffc, (gi_, ui_) in pf(range(44), load_gu):
            grp, fc = ffc // 4, ffc % 4
            slot = (grp % 2) * 4 + fc
            for (c0, ncols) in NTL:
                bg = fm_matmul(GU[gi_], "gu%d" % gi_, 16, lambda k: A[:, k, :], Akeys, c0, ncols)
                bu = fm_matmul(GU[ui_], "gu%d" % ui_, 16, lambda k: A[:, k, :], Akeys, c0, ncols)
                act(SG[:, 0:ncols], PS[bg][:, 0:ncols], AF.Silu, [("ps", bg)], ["SG"])
                tt("dve", B[:, slot, c0:c0 + ncols], SG[:, 0:ncols], PS[bu][:, 0:ncols], ALU.mult, ["SG", ("ps", bu)], [("B", slot)])
            if fc == 3:
                di = dn_next
                if grp + 1 < 11:
                    dn_next = load_dn(grp + 1)
                for ti, (r0, R) in enumerate(TT):
                    for cb in range(4):
                        b = bank()
                        for f2 in range(4):
                            s2 = (grp % 2) * 4 + f2
                            mm(PS[b][0:R, :], B[:, s2, r0:r0 + R], DN[di][:, f2, cb * 512:(cb + 1) * 512], f2 == 0, f2 == 3, ["dn%d" % di, ("B", s2)], [("ps", b)])
                        xs_ap = Xt(ti)[0:R, cb * 512:(cb + 1) * 512]
                        tt("dve", xs_ap, xs_ap, PS[b][0:R, :], ALU.add, [("X", ti), ("ps", b)], [("X", ti)])
        gfin = WA("gfin", 4096, 2048)
        dma("sp", gfin, g_final.partition_broadcast(128), [], ["gfin"], "gfin")
        YO = [WA("YO0", 0, 2048), WA("YO1", 2048, 2048)]
        for ti, (r0, R) in enumerate(TT):
            xk = [("X", ti)]
            xt_ap = Xt(ti)[0:R, :]
            i = ti % 2
            yk = "YO%d" % i
            o_ = 4 * i
            ka, kb, kc_ = "stat%d" % o_, "stat%d" % (o_ + 1), "stat%d" % (o_ + 2)
            jnk = B[:, 8 + 2 * i:10 + 2 * i, :].rearrange("p a n -> p (a n)")[:, 0:2048]
            act(jnk[0:R, :], xt_ap, AF.Square, xk, [("B", 8 + 2 * i), ("B", 9 + 2 * i), ka], accum=stat[0:R, o_:o_ + 1])
            ts("dve", stat[0:R, o_ + 1:o_ + 2], stat[0:R, o_:o_ + 1], 1.0 / D, EPS, ALU.mult, ALU.add, [ka], [kb])
            act(stat[0:R, o_ + 1:o_ + 2], stat[0:R, o_ + 1:o_ + 2], AF.Sqrt, [kb], [kb])
            recip(stat[0:R, o_ + 2:o_ + 3], stat[0:R, o_ + 1:o_ + 2], [kb], [kc_])
            stt("dve", YO[i][0:R, :], xt_ap, stat[0:R, o_ + 2:o_ + 3], gfin[0:R, :], ALU.mult, ALU.mult, xk + [kc_, "gfin"], [yk])
            dst = yp[r0:r0 + R, :] if ti < 8 else ys[:, :]
            dma("sp", dst, YO[i][0:R, :], [yk], [], "o_" + yk)

    try:
        body()
    except _Stop:
        pass
    P.emit(nc, st)
    st.close()
    return nc


_NC = None


def make_in_maps(inp):
    f = lambda a: np.ascontiguousarray(np.asarray(a, dtype=np.float32))
    x_prompt = f(inp["x_prompt"]); x_sample = f(inp["x_sample"]); mem_prompt = f(inp["mem_prompt"])
    spb = f(inp["state_pool_buf"]); s_re = f(inp["state_ssm_re"]); s_im = f(inp["state_ssm_im"])
    cmk = f(inp["cache_mem_k"]); cmv = f(inp["cache_mem_v"])
    shared = {
        "g_mix": f(inp["g_mix"][0]), "w_in": f(inp["w_in"][0]), "w_pool": f(inp["w_pool"][0]), "pool_scale": f(inp["pool_scale"][0]),
        "lam_re": f(inp["ssm_lam_re"][0]), "lam_im": f(inp["ssm_lam_im"][0]), "log_step": f(inp["ssm_log_step"][0]),
        "b_re": f(inp["ssm_b_re"][0]), "b_im": f(inp["ssm_b_im"][0]), "c_re": f(inp["ssm_c_re"][0]), "c_im": f(inp["ssm_c_im"][0]),
        "ssm_d": f(inp["ssm_d"][0]), "w_glu": f(inp["w_glu"][0]), "b_glu": f(inp["b_glu"][0]), "w_out": f(inp["w_out"][0]),
        "g_cross": f(inp["g_cross"][0]), "g_mem": f(inp["g_mem"][0]), "w_q": f(inp["w_q"][0]), "w_k": f(inp["w_k"][0]), "w_v": f(inp["w_v"][0]),
        "w_o": f(inp["w_o"][0]), "g_ffn": f(inp["g_ffn"][0]), "w_gate": f(inp["w_gate"][0]), "w_up": f(inp["w_up"][0]), "w_down": f(inp["w_down"][0]),
        "g_final": f(inp["g_final"]), "ident": np.eye(128, dtype=np.float32),
        "bmask": np.kron(np.eye(8, dtype=np.float32), np.ones((16, 16), np.float32)),
    }
    in_maps = []
    for c in range(8):
        b, half = c // 2, c % 2
        m = dict(shared)
        m["xp"] = f(x_prompt[b, half * 1024:(half + 1) * 1024])
        m["xprev"] = f(x_prompt[b, 0:1024]) if half == 1 else np.zeros((1024, D), np.float32)
        m["xs"] = f(x_sample[16 * c:16 * c + 16].reshape(64, D))
        m["mem"] = f(mem_prompt[b])
        m["pbuf"] = f(spb[0, 16 * c:16 * c + 16].reshape(240, 1024))
        m["sre"] = f(s_re[0, 16 * c:16 * c + 16].reshape(16, 4096))
        m["sim"] = f(s_im[0, 16 * c:16 * c + 16].reshape(16, 4096))
        m["ck"] = f(cmk[0, 16 * c:16 * c + 16].reshape(16, 256, D))
        m["cv"] = f(cmv[0, 16 * c:16 * c + 16].reshape(16, 256, D))
        ic = np.zeros((128, 4, 15), np.float32)
        for wg, w in enumerate((2, 4, 8, 16)):
            for t in range(15):
                ic[:, wg, t] = 1.0 / min(half * 1024 + t + 1, w)
        m["invc"] = ic
        in_maps.append(m)
    return in_maps


def kernel(**inp):
    global _NC
    in_maps = make_in_maps(inp)
    if _NC is None:
        _NC = build()
    res = run_bass_kernel_spmd(_NC, in_maps, core_ids=list(range(8))).results
    y_prompt = np.stack([np.concatenate([res[2 * b]["yp"], res[2 * b + 1]["yp"]], 0) for b in range(4)])
    y_sample = np.concatenate([res[c]["ys"].reshape(16, 4, D) for c in range(8)], 0)
    pb_p = np.stack([res[2 * b + 1]["pbp"] for b in range(4)])[None]
    re_p = np.stack([res[2 * b + 1]["srp"].reshape(64, 64) for b in range(4)])[None]
    im_p = np.stack([res[2 * b + 1]["sip"].reshape(64, 64) for b in range(4)])[None]
    mk_p = np.stack([res[2 * b]["mk"].reshape(256, 4, 512) for b in range(4)])[None]
    mv_p = np.stack([res[2 * b]["mv"].reshape(256, 4, 512) for b in range(4)])[None]
    pb_s = np.concatenate([res[c]["pbs"].reshape(16, 15, 1024) for c in range(8)], 0)[None]
    re_s = np.concatenate([res[c]["srs"].reshape(16, 64, 64) for c in range(8)], 0)[None]
    im_s = np.concatenate([res[c]["sis"].reshape(16, 64, 64) for c in range(8)], 0)[None]
    return (y_prompt.astype(np.float32), y_sample.astype(np.float32), pb_p.astype(np.float32), re_p.astype(np.float32), im_p.astype(np.float32),
            mk_p.astype(np.float32), mv_p.astype(np.float32), pb_s.astype(np.float32), re_s.astype(np.float32), im_s.astype(np.float32))
```
